# Optimizing a Trainium2 kernel written in Bass

```python
import math
import jax, jax.numpy as jnp
from jax import lax
import numpy as np


D_MODEL = 2048
BATCH = 8
SEQ = 2048
DEPTH = 2

CHUNK = 64
Q_BLOCK = 128
GROUP_W = 512
D_MIX = 4 * GROUP_W
EPS = 1e-6
NEG = -1e30

GMLP_BLOCK = 128
A_GROUPS = 4
A_GDIM = GROUP_W // A_GROUPS

B_HEADS = 8
B_HDIM = GROUP_W // B_HEADS
IDX_HEADS = 8
IDX_DIM = 64
TOPK_MAX = 256
T5_BUCKETS = 32
T5_MAX_DIST = 128

C_HEADS = 4
C_NOPE = 128
C_ROPE = 64
C_VDIM = GROUP_W // C_HEADS
C_QK = C_NOPE + C_ROPE
Q_LORA = 384
KV_LORA = 128
ROPE_BASE = 10000.0

D_HEADS = 8
D_HDIM = GROUP_W // D_HEADS
D_LEFT_CHUNKS = 8
D_BAND = (D_LEFT_CHUNKS + 1) * CHUNK
REL_CLIP = 128

IN_SIZES = (
    GROUP_W, GROUP_W, GROUP_W,
    GROUP_W, B_HDIM, B_HDIM, IDX_HEADS * IDX_DIM, IDX_DIM, IDX_HEADS, GROUP_W,
    Q_LORA, KV_LORA, C_ROPE, GROUP_W,
    GROUP_W, GROUP_W, GROUP_W, GROUP_W,
)
IN_COLS = sum(IN_SIZES)

kernel_name = "chunk_causal_hybrid_head_groups"


def rms_norm(x, g):
    xf = x.astype(jnp.float32)
    y = xf * lax.rsqrt(jnp.mean(xf * xf, axis=-1, keepdims=True) + EPS)
    return (y * g.astype(jnp.float32)).astype(x.dtype)


def softmax_f32(s):
    return jax.nn.softmax(s.astype(jnp.float32), axis=-1)


def split_cols(h, sizes):
    out, o = [], 0
    for n in sizes:
        out.append(h[..., o:o + n])
        o += n
    return out


def to_blocks(a, size):
    b, s = a.shape[0], a.shape[1]
    return a.reshape(b, s // size, size, *a.shape[2:]).swapaxes(0, 1)


def from_blocks(o):
    nb, b, size = o.shape[0], o.shape[1], o.shape[2]
    return o.swapaxes(0, 1).reshape(b, nb * size, -1)


def gmlp_mixer(u, v, v_gain, w_s, b_s):
    bsz, s, _ = u.shape
    nb = s // GMLP_BLOCK
    u = jax.nn.gelu(u)
    v = rms_norm(jax.nn.gelu(v), v_gain)
    pos_chunk = jnp.arange(GMLP_BLOCK) // CHUNK
    mask = pos_chunk[None, :] <= pos_chunk[:, None]
    w = jnp.where(mask[None], w_s, 0.0)
    vb = v.reshape(bsz, nb, GMLP_BLOCK, A_GROUPS, A_GDIM)
    sg = jnp.einsum('gij,bnjgc->bnigc', w, vb) + b_s.T[None, None, :, :, None]
    return u * sg.reshape(bsz, s, GROUP_W)


def t5_bucket(rel):
    nb = T5_BUCKETS // 2
    max_exact = nb // 2
    ret = jnp.where(rel > 0, nb, 0)
    n = jnp.abs(rel)
    nf = jnp.maximum(n, 1).astype(jnp.float32)
    large = max_exact + (jnp.log(nf / max_exact) / math.log(T5_MAX_DIST / max_exact)
                         * (nb - max_exact)).astype(jnp.int32)
    large = jnp.minimum(large, nb - 1)
    return ret + jnp.where(n < max_exact, n, large)


def dsa_mixer(q, k, v, iq, ik, iw, q_gain, k_gain, t5_bias):
    bsz, s, _ = q.shape
    topk = min(TOPK_MAX, s // 4)
    q = rms_norm(q.reshape(bsz, s, B_HEADS, B_HDIM), q_gain)
    k = rms_norm(k, k_gain)
    iq = iq.reshape(bsz, s, IDX_HEADS, IDX_DIM)
    key_chunk = jnp.arange(s) // CHUNK
    starts = jnp.arange(s // Q_BLOCK) * Q_BLOCK

    def block(args):
        qb, iqb, iwb, start = args
        qpos = start + jnp.arange(Q_BLOCK)
        qchunk = qpos // CHUNK
        adm = key_chunk[None, :] <= qchunk[:, None]
        logits = jnp.einsum('bthd,bsd->bths', iqb, ik).astype(jnp.float32) * (IDX_DIM ** -0.5)
        score = jnp.einsum('bth,bths->bts', iwb.astype(jnp.float32) * (IDX_HEADS ** -0.5),
                           jax.nn.relu(logits))
        score = jnp.where(adm[None], score, -jnp.inf)
        _, idx = lax.top_k(score, topk)
        valid = (idx // CHUNK) <= qchunk[None, :, None]
        ks = jax.vmap(lambda a, i: a[i])(k, idx)
        vs = jax.vmap(lambda a, i: a[i])(v, idx)
        sc = jnp.einsum('bthd,btkd->bhtk', qb, ks).astype(jnp.float32) * (B_HDIM ** -0.5)
        bias = t5_bias[t5_bucket(idx - qpos[None, :, None])]
        sc = jnp.where(valid[:, None], sc + bias.transpose(0, 3, 1, 2).astype(jnp.float32), NEG)
        p = softmax_f32(sc).astype(vs.dtype)
        return jnp.einsum('bhtk,btkd->bthd', p, vs)

    out = lax.map(block, (to_blocks(q, Q_BLOCK), to_blocks(iq, Q_BLOCK),
                          to_blocks(iw, Q_BLOCK), starts))
    return from_blocks(out)


def rope_tables(s):
    inv = ROPE_BASE ** (-jnp.arange(0, C_ROPE, 2, dtype=jnp.float32) / C_ROPE)
    ang = jnp.arange(s, dtype=jnp.float32)[:, None] * inv[None, :]
    return jnp.cos(ang), jnp.sin(ang)


def apply_rope(x, cos, sin):
    x1, x2 = jnp.split(x.astype(jnp.float32), 2, axis=-1)
    c, s_ = cos[None, :, None], sin[None, :, None]
    return jnp.concatenate([x1 * c - x2 * s_, x1 * s_ + x2 * c], axis=-1).astype(x.dtype)


def mla_mixer(cq, ckv, krope, qa_gain, kva_gain, w_qb, w_kvb, q_gain, k_gain):
    bsz, s, _ = cq.shape
    cq = rms_norm(cq, qa_gain)
    ckv = rms_norm(ckv, kva_gain)
    q = (cq @ w_qb).reshape(bsz, s, C_HEADS, C_QK)
    kv = (ckv @ w_kvb).reshape(bsz, s, C_HEADS, C_NOPE + C_VDIM)
    k_nope, v = kv[..., :C_NOPE], kv[..., C_NOPE:]
    k = jnp.concatenate([k_nope, jnp.broadcast_to(krope[:, :, None], (bsz, s, C_HEADS, C_ROPE))], axis=-1)
    q = rms_norm(q, q_gain)
    k = rms_norm(k, k_gain)
    cos, sin = rope_tables(s)
    q = jnp.concatenate([q[..., :C_NOPE], apply_rope(q[..., C_NOPE:], cos, sin)], axis=-1)
    k = jnp.concatenate([k[..., :C_NOPE], apply_rope(k[..., C_NOPE:], cos, sin)], axis=-1)
    key_chunk = jnp.arange(s) // CHUNK
    starts = jnp.arange(s // Q_BLOCK) * Q_BLOCK

    def block(args):
        qb, start = args
        qchunk = (start + jnp.arange(Q_BLOCK)) // CHUNK
        mask = key_chunk[None, :] <= qchunk[:, None]
        sc = jnp.einsum('bthd,bshd->bhts', qb, k).astype(jnp.float32) * (C_QK ** -0.5)
        sc = jnp.where(mask[None, None], sc, NEG)
        p = softmax_f32(sc).astype(v.dtype)
        return jnp.einsum('bhts,bshd->bthd', p, v)

    out = lax.map(block, (to_blocks(q, Q_BLOCK), starts))
    return from_blocks(out)


def band_mixer(q, k, v, q_gain, k_gain, rel_bias):
    bsz, s, _ = q.shape
    nc = s // CHUNK
    q = rms_norm(q.reshape(bsz, s, D_HEADS, D_HDIM), q_gain)
    k = rms_norm(k.reshape(bsz, s, D_HEADS, D_HDIM), k_gain)
    v = v.reshape(bsz, s, D_HEADS, D_HDIM)
    pad = D_LEFT_CHUNKS * CHUNK
    kp = jnp.pad(k, ((0, 0), (pad, 0), (0, 0), (0, 0)))
    vp = jnp.pad(v, ((0, 0), (pad, 0), (0, 0), (0, 0)))
    i = jnp.arange(CHUNK)
    j = jnp.arange(D_BAND)
    dist = (pad + i)[:, None] - j[None, :]
    bias = rel_bias[jnp.clip(dist, -REL_CLIP, REL_CLIP) + REL_CLIP].transpose(2, 0, 1)
    bias = bias.astype(jnp.float32)

    def chunk(args):
        qb, c = args
        kb = lax.dynamic_slice_in_dim(kp, c * CHUNK, D_BAND, axis=1)
        vb = lax.dynamic_slice_in_dim(vp, c * CHUNK, D_BAND, axis=1)
        valid = j >= (D_LEFT_CHUNKS - c) * CHUNK
        sc = jnp.einsum('bthd,bshd->bhts', qb, kb).astype(jnp.float32) * (D_HDIM ** -0.5) + bias[None]
        sc = jnp.where(valid[None, None, None], sc, NEG)
        p = softmax_f32(sc).astype(vb.dtype)
        return jnp.einsum('bhts,bshd->bthd', p, vb)

    out = lax.map(chunk, (to_blocks(q, CHUNK), jnp.arange(nc)))
    return from_blocks(out)


def setup_inputs(seed: int = 0) -> dict:
    key = jax.random.key(seed)
    ks = jax.random.split(key, 19)

    def nrm(k, shape, scale):
        return scale * jax.random.normal(k, shape, jnp.float32)

    def gain(k, shape):
        return 1.0 + 0.05 * jax.random.normal(k, shape, jnp.float32)

    return {
        "x": nrm(ks[0], (BATCH, SEQ, D_MODEL), 1.0),
        "t5_bias": nrm(ks[1], (T5_BUCKETS, B_HEADS), 0.5),
        "norm_g": gain(ks[2], (DEPTH, D_MODEL)),
        "w_in": nrm(ks[3], (DEPTH, D_MODEL, IN_COLS), D_MODEL ** -0.5),
        "a_v_gain": gain(ks[4], (DEPTH, GROUP_W)),
        "a_ws": nrm(ks[5], (DEPTH, A_GROUPS, GMLP_BLOCK, GMLP_BLOCK), GMLP_BLOCK ** -0.5),
        "a_bs": 1.0 + nrm(ks[6], (DEPTH, A_GROUPS, GMLP_BLOCK), 0.1),
        "b_q_gain": gain(ks[7], (DEPTH, B_HDIM)),
        "b_k_gain": gain(ks[8], (DEPTH, B_HDIM)),
        "c_qa_gain": gain(ks[9], (DEPTH, Q_LORA)),
        "c_kva_gain": gain(ks[10], (DEPTH, KV_LORA)),
        "c_w_qb": nrm(ks[11], (DEPTH, Q_LORA, C_HEADS * C_QK), Q_LORA ** -0.5),
        "c_w_kvb": nrm(ks[12], (DEPTH, KV_LORA, C_HEADS * (C_NOPE + C_VDIM)), KV_LORA ** -0.5),
        "c_q_gain": gain(ks[13], (DEPTH, C_QK)),
        "c_k_gain": gain(ks[14], (DEPTH, C_QK)),
        "d_q_gain": gain(ks[15], (DEPTH, D_HDIM)),
        "d_k_gain": gain(ks[16], (DEPTH, D_HDIM)),
        "d_rel_bias": nrm(ks[17], (DEPTH, 2 * REL_CLIP + 1, D_HEADS), 0.5),
        "w_out": nrm(ks[18], (DEPTH, D_MIX, D_MODEL), D_MIX ** -0.5),
    }


def reference(x, t5_bias, norm_g, w_in, a_v_gain, a_ws, a_bs, b_q_gain, b_k_gain,
              c_qa_gain, c_kva_gain, c_w_qb, c_w_kvb, c_q_gain, c_k_gain,
              d_q_gain, d_k_gain, d_rel_bias, w_out):
    for l in range(DEPTH):
        h = rms_norm(x, norm_g[l]) @ w_in[l]
        (a_u, a_v, a_z,
         b_q, b_k, b_v, b_iq, b_ik, b_iw, b_z,
         c_q, c_kv, c_kr, c_z,
         d_q, d_k, d_v, d_z) = split_cols(h, IN_SIZES)
        y_a = gmlp_mixer(a_u, a_v, a_v_gain[l], a_ws[l], a_bs[l]) * jax.nn.silu(a_z)
        y_b = dsa_mixer(b_q, b_k, b_v, b_iq, b_ik, b_iw, b_q_gain[l], b_k_gain[l], t5_bias) * jax.nn.silu(b_z)
        y_c = mla_mixer(c_q, c_kv, c_kr, c_qa_gain[l], c_kva_gain[l], c_w_qb[l], c_w_kvb[l],
                        c_q_gain[l], c_k_gain[l]) * jax.nn.silu(c_z)
        y_d = band_mixer(d_q, d_k, d_v, d_q_gain[l], d_k_gain[l], d_rel_bias[l]) * jax.nn.silu(d_z)
        x = x + jnp.concatenate([y_a, y_b, y_c, y_d], axis=-1) @ w_out[l]
    return x
```

```python
import numpy as np
from contextlib import ExitStack

import concourse.bass as bass
import concourse.mybir as mybir
from concourse.bass_utils import run_bass_kernel_spmd

F32 = mybir.dt.float32
BF16 = mybir.dt.bfloat16
ALU = mybir.AluOpType
AF = mybir.ActivationFunctionType
AX = mybir.AxisListType

D_MODEL = 2048
SEQ = 2048
DEPTH = 2
NT = SEQ // 128
KC = D_MODEL // 128
IN_SIZES = (512, 512, 512, 512, 64, 64, 512, 64, 8, 512, 384, 128, 64, 512, 512, 512, 512, 512)
IN_COLS = sum(IN_SIZES)
OFF = np.concatenate([[0], np.cumsum(IN_SIZES)]).astype(int)
EPS = 1e-6
NEGM = -30000.0


class Op:
    __slots__ = ("eng", "fn", "dma", "deps", "ev", "prev", "signal")

    def __init__(self, eng, fn, dma):
        self.eng, self.fn, self.dma = eng, fn, dma
        self.deps = set()
        self.ev = None
        self.prev = None
        self.signal = False


class Sched:
    ENGS = ("sp", "act", "dve", "pool", "pe")
    NDS = 8

    def __init__(self, nc):
        self.nc = nc
        self.ops = {e: [] for e in self.ENGS}
        self.lastw = {}
        self.readers = {}
        self.dmas_since_barrier = []

    def op(self, eng, fn, r=(), w=(), dma=False):
        o = Op(eng, fn, dma)
        deps = {}

        def add(d, raw):
            if d is None or d is o:
                return
            deps[d] = deps.get(d, False) or raw

        for k in r:
            add(self.lastw.get(k), True)
        for k in w:
            add(self.lastw.get(k), False)
            for rd in self.readers.get(k, ()):
                add(rd, False)
        for d, raw in deps.items():
            if d.eng == eng and not d.dma and not dma:
                if eng == "pe":
                    continue
            o.deps.add(d)
            d.signal = True
        for k in r:
            self.readers.setdefault(k, []).append(o)
        for k in w:
            self.lastw[k] = o
            self.readers[k] = []
        self.ops[eng].append(o)
        if dma:
            self.dmas_since_barrier.append(o)
        return o

    def fence(self, eng, deps):
        o = Op(eng, None, False)
        for d in deps:
            if d is not None:
                o.deps.add(d)
                d.signal = True
        self.ops[eng].append(o)
        return o

    def barrier(self):
        last = [self.ops[e][-1] for e in self.ENGS if self.ops[e] and self.ops[e][-1].fn is not None]
        dm = list(self.dmas_since_barrier)
        self.dmas_since_barrier = []
        for e in self.ENGS:
            self.fence(e, [d for d in last + dm if not (d.eng == e and not d.dma and e == "pe")])

    def emit(self, es):
        nc = self.nc
        sem_eng = {e: es.enter_context(nc.semaphore("s_" + e)) for e in self.ENGS}
        sem_dma = {e: [es.enter_context(nc.semaphore("d_%s%d" % (e, k))) for k in range(self.NDS)]
                   for e in ("sp", "act", "pool")}
        for e in self.ENGS:
            cnt = 0
            dcnt = 0
            for o in self.ops[e]:
                if o.fn is None:
                    continue
                if o.dma:
                    k = dcnt % self.NDS
                    o.ev = (sem_dma[e][k], 16 * (dcnt // self.NDS + 1))
                    o.prev = (sem_dma[e][k], 16 * (dcnt // self.NDS))
                    dcnt += 1
                elif o.signal:
                    cnt += 1
                    o.ev = (sem_eng[e], cnt)
        block = es.enter_context(nc.Block())

        def run(e, eng):
            known = {}
            for o in self.ops[e]:
                waits = {}
                for d in o.deps:
                    s, v = d.ev
                    waits[s] = max(waits.get(s, 0), v)
                if o.dma and o.prev[1] > 0:
                    s, v = o.prev
                    waits[s] = max(waits.get(s, 0), v)
                for s, v in waits.items():
                    if known.get(s, 0) < v:
                        eng.wait_ge(s, v)
                        known[s] = v
                if o.fn is None:
                    continue
                inst = o.fn(eng)
                if o.dma:
                    inst.then_inc(o.ev[0], 16)
                elif o.signal:
                    inst.then_inc(o.ev[0], 1)

        block.sync(lambda eng: run("sp", eng))
        block.scalar(lambda eng: run("act", eng))
        block.vector(lambda eng: run("dve", eng))
        block.gpsimd(lambda eng: run("pool", eng))
        block.tensor(lambda eng: run("pe", eng))


class Arena:
    def __init__(self, nc, base=16640, top=229376 - 128):
        self.nc, self.off, self.top, self.n = nc, base, top, 0
        self.mode = "bottom"

    def alloc(self, name, shape, dtype):
        isz = 2 if dtype == BF16 else 4
        size = isz * int(np.prod(shape[1:]))
        size = (size + 63) // 64 * 64
        assert self.off + size <= self.top, ("SBUF overflow", name, self.off, self.top, size)
        self.n += 1
        if self.mode == "top":
            self.top -= size
            at = self.top
        else:
            at = self.off
            self.off += size
        return self.nc.alloc_sbuf_tensor_at("%s_%d" % (name, self.n), list(shape), dtype, offset=at)

    def mark(self):
        return self.top if self.mode == "top" else self.off

    def release(self, m):
        if self.mode == "top":
            self.top = m
        else:
            self.off = m


class K:
    pass


def ps_next(k, pool="all"):
    if pool == "all":
        i = k.ps_i % 4
        k.ps_i += 1
    elif pool == "lo":
        i = k.ps_lo % 2
        k.ps_lo += 1
    else:
        i = 2 + k.ps_hi % 2
        k.ps_hi += 1
    return k.psf[i], ("psf", i)


def pst_next(k):
    i = k.pst_i % 2
    k.pst_i += 1
    return k.pstt[i], ("pst", i)


def evac_eng(k):
    k.ev_i += 1
    return "act" if k.ev_i % 2 else "dve"


def copy_op(S, eng, out, in_, r, w):
    if eng == "act":
        return S.op("act", lambda e: e.activation(out=out, in_=in_, func=AF.Copy), r=r, w=w)
    return S.op(eng, lambda e: e.tensor_copy(out=out, in_=in_), r=r, w=w)


def transpose_to(k, blocks, dst, dst_key, eng=None, np_out=128, extra_r=()):
    S = k.S
    n = len(blocks)
    pt, pk = pst_next(k)
    for bi, (ap, key) in enumerate(blocks):
        S.op("pe", lambda e, ap=ap, bi=bi: e.transpose(out=pt[0:np_out, bi * 128:(bi + 1) * 128], in_=ap, identity=k.ident[:]),
             r=[key, "ident"] + list(extra_r), w=[pk])
    src = pt[0:np_out, 0:n * 128].rearrange("p (n t) -> p n t", n=n)
    copy_op(S, eng or evac_eng(k), dst, src, r=[pk], w=[dst_key])


def rstd_op(k, ss, n_feat, out, key_ss, key_out):
    S = k.S
    S.op("act", lambda e: e.activation(out=ss, in_=ss, func=AF.Ln, scale=1.0 / n_feat, bias=k.eps_t[:, 0:1]), r=[key_ss, "eps_t"], w=[key_ss])
    S.op("act", lambda e: e.activation(out=out, in_=ss, func=AF.Exp, scale=-0.5), r=[key_ss], w=[key_out])


def phase_norm(k, l, x_src, xnT):
    S, A = k.S, k.A
    A.mode = "top"
    m = A.mark()
    xt = [A.alloc("xt", [128, D_MODEL], F32) for _ in range(2)]
    xn = [A.alloc("xn", [128, D_MODEL], BF16) for _ in range(2)]
    junk = A.alloc("junk", [128, D_MODEL], BF16)
    ss = A.alloc("ss", [128, NT], F32)
    rs = A.alloc("rs", [128, NT], F32)
    gsb = A.alloc("gsb", [128, D_MODEL], F32)
    S.op("sp", lambda e: e.dma_start(out=gsb[:], in_=k.d["norm_g"][l].partition_broadcast(128)), w=["gsb"], dma=True)
    S.op("dve", lambda e: e.memset(ss[:], 0.0), w=[("ss", i) for i in range(NT)])
    for i in range(NT):
        b = i % 2
        S.op("sp", lambda e, i=i, b=b: e.dma_start(out=xt[b][:], in_=x_src[i * 128:(i + 1) * 128, :]), w=[("xt", b)], dma=True)
        S.op("act", lambda e, i=i, b=b: e.activation(out=junk[:], in_=xt[b][:], func=AF.Square, accum_out=ss[:, i:i + 1]),
             r=[("xt", b)], w=["junk", ("ss", i)])
        rstd_op(k, ss[:, i:i + 1], D_MODEL, rs[:, i:i + 1], ("ss", i), ("rs", i))
        S.op("dve", lambda e, i=i, b=b: e.scalar_tensor_tensor(out=xn[b][:], in0=xt[b][:], scalar=rs[:, i:i + 1], in1=gsb[:], op0=ALU.mult, op1=ALU.mult),
             r=[("xt", b), ("rs", i), "gsb"], w=[("xn", b)])
        for half in range(2):
            pt, pk = pst_next(k)
            for j in range(8):
                kc = half * 8 + j
                S.op("pe", lambda e, kc=kc, j=j, b=b, pt=pt: e.transpose(out=pt[:, j * 128:(j + 1) * 128], in_=xn[b][:, kc * 128:(kc + 1) * 128], identity=k.ident[:]),
                     r=[("xn", b), "ident"], w=[pk])
            src = pt[:, :].rearrange("p (n t) -> p n t", n=8)
            dst = xnT[:, half * 8:(half + 1) * 8, i * 128:(i + 1) * 128]
            copy_op(S, evac_eng(k), dst, src, r=[pk], w=[("xnT", i)])
    A.release(m)
    A.mode = "bottom"


def proj_bufs(A):
    return ([A.alloc("wst", [128, KC, 512], F32)], [A.alloc("wbf", [128, KC, 512], BF16) for _ in range(2)])


def phase_proj(k, lhsT, lhs_key, w_src, ncols, evac, bufs=None, order=None, hook=None, cast_engs=("dve", "act"), ps_pool="all"):
    S, A = k.S, k.A
    m = A.mark()
    wst, wbf = bufs if bufs is not None else proj_bufs(A)
    ncb = (ncols + 511) // 512
    order = list(order) if order is not None else list(range(ncb))
    wv = w_src.rearrange("(kc p) c -> p kc c", p=128)

    def load_w(pos):
        cb = order[pos]
        c0 = cb * 512
        cw = min(512, ncols - c0)
        b = pos % 2
        for q in range(4):
            S.op("sp", lambda e, q=q: e.dma_start(out=wst[0][:, q * 4:(q + 1) * 4, 0:cw], in_=wv[:, q * 4:(q + 1) * 4, c0:c0 + cw]),
                 w=[("wst", q)], dma=True)
            copy_op(S, cast_engs[q % 2], wbf[b][:, q * 4:(q + 1) * 4, 0:cw], wst[0][:, q * 4:(q + 1) * 4, 0:cw],
                    r=[("wst", q)], w=[("wbf", b, q)])

    load_w(0)
    for pos, cb in enumerate(order):
        c0 = cb * 512
        cw = min(512, ncols - c0)
        b = pos % 2
        if pos + 1 < len(order):
            load_w(pos + 1)
        for i in range(NT):
            ps, pk = ps_next(k, ps_pool)
            for kc in range(KC):
                S.op("pe", lambda e, kc=kc, i=i, b=b, cw=cw, ps=ps: e.matmul(ps[:, 0:cw], lhsT[:, kc, i * 128:(i + 1) * 128], wbf[b][:, kc, 0:cw],
                                                                          start=(kc == 0), stop=(kc == KC - 1)),
                     r=[lhs_key(i), ("wbf", b, kc // 4)], w=[pk])
            evac(i, c0, cw, ps, pk)
            if hook is not None:
                hook(pos, i)
    A.release(m)


def phase_inproj(k, l, xnT, bufs=None, hsb=None, order=None, hook=None, keep_dve_free=False):
    S, A = k.S, k.A
    m = A.mark()
    if hsb is None:
        hsb = [A.alloc("hsb", [128, 512], BF16) for _ in range(4)]
    cnt = [0]

    def evac(i, c0, cw, ps, pk):
        b = cnt[0] % 4
        cnt[0] += 1
        copy_op(S, "act" if keep_dve_free else evac_eng(k), hsb[b][:, 0:cw], ps[:, 0:cw], r=[pk], w=[("hsb", b)])
        S.op("pool", lambda e: e.dma_start(out=k.h_d[i * 128:(i + 1) * 128, c0:c0 + cw], in_=hsb[b][:, 0:cw]),
             r=[("hsb", b)], w=[("h_d", i, c0 // 512)], dma=True)

    phase_proj(k, xnT, lambda i: ("xnT", i), k.d["w_in"][l], IN_COLS, evac, bufs, order, hook,
               cast_engs=("act", "act") if keep_dve_free else ("dve", "act"), ps_pool="lo" if keep_dve_free else "all")
    A.release(m)


def phase_outproj(k, l, yT, x_src, x_dst):
    S, A = k.S, k.A
    m = A.mark()
    xr = [A.alloc("xr", [128, 512], F32) for _ in range(3)]
    cnt = [0]

    def evac(i, c0, cw, ps, pk):
        b = cnt[0] % 3
        cnt[0] += 1
        S.op("sp", lambda e: e.dma_start(out=xr[b][:], in_=x_src[i * 128:(i + 1) * 128, c0:c0 + 512]), w=[("xr", b)], dma=True)
        S.op("dve", lambda e: e.tensor_tensor(out=xr[b][:], in0=ps[:, :], in1=xr[b][:], op=ALU.add), r=[pk, ("xr", b)], w=[("xr", b)])
        o = S.op("pool", lambda e: e.dma_start(out=x_dst[i * 128:(i + 1) * 128, c0:c0 + 512], in_=xr[b][:]),
                 r=[("xr", b)], w=[("xdst", i, c0 // 512)], dma=True)
        k.out_dmas.append(o)

    phase_proj(k, yT, lambda i: ("yT", i), k.d["w_out"][l], D_MODEL, evac)
    A.release(m)


def load_h(k, eng, dst, i, c0, cw, wkey):
    rk = [("h_d", i, cb) for cb in range(c0 // 512, (c0 + cw - 1) // 512 + 1)]
    return k.S.op(eng, lambda e: e.dma_start(out=dst, in_=k.h_d[i * 128:(i + 1) * 128, c0:c0 + cw]), r=rk, w=[wkey], dma=True)


def y_to_yT(k, ysb, ykey, yT, g, i):
    blocks = [(ysb[:, j * 128:(j + 1) * 128], ykey) for j in range(4)]
    transpose_to(k, blocks, yT[:, 4 * g:4 * g + 4, i * 128:(i + 1) * 128], ("yT", i))


def mixer_stub(k, l, yT, g):
    S, A = k.S, k.A
    m = A.mark()
    yb = [A.alloc("yb", [128, 512], BF16) for _ in range(2)]
    for i in range(NT):
        b = i % 2
        load_h(k, "sp", yb[b][:], i, 512 * g, 512, ("yb", b))
        y_to_yT(k, yb[b], ("yb", b), yT, g, i)
    A.release(m)


def mixer_a_gen(k, l, yT, ps_pool="all"):
    S, A, d = k.S, k.A, k.d
    m = A.mark()
    wsf = A.alloc("a_wsf", [128, 4, 128], F32)
    wsb = A.alloc("a_wsb", [128, 4, 128], BF16)
    bsb = A.alloc("a_bsb", [128, 4], F32)
    vg = A.alloc("a_vg", [128, 512], F32)
    ss = A.alloc("a_ss", [128, NT], F32)
    rs = A.alloc("a_rs", [128, NT], F32)
    hin = [A.alloc("a_hin", [128, 1536], BF16) for _ in range(2)]
    guv = [A.alloc("a_guv", [128, 1024], F32) for _ in range(2)]
    junk = A.alloc("a_junk", [128, 512], BF16)
    vn = [A.alloc("a_vn", [128, 512], BF16) for _ in range(2)]
    t1 = [A.alloc("a_t1", [128, 512], F32) for _ in range(2)]
    sz = [A.alloc("a_sz", [128, 512], F32) for _ in range(2)]
    ysb = [A.alloc("a_y", [128, 512], BF16) for _ in range(2)]
    S.op("sp", lambda e: e.dma_start(out=wsf[:], in_=d["a_wsT"][l]), w=["a_wsf"], dma=True)
    S.op("sp", lambda e: e.dma_start(out=bsb[:], in_=d["a_bsT"][l]), w=["a_bsb"], dma=True)
    S.op("sp", lambda e: e.dma_start(out=vg[:], in_=d["a_v_gain"][l].partition_broadcast(128)), w=["a_vg"], dma=True)
    S.op("dve", lambda e: e.tensor_copy(out=wsb[:], in_=wsf[:]), r=["a_wsf"], w=["a_wsb"])
    S.op("dve", lambda e: e.memset(wsb[64:128, :, 0:64], 0.0), w=["a_wsb"])
    S.op("dve", lambda e: e.memset(ss[:], 0.0), w=[("a_ss", i) for i in range(NT)])
    def sA(i):
        b = i % 2
        load_h(k, "sp", hin[b][:], i, 0, 1536, ("a_hin", b))
        S.op("act", lambda e, b=b: e.activation(out=guv[b][:], in_=hin[b][:, 0:1024], func=AF.Gelu_apprx_tanh),
             r=[("a_hin", b)], w=[("a_guv", b)])
        S.op("act", lambda e, b=b, i=i: e.activation(out=junk[:], in_=guv[b][:, 512:1024], func=AF.Square, accum_out=ss[:, i:i + 1]),
             r=[("a_guv", b)], w=["a_junk", ("a_ss", i)])
        rstd_op(k, ss[:, i:i + 1], 512, rs[:, i:i + 1], ("a_ss", i), ("a_rs", i))
        S.op("dve", lambda e, b=b, i=i: e.scalar_tensor_tensor(out=vn[b][:], in0=guv[b][:, 512:1024], scalar=rs[:, i:i + 1], in1=vg[:],
                                                             op0=ALU.mult, op1=ALU.mult),
             r=[("a_guv", b), ("a_rs", i), "a_vg"], w=[("a_vn", b)])

    def sB(i):
        b = i % 2
        ps, pk = ps_next(k, ps_pool)
        for g in range(4):
            S.op("pe", lambda e, g=g, b=b, ps=ps: e.matmul(ps[:, g * 128:(g + 1) * 128], wsb[:, g, :], vn[b][:, g * 128:(g + 1) * 128], start=True, stop=True),
                 r=["a_wsb", ("a_vn", b)], w=[pk])
        for g in range(4):
            S.op("dve", lambda e, g=g, b=b, ps=ps: e.scalar_tensor_tensor(out=t1[b][:, g * 128:(g + 1) * 128], in0=ps[:, g * 128:(g + 1) * 128],
                                                                       scalar=bsb[:, g:g + 1], in1=guv[b][:, g * 128:(g + 1) * 128],
                                                                       op0=ALU.add, op1=ALU.mult),
                 r=[pk, "a_bsb", ("a_guv", b)], w=[("a_t1", b)])
        S.op("act", lambda e, b=b: e.activation(out=sz[b][:], in_=hin[b][:, 1024:1536], func=AF.Silu), r=[("a_hin", b)], w=[("a_sz", b)])
        S.op("pool", lambda e, b=b: e.tensor_tensor(out=ysb[b][:], in0=sz[b][:], in1=t1[b][:], op=ALU.mult), r=[("a_t1", b), ("a_sz", b)], w=[("a_y", b)])

    def sC(i):
        b = i % 2
        y_to_yT(k, ysb[b], ("a_y", b), yT, 0, i)

    for n in range(NT + 2):
        if n < NT:
            sA(n)
        if 0 <= n - 1 < NT:
            sB(n - 1)
        if 0 <= n - 2 < NT:
            sC(n - 2)
        yield
    A.release(m)


def mixer_a(k, l, yT):
    for _ in mixer_a_gen(k, l, yT):
        pass


def head_norm(k, src, nh, hd, gain_full, out, ss, rs, tmp, rkeys, key_ss, key_rs, key_tmp, key_out, sq, gkey="gfull"):
    S = k.S
    S.op("act", lambda e: e.activation(out=sq, in_=src, func=AF.Square), r=rkeys, w=[key_tmp + ("sq",)])
    S.op("dve", lambda e: e.tensor_reduce(out=ss, in_=sq.rearrange("p (h d) -> p h d", d=hd), axis=AX.X, op=ALU.add),
         r=[key_tmp + ("sq",)], w=[key_ss])
    rstd_op(k, ss, hd, rs, key_ss, key_rs)
    S.op("dve", lambda e: e.tensor_tensor(out=tmp.rearrange("p (h d) -> p h d", d=hd), in0=src.rearrange("p (h d) -> p h d", d=hd),
                                          in1=rs.unsqueeze(2).to_broadcast([128, nh, hd]), op=ALU.mult),
         r=rkeys + [key_rs], w=[key_tmp])
    S.op("pool", lambda e: e.tensor_tensor(out=out, in0=tmp, in1=gain_full, op=ALU.mult), r=[key_tmp, gkey], w=[key_out])


def rep_gain(k, gfull, gain, nh, hd, gkey="gfull"):
    for h in range(nh):
        k.S.op("pool", lambda e, h=h: e.tensor_copy(out=gfull[:, h * hd:(h + 1) * hd], in_=gain), r=["gains"], w=[gkey])


def gate_and_store(k, l, yv, yv_key, c0z, i, g, yT, bufs, b):
    S = k.S
    zin, sz, ysb = bufs["zin"][b], bufs["sz"][b], bufs["ysb"][b]
    load_h(k, "sp", zin[:], i, c0z, 512, ("g_zin", b))
    S.op("act", lambda e: e.activation(out=sz[:], in_=zin[:], func=AF.Silu), r=[("g_zin", b)], w=[("g_sz", b)])
    S.op("pool", lambda e: e.tensor_tensor(out=ysb[:], in0=yv, in1=sz[:], op=ALU.mult), r=[yv_key, ("g_sz", b)], w=[("g_y", b)])
    y_to_yT(k, ysb, ("g_y", b), yT, g, i)


def gate_bufs(A):
    return {"zin": [A.alloc("g_zin", [128, 512], BF16) for _ in range(2)],
            "sz": [A.alloc("g_sz", [128, 512], F32) for _ in range(2)],
            "ysb": [A.alloc("g_y", [128, 512], BF16) for _ in range(2)]}


def run_pipelined(pairs, stage1, stage2):
    prev = None
    n = 0
    for p in pairs:
        if "pre" in p:
            stage1(p, 0)
            continue
        stage1(p, n % 2)
        if prev is not None:
            stage2(prev, (n - 1) % 2)
        prev = p
        n += 1
    if prev is not None:
        stage2(prev, (n - 1) % 2)


def finalize_heads(k, pfx, psO, pko, rden, yv_b, yv_key, nh_bank, hd, slot):
    S = k.S
    for hg in range(2):
        pv = psO[hg][:, :].rearrange("p (h c) -> p h c", c=slot)
        S.op("dve", lambda e, hg=hg, pv=pv: e.reciprocal(out=rden[:, hg * nh_bank:(hg + 1) * nh_bank], in_=pv[:, :, hd]), r=[pko[hg]], w=[(pfx + "_rden", hg)])
        S.op("dve", lambda e, hg=hg, pv=pv: e.tensor_tensor(out=yv_b[:, hg * nh_bank * hd:(hg + 1) * nh_bank * hd].rearrange("p (h c) -> p h c", c=hd), in0=pv[:, :, 0:hd],
                                                          in1=rden[:, hg * nh_bank:(hg + 1) * nh_bank].unsqueeze(2).to_broadcast([128, nh_bank, hd]), op=ALU.mult),
             r=[pko[hg], (pfx + "_rden", hg)], w=[yv_key])


def mixer_d(k, l, yT):
    S, A, d = k.S, k.A, k.d
    m = A.mark()
    c0 = int(OFF[14])
    qkT = A.alloc("d_qkT", [128, 8, SEQ], BF16)
    vaug = A.alloc("d_vaug", [128, NT, 8, 80], BF16)
    bias5 = A.alloc("d_bias5", [128, 5, 8, 128], F32)
    gains = A.alloc("d_gains", [128, 2, 64], F32)
    S.op("sp", lambda e: e.dma_start(out=bias5[:], in_=d["d_bias5"][l]), w=["d_bias5"], dma=True)
    S.op("sp", lambda e: e.dma_start(out=gains[:, 0, :], in_=d["d_q_gain"][l].partition_broadcast(128)), w=["gains"], dma=True)
    S.op("sp", lambda e: e.dma_start(out=gains[:, 1, :], in_=d["d_k_gain"][l].partition_broadcast(128)), w=["gains"], dma=True)
    S.op("dve", lambda e: e.tensor_scalar(out=gains[:, 0, :], in0=gains[:, 0, :], scalar1=0.125, scalar2=None, op0=ALU.mult), r=["gains"], w=["gains"])
    S.op("pool", lambda e: e.memset(vaug[:, :, :, 64:65], 1.0), w=[("d_v", j) for j in range(NT)])
    gfull = A.alloc("d_gfull", [128, 2, 512], F32)
    for qk in range(2):
        rep_gain(k, gfull[:, qk, :], gains[:, qk, :], 8, 64)
    m1 = A.mark()
    qkv = [A.alloc("d_qkv", [128, 1536], BF16) for _ in range(2)]
    sq = A.alloc("d_sq", [128, 1024], F32)
    tmp = A.alloc("d_tmp", [128, 1024], F32)
    qkn = [A.alloc("d_qkn", [128, 1024], BF16) for _ in range(2)]
    ss = A.alloc("d_ss", [128, NT, 16], F32)
    rs = A.alloc("d_rs", [128, NT, 16], F32)
    def s1a(i):
        b = i % 2
        load_h(k, "sp", qkv[b][:], i, c0, 1536, ("d_qkv", b))
        for qk in range(2):
            head_norm(k, qkv[b][:, qk * 512:(qk + 1) * 512], 8, 64, gfull[:, qk, :], qkn[b][:, qk * 512:(qk + 1) * 512],
                      ss[:, i, qk * 8:(qk + 1) * 8], rs[:, i, qk * 8:(qk + 1) * 8], tmp[:, qk * 512:(qk + 1) * 512],
                      [("d_qkv", b)], ("d_ss", i, qk), ("d_rs", i, qk), ("d_tmp", qk), ("d_qkn", b, qk), sq[:, qk * 512:(qk + 1) * 512])

    def s1b(i):
        b = i % 2
        blocks = [(qkn[b][:, c * 128:(c + 1) * 128], ("d_qkn", b, c // 4)) for c in range(8)]
        transpose_to(k, blocks, qkT[:, :, i * 128:(i + 1) * 128], ("d_qkT", i))
        S.op("pool", lambda e, i=i, b=b: e.tensor_copy(out=vaug[:, i, :, 0:64], in_=qkv[b][:, 1024:1536].rearrange("p (h d) -> p h d", d=64)),
             r=[("d_qkv", b)], w=[("d_v", i)])
    ein = [A.alloc("d_ein", [128, 1024], F32) for _ in range(2)]
    expT = [A.alloc("d_expT", [128, 8, 128], BF16) for _ in range(2)]
    rden = A.alloc("d_rden", [128, 8], F32)
    yv = [A.alloc("d_yv", [128, 512], F32) for _ in range(2)]
    gb = gate_bufs(A)
    psO = [k.psf[4], k.psf[5]]
    pko = [("psf", 4), ("psf", 5)]
    pairs = [dict(pre=(0, "a")), dict(pre=(0, "b"))]
    for i in range(NT):
        P = [dict(i=i, j=j, first=(j == max(0, i - 4)), last=(j == i)) for j in range(max(0, i - 4), i + 1)]
        n = len(P)
        if i + 1 < NT:
            P = [dict(pre=(i + 1, "a"))] + P[0:(n + 1) // 2] + [dict(pre=(i + 1, "b"))] + P[(n + 1) // 2:]
        pairs += P
    s1 = {"a": s1a, "b": s1b}

    def stage1(p, b2):
        if "pre" in p:
            s1[p["pre"][1]](p["pre"][0])
            return
        i, j = p["i"], p["j"]
        dl = i - j
        pss = [ps_next(k), ps_next(k)]
        for h in range(8):
            ps, pk = pss[h % 2]
            hp = h % 2
            S.op("pe", lambda e, h=h, hp=hp, ps=ps: e.matmul(ps[:, (h // 2) * 128:(h // 2 + 1) * 128],
                                                           qkT[hp * 64:(hp + 1) * 64, 4 + h // 2, j * 128:(j + 1) * 128],
                                                           qkT[hp * 64:(hp + 1) * 64, h // 2, i * 128:(i + 1) * 128], start=True, stop=True),
                 r=[("d_qkT", i), ("d_qkT", j)], w=[pk])
        for hg in range(2):
            ps, pk = pss[hg]
            S.op("dve", lambda e, hg=hg, ps=ps: e.tensor_tensor(
                out=ein[b2][:, hg * 512:(hg + 1) * 512], in0=ps[:, :],
                in1=bias5[:, dl, hg * 4:(hg + 1) * 4, :].rearrange("p h t -> p (h t)"), op=ALU.add),
                r=[pk, "d_bias5"], w=[("d_ein", b2, hg)])
            S.op("act", lambda e, hg=hg: e.activation(out=expT[b2][:, hg * 4:(hg + 1) * 4, :].rearrange("p h t -> p (h t)"),
                                                    in_=ein[b2][:, hg * 512:(hg + 1) * 512], func=AF.Exp),
                 r=[("d_ein", b2, hg)], w=[("d_expT", b2, hg)])

    def stage2(p, b2):
        if "pre" in p:
            return
        i, j = p["i"], p["j"]
        for h in range(8):
            po = psO[h // 4]
            S.op("pe", lambda e, h=h, po=po: e.matmul(po[:, (h % 4) * 128:(h % 4) * 128 + 65], expT[b2][:, (h % 2) * 4 + h // 2, :], vaug[:, j, h, 0:65],
                                                    start=(p["first"] and h % 4 == 0), stop=(p["last"] and h % 4 == 3), skip_group_check=True),
                 r=[("d_expT", b2, h % 2), ("d_v", j)], w=[pko[h // 4]])
        if p["last"]:
            b = i % 2
            finalize_heads(k, "d", psO, pko, rden, yv[b], ("d_yv", b), 4, 64, 128)
            gate_and_store(k, l, yv[b][:], ("d_yv", b), c0 + 1536, i, 3, yT, gb, b)

    run_pipelined(pairs, stage1, stage2)
    A.release(m)


def bcast_load(k, dst, src_vec, key):
    return k.S.op("sp", lambda e: e.dma_start(out=dst, in_=src_vec.partition_broadcast(128)), w=[key], dma=True)


def mixer_c(k, l, yT):
    S, A, d = k.S, k.A, k.d
    m = A.mark()
    c0 = int(OFF[10])
    QKn = A.alloc("c_QKn", [128, 8, SEQ], BF16)
    QKr = A.alloc("c_QKr", [128, 4, SEQ], BF16)
    vaug = A.alloc("c_vaug", [128, NT, 4, 144], BF16)
    S.op("pool", lambda e: e.memset(vaug[:, :, :, 128:129], 1.0), w=[("c_v", j) for j in range(NT)])
    m1 = A.mark()
    wstg = A.alloc("c_wstg", [128, 1024], F32)
    wqb = A.alloc("c_wqb", [128, 3, 768], BF16)
    wkb = A.alloc("c_wkb", [128, 1024], BF16)
    g_qa = A.alloc("c_gqa", [128, 384], F32)
    g_kva = A.alloc("c_gkva", [128, 128], F32)
    g_q = A.alloc("c_gq", [128, 192], F32)
    g_k = A.alloc("c_gk", [128, 192], F32)
    cs = A.alloc("c_cs", [128, NT, 64], F32)
    S.op("sp", lambda e: e.dma_start(out=cs[:], in_=d["rope_cs"]), w=["c_cs"], dma=True)
    bcast_load(k, g_qa[:], d["c_qa_gain"][l], "gains")
    bcast_load(k, g_kva[:], d["c_kva_gain"][l], "gains")
    bcast_load(k, g_q[:], d["c_q_gain"][l], "gains")
    bcast_load(k, g_k[:], d["c_k_gain"][l], "gains")
    gq_full = A.alloc("c_gqf", [128, 768], F32)
    gk_full = A.alloc("c_gkf", [128, 768], F32)
    rep_gain(k, gq_full[:], g_q[:], 4, 192)
    rep_gain(k, gk_full[:], g_k[:], 4, 192)
    for kc in range(3):
        S.op("sp", lambda e, kc=kc: e.dma_start(out=wstg[:, 0:768], in_=d["c_w_qb"][l][kc * 128:(kc + 1) * 128, :]), w=["c_wstg"], dma=True)
        S.op("dve", lambda e, kc=kc: e.tensor_copy(out=wqb[:, kc, :], in_=wstg[:, 0:768]), r=["c_wstg"], w=["c_wqb"])
    S.op("sp", lambda e: e.dma_start(out=wstg[:, :], in_=d["c_w_kvb"][l]), w=["c_wstg"], dma=True)
    S.op("dve", lambda e: e.tensor_copy(out=wkb[:], in_=wstg[:, :]), r=["c_wstg"], w=["c_wkb"])
    hin = [A.alloc("c_hin", [128, 576], BF16) for _ in range(2)]
    sq = A.alloc("c_sq", [128, 768], F32)
    tmp = A.alloc("c_tmp", [128, 768], F32)
    lat = [A.alloc("c_lat", [128, 512], BF16) for _ in range(2)]
    latT = [A.alloc("c_latT", [128, 4, 128], BF16) for _ in range(2)]
    qkf = A.alloc("c_qkf", [128, 8, 192], F32)
    qkn = A.alloc("c_qkn", [128, 8, 192], F32)
    qkb = [A.alloc("c_qkb", [128, 8, 128], BF16) for _ in range(2)]
    qkr = [A.alloc("c_qkr", [128, 8, 64], BF16) for _ in range(2)]
    rt = [A.alloc("c_rt", [128, 8, 32], F32) for _ in range(4)]
    ss = A.alloc("c_ss", [128, NT, 16], F32)
    rs = A.alloc("c_rs", [128, NT, 16], F32)
    def s1a(i):
        b = i % 2
        load_h(k, "sp", hin[b][:], i, c0, 576, ("c_hin", b))
        head_norm(k, hin[b][:, 0:384], 1, 384, g_qa[:], lat[b][:, 0:384], ss[:, i, 0:1], rs[:, i, 0:1], tmp[:, 0:384],
                  [("c_hin", b)], ("c_ss", i, 0), ("c_rs", i, 0), ("c_tmp", 2), ("c_lat", b, 0), sq[:, 0:384], gkey="gains")
        head_norm(k, hin[b][:, 384:512], 1, 128, g_kva[:], lat[b][:, 384:512], ss[:, i, 1:2], rs[:, i, 1:2], tmp[:, 384:512],
                  [("c_hin", b)], ("c_ss", i, 1), ("c_rs", i, 1), ("c_tmp", 2), ("c_lat", b, 1), sq[:, 384:512], gkey="gains")

    def s1b(i):
        b = i % 2
        blocks = [(lat[b][:, c * 128:(c + 1) * 128], ("c_lat", b, 0 if c < 3 else 1)) for c in range(4)]
        transpose_to(k, blocks, latT[b][:, :, :], ("c_latT", b))
        pq = [ps_next(k), ps_next(k)]
        for nh, (n0, n1) in enumerate(((0, 512), (512, 768))):
            ps, pk = pq[nh]
            for kc in range(3):
                S.op("pe", lambda e, ps=ps, kc=kc, n0=n0, n1=n1, b=b: e.matmul(ps[:, 0:n1 - n0], latT[b][:, kc, :], wqb[:, kc, n0:n1], start=(kc == 0), stop=(kc == 2)),
                     r=[("c_latT", b), "c_wqb"], w=[pk])
        pkv = [ps_next(k), ps_next(k)]
        for nh in range(2):
            ps, pk = pkv[nh]
            S.op("pe", lambda e, ps=ps, nh=nh, b=b: e.matmul(ps[:, :], latT[b][:, 3, :], wkb[:, nh * 512:(nh + 1) * 512], start=True, stop=True),
                 r=[("c_latT", b), "c_wkb"], w=[pk])
        qflat = qkf[:, 0:4, :].rearrange("p h c -> p (h c)")
        copy_op(S, "act", qflat[:, 0:512], pq[0][0][:, :], r=[pq[0][1]], w=[("c_qkf", 0)])
        copy_op(S, "act", qflat[:, 512:768], pq[1][0][:, 0:256], r=[pq[1][1]], w=[("c_qkf", 0)])
        for nh in range(2):
            pv = pkv[nh][0][:, :].rearrange("p (h c) -> p h c", c=256)
            copy_op(S, "dve", qkf[:, 4 + 2 * nh:6 + 2 * nh, 0:128], pv[:, :, 0:128], r=[pkv[nh][1]], w=[("c_qkf", 1)])
            copy_op(S, "dve", vaug[:, i, 2 * nh:2 * nh + 2, 0:128], pv[:, :, 128:256], r=[pkv[nh][1]], w=[("c_v", i)])
        for h in range(4):
            S.op("pool", lambda e, b=b, h=h: e.tensor_copy(out=qkf[:, 4 + h, 128:192], in_=hin[b][:, 512:576]),
                 r=[("c_hin", b)], w=[("c_qkf", 1)])
        for qk, gain in ((0, gq_full), (1, gk_full)):
            head_norm(k, qkf[:, 4 * qk:4 * qk + 4, :].rearrange("p h c -> p (h c)"), 4, 192, gain[:],
                      qkn[:, 4 * qk:4 * qk + 4, :].rearrange("p h c -> p (h c)"), ss[:, i, 4 + 4 * qk:8 + 4 * qk], rs[:, i, 4 + 4 * qk:8 + 4 * qk],
                      tmp[:, :], [("c_qkf", qk)], ("c_ss", i, 2 + qk), ("c_rs", i, 2 + qk), ("c_tmp", 2), ("c_qkn", qk), sq[:, :])
        rq = [("c_qkn", 0), ("c_qkn", 1)]
        S.op("pool", lambda e, b=b: e.tensor_copy(out=qkb[b][:, :, :], in_=qkn[:, :, 0:128]), r=rq, w=[("c_qkb", b)])
        x1, x2 = qkn[:, :, 128:160], qkn[:, :, 160:192]
        cc = cs[:, i, 0:32].unsqueeze(1).to_broadcast([128, 8, 32])
        sn = cs[:, i, 32:64].unsqueeze(1).to_broadcast([128, 8, 32])
        for ti, (xa, tb) in enumerate(((x1, cc), (x2, sn), (x1, sn), (x2, cc))):
            S.op("dve", lambda e, ti=ti, xa=xa, tb=tb: e.tensor_tensor(out=rt[ti][:], in0=xa, in1=tb, op=ALU.mult), r=rq + ["c_cs"], w=[("c_rt", ti)])
        S.op("pool", lambda e, b=b: e.tensor_tensor(out=qkr[b][:, :, 0:32], in0=rt[0][:], in1=rt[1][:], op=ALU.subtract),
             r=[("c_rt", 0), ("c_rt", 1)], w=[("c_qkr", b)])
        S.op("pool", lambda e, b=b: e.tensor_tensor(out=qkr[b][:, :, 32:64], in0=rt[2][:], in1=rt[3][:], op=ALU.add),
             r=[("c_rt", 2), ("c_rt", 3)], w=[("c_qkr", b)])

    def s1c(i):
        b = i % 2
        blocks = [(qkb[b][:, c, :], ("c_qkb", b)) for c in range(8)]
        transpose_to(k, blocks, QKn[:, :, i * 128:(i + 1) * 128], ("c_QKn", i))
        blocks = [(qkr[b][:, 2 * c:2 * c + 2, :].rearrange("p h c -> p (h c)"), ("c_qkr", b)) for c in range(4)]
        transpose_to(k, blocks, QKr[:, :, i * 128:(i + 1) * 128], ("c_QKr", i))
    expT = [A.alloc("c_expT", [128, 4, 128], BF16) for _ in range(2)]
    rden = A.alloc("c_rden", [128, 4], F32)
    yv = [A.alloc("c_yv", [128, 512], F32) for _ in range(2)]
    gb = gate_bufs(A)
    psO = [k.psf[4], k.psf[5]]
    pko = [("psf", 4), ("psf", 5)]
    scale = float(192 ** -0.5)
    pairs = [dict(pre=(0, "a")), dict(pre=(0, "b")), dict(pre=(0, "c"))]
    for i in range(NT):
        P = [dict(i=i, j=j, first=(j == 0), last=(j == i)) for j in range(i + 1)]
        n = len(P)
        if i + 1 < NT:
            P = [dict(pre=(i + 1, "a"))] + P[0:n // 3] + [dict(pre=(i + 1, "b"))] + P[n // 3:2 * n // 3] + [dict(pre=(i + 1, "c"))] + P[2 * n // 3:]
        pairs += P
    s1 = {"a": s1a, "b": s1b, "c": s1c}

    def stage1(p, b2):
        if "pre" in p:
            s1[p["pre"][1]](p["pre"][0])
            return
        i, j = p["i"], p["j"]
        pss = [ps_next(k), ps_next(k)]
        for h in (0, 2, 1, 3):
            hp = h % 2
            ps, pk = pss[hp]
            c_ = (h // 2) * 128
            S.op("pe", lambda e, h=h, ps=ps, c_=c_: e.matmul(ps[:, c_:c_ + 128], QKn[:, 4 + h, j * 128:(j + 1) * 128], QKn[:, h, i * 128:(i + 1) * 128],
                                                           start=True, stop=False),
                 r=[("c_QKn", i), ("c_QKn", j)], w=[pk])
            S.op("pe", lambda e, h=h, hp=hp, ps=ps, c_=c_: e.matmul(ps[:, c_:c_ + 128], QKr[hp * 64:(hp + 1) * 64, 2 + h // 2, j * 128:(j + 1) * 128],
                                                                  QKr[hp * 64:(hp + 1) * 64, h // 2, i * 128:(i + 1) * 128], start=False, stop=True),
                 r=[("c_QKr", i), ("c_QKr", j)], w=[pk])
        for hp in range(2):
            ps, pk = pss[hp]
            S.op("act", lambda e, ps=ps, hp=hp: e.activation(out=expT[b2][:, 2 * hp:2 * hp + 2, :].rearrange("p h t -> p (h t)"), in_=ps[:, 0:256], func=AF.Exp, scale=scale),
                 r=[pk], w=[("c_expT", b2)])
        if j == i:
            S.op("pool", lambda e: e.memset(expT[b2][64:128, :, 0:64], 0.0), r=[("c_expT", b2)], w=[("c_expT", b2)])

    def stage2(p, b2):
        if "pre" in p:
            return
        i, j = p["i"], p["j"]
        for h in range(4):
            po = psO[h // 2]
            S.op("pe", lambda e, h=h, po=po: e.matmul(po[:, (h % 2) * 256:(h % 2) * 256 + 129], expT[b2][:, (h % 2) * 2 + h // 2, :], vaug[:, j, h, 0:129],
                                                    start=(j == 0 and h % 2 == 0), stop=(j == i and h % 2 == 1), skip_group_check=True),
                 r=[("c_expT", b2), ("c_v", j)], w=[pko[h // 2]])
        if p["last"]:
            b = i % 2
            finalize_heads(k, "c", psO, pko, rden, yv[b], ("c_yv", b), 2, 128, 256)
            gate_and_store(k, l, yv[b][:], ("c_yv", b), c0 + 576, i, 2, yT, gb, b)

    run_pipelined(pairs, stage1, stage2)
    A.release(m)


def b_pre_gen(k, l):
    S, A, d = k.S, k.A, k.d
    A.mode = "top"
    m = A.mark()
    c0 = int(OFF[3])
    IQT = A.alloc("b_IQT", [128, 5, SEQ], BF16)
    absw = A.alloc("b_absw", [128, NT, 8], F32)
    sgnw = A.alloc("b_sgnw", [128, NT, 8], F32)
    hin = [A.alloc("b_hiq", [128, 584], BF16) for _ in range(2)]
    ikk = [A.alloc("b_ikk", [128, 128], BF16) for _ in range(2)]
    NBIS = 18
    score2 = [A.alloc("b_score", [128, SEQ], F32) for _ in range(2)]
    bs2 = [A.alloc("b_bs", [128, 8], F32) for _ in range(2)]
    W2 = [A.alloc("b_W", [128, NBIS + 1], F32) for _ in range(2)]
    pow2 = A.alloc("b_pow2", [128, NBIS + 1], F32)
    mb = [A.alloc("b_mb", [128, SEQ], BF16) for _ in range(2)]
    tmpr = [A.alloc("b_tmpr", [128, 512], F32) for _ in range(4)]
    tr = [0]
    A.mode = "bottom"
    for kk in range(NBIS + 1):
        S.op("pool", lambda e, kk=kk: e.memset(pow2[:, kk:kk + 1], float(2.0 ** -(kk + 1))), w=["b_pow2"])

    def s1iq(i):
        b = i % 2
        load_h(k, "sp", hin[b][:], i, c0 + 640, 584, ("b_hiq", b))
        for hh in range(2):
            S.op("pool", lambda e, hh=hh: e.tensor_copy(out=ikk[b][:, hh * 64:(hh + 1) * 64], in_=hin[b][:, 512:576]), r=[("b_hiq", b)], w=[("b_ikk", b)])
        S.op("act", lambda e: e.activation(out=absw[:, i, :], in_=hin[b][:, 576:584], func=AF.Abs), r=[("b_hiq", b)], w=[("b_absw", i)])
        S.op("act", lambda e: e.activation(out=sgnw[:, i, :], in_=hin[b][:, 576:584], func=AF.Sign), r=[("b_hiq", b)], w=[("b_sgnw", i)])
        blocks = [(hin[b][:, c * 128:(c + 1) * 128], ("b_hiq", b)) for c in range(4)] + [(ikk[b][:, :], ("b_ikk", b))]
        transpose_to(k, blocks, IQT[:, :, i * 128:(i + 1) * 128], ("b_IQT", i))

    def idx_part(i):
        p2 = i % 2
        score, bs, W = score2[p2], bs2[p2], W2[p2]
        n_i = 128 * (i + 1)
        nch = (n_i + 511) // 512
        for c_ in range(nch):
            cw = min(512, n_i - 512 * c_)
            for h in range(8):
                hp = h % 2
                ps, pk = ps_next(k, k.b_pre_pool)
                S.op("pe", lambda e, ps=ps, h=h, hp=hp, c_=c_, cw=cw: e.matmul(ps[:, 0:cw], IQT[hp * 64:(hp + 1) * 64, h // 2, i * 128:(i + 1) * 128],
                                                                           IQT[hp * 64:(hp + 1) * 64, 4, c_ * 512:c_ * 512 + cw], start=True, stop=True),
                     r=[("b_IQT", i)] + [("b_IQT", jj) for jj in range(4 * c_, min(4 * c_ + 4, i + 1))], w=[pk])
                b3 = tr[0] % 4
                tr[0] += 1
                S.op("act", lambda e, ps=ps, b3=b3, cw=cw, h=h: e.activation(out=tmpr[b3][:, 0:cw], in_=ps[:, 0:cw], func=AF.Relu, scale=absw[:, i, h:h + 1]),
                     r=[pk, ("b_absw", i)], w=[("b_tmpr", b3)])
                sc = score[:, c_ * 512:c_ * 512 + cw]
                if h == 0:
                    S.op("dve", lambda e, sc=sc, b3=b3, cw=cw: e.tensor_scalar(out=sc, in0=tmpr[b3][:, 0:cw], scalar1=sgnw[:, i, 0:1], scalar2=None, op0=ALU.mult),
                         r=[("b_tmpr", b3), ("b_sgnw", i)], w=[("b_score", p2, c_)])
                else:
                    S.op("dve", lambda e, sc=sc, b3=b3, cw=cw, h=h: e.scalar_tensor_tensor(out=sc, in0=tmpr[b3][:, 0:cw], scalar=sgnw[:, i, h:h + 1], in1=sc,
                                                                                       op0=ALU.mult, op1=ALU.add),
                         r=[("b_tmpr", b3), ("b_sgnw", i), ("b_score", p2, c_)], w=[("b_score", p2, c_)])
                yield
        allsc = [("b_score", p2, c_) for c_ in range(nch)]
        bk = ("b_bs", p2)
        lo, hi, w0, mid = (bs[:, c_:c_ + 1] for c_ in range(4))
        if i >= 2:
            S.op("dve", lambda e: e.tensor_reduce(out=lo, in_=score[:, 0:n_i], axis=AX.X, op=ALU.min), r=allsc, w=[bk])
        S.op("dve", lambda e: e.memset(score[0:64, n_i - 64:n_i], -1e30), r=allsc, w=allsc)
        if i >= 2:
            S.op("dve", lambda e: e.tensor_reduce(out=hi, in_=score[:, 0:n_i], axis=AX.X, op=ALU.max), r=allsc, w=[bk])
            S.op("dve", lambda e: e.tensor_tensor(out=w0, in0=hi, in1=lo, op=ALU.subtract), r=[bk], w=[bk])
            S.op("dve", lambda e: e.tensor_scalar(out=W[:, :], in0=pow2[:, :], scalar1=w0, scalar2=None, op0=ALU.mult), r=[bk, "b_pow2"], w=[("b_W", p2)])
            S.op("dve", lambda e: e.tensor_tensor(out=mid, in0=lo, in1=W[:, 0:1], op=ALU.add), r=[bk, ("b_W", p2)], w=[bk])
        else:
            S.op("dve", lambda e: e.memset(lo, -1e29), w=[bk])

    def bis_step(i, kk):
        if i < 2:
            return
        p2 = i % 2
        score, bs, W = score2[p2], bs2[p2], W2[p2]
        n_i = 128 * (i + 1)
        allsc = [("b_score", p2, c_) for c_ in range((n_i + 511) // 512)]
        bk = ("b_bs", p2)
        mid, cnt, gw = bs[:, 3:4], bs[:, 4:5], bs[:, 5:6]
        S.op("dve", lambda e: e.tensor_scalar(out=mb[p2][:, 0:n_i], in0=score[:, 0:n_i], scalar1=mid, scalar2=0.0, op0=ALU.is_ge, op1=ALU.add, accum_out=cnt),
             r=allsc + [bk], w=[bk, ("b_mb", p2)])
        S.op("dve", lambda e: e.tensor_scalar(out=gw, in0=cnt, scalar1=256.0, scalar2=-0.5, op0=ALU.is_ge, op1=ALU.add), r=[bk], w=[bk])
        S.op("dve", lambda e: e.scalar_tensor_tensor(out=mid, in0=gw, scalar=W[:, kk:kk + 1], in1=mid, op0=ALU.mult, op1=ALU.add),
             r=[bk, ("b_W", p2)], w=[bk])

    def mask_part(i):
        p2 = i % 2
        score, bs, W = score2[p2], bs2[p2], W2[p2]
        n_i = 128 * (i + 1)
        allsc = [("b_score", p2, c_) for c_ in range((n_i + 511) // 512)]
        bk = ("b_bs", p2)
        lo, mid = bs[:, 0:1], bs[:, 3:4]
        if i >= 2:
            S.op("dve", lambda e: e.tensor_tensor(out=lo, in0=mid, in1=W[:, NBIS:NBIS + 1], op=ALU.subtract), r=[bk, ("b_W", p2)], w=[bk])
        mbb = mb[p2]
        S.op("dve", lambda e: e.tensor_scalar(out=mbb[:, 0:n_i], in0=score[:, 0:n_i], scalar1=lo, scalar2=NEGM, op0=ALU.is_lt, op1=ALU.mult),
             r=allsc + [bk], w=[("b_mb", p2)])
        S.op("pool", lambda e: e.dma_start(out=k.mask_d[i * 128:(i + 1) * 128, 0:n_i], in_=mbb[:, 0:n_i]), r=[("b_mb", p2)], w=[("mask_d", i)], dma=True)

    for t in range(min(4, NT)):
        s1iq(t)
        yield
    for _ in idx_part(0):
        yield
    for t in range(NT):
        if t + 4 < NT:
            s1iq(t + 4)
            yield
        gi = idx_part(t + 1) if t + 1 < NT else iter(())
        steps = list(range(NBIS)) if t >= 2 else []
        n_idx = 8 * ((128 * (t + 2) + 511) // 512) if t + 1 < NT else 0
        per = max(1, -(-n_idx // max(1, len(steps)))) if steps else n_idx
        for kk in steps:
            bis_step(t, kk)
            yield
            for _ in range(per):
                if next(gi, "end") != "end":
                    yield
        for _ in gi:
            yield
        mask_part(t)
        yield
    A.mode = "top"
    A.release(m)
    A.mode = "bottom"


def mixer_b(k, l, yT):
    S, A, d = k.S, k.A, k.d
    m = A.mark()
    c0 = int(OFF[3])
    QT = A.alloc("b_QT", [128, 5, SEQ], BF16)
    vaug = A.alloc("b_vaug", [128, NT, 80], BF16)
    bias3 = A.alloc("b_bias3", [128, 3, 8, 128], F32)
    gains = A.alloc("b_gains", [128, 2, 64], F32)
    S.op("sp", lambda e: e.dma_start(out=bias3[:], in_=d["b_bias3"]), w=["b_bias3"], dma=True)
    S.op("sp", lambda e: e.dma_start(out=gains[:, 0, :], in_=d["b_q_gain"][l].partition_broadcast(128)), w=["gains"], dma=True)
    S.op("sp", lambda e: e.dma_start(out=gains[:, 1, :], in_=d["b_k_gain"][l].partition_broadcast(128)), w=["gains"], dma=True)
    S.op("dve", lambda e: e.tensor_scalar(out=gains[:, 0, :], in0=gains[:, 0, :], scalar1=0.125, scalar2=None, op0=ALU.mult), r=["gains"], w=["gains"])
    for c in range(2):
        S.op("dve", lambda e, c=c: e.tensor_tensor(out=bias3[:, c, :, :], in0=bias3[:, c, :, :], in1=bias3[:, 2, :, :], op=ALU.subtract),
             r=["b_bias3"], w=["b_bias3"])
    S.op("pool", lambda e: e.memset(vaug[:, :, 64:65], 1.0), w=[("b_v", j) for j in range(NT)])
    gqf = A.alloc("b_gqf", [128, 512], F32)
    rep_gain(k, gqf[:], gains[:, 0, :], 8, 64)
    hin = [A.alloc("b_hin", [128, 640], BF16) for _ in range(2)]
    sq = A.alloc("b_sq", [128, 576], F32)
    tmp = A.alloc("b_tmp", [128, 576], F32)
    qn = [A.alloc("b_qn", [128, 640], BF16) for _ in range(2)]
    ss = A.alloc("b_ss", [128, NT, 16], F32)
    rs = A.alloc("b_rs", [128, NT, 16], F32)

    def s1a(i):
        b = i % 2
        load_h(k, "sp", hin[b][:], i, c0, 640, ("b_hin", b))
        head_norm(k, hin[b][:, 0:512], 8, 64, gqf[:], qn[b][:, 0:512], ss[:, i, 0:8], rs[:, i, 0:8], tmp[:, 0:512],
                  [("b_hin", b)], ("b_ss", i, 0), ("b_rs", i, 0), ("b_tmp", 0), ("b_qn", b, 0), sq[:, 0:512])
        head_norm(k, hin[b][:, 512:576], 1, 64, gains[:, 1, :], qn[b][:, 512:576], ss[:, i, 8:9], rs[:, i, 8:9], tmp[:, 512:576],
                  [("b_hin", b)], ("b_ss", i, 1), ("b_rs", i, 1), ("b_tmp", 1), ("b_qn", b, 1), sq[:, 512:576], gkey="gains")
        S.op("pool", lambda e, b=b: e.tensor_copy(out=qn[b][:, 576:640], in_=qn[b][:, 512:576]), r=[("b_qn", b, 1)], w=[("b_qn", b, 2)])
        S.op("pool", lambda e, b=b, i=i: e.tensor_copy(out=vaug[:, i, 0:64], in_=hin[b][:, 576:640]), r=[("b_hin", b)], w=[("b_v", i)])

    def s1b(i):
        b = i % 2
        blocks = [(qn[b][:, c * 128:(c + 1) * 128], ("b_qn", b, 0)) for c in range(4)] + [(qn[b][:, 512:640], ("b_qn", b, 2))]
        transpose_to(k, blocks, QT[:, :, i * 128:(i + 1) * 128], ("b_QT", i), extra_r=[("b_qn", b, 1)])

    mbl = [A.alloc("b_mbl", [128, SEQ], BF16) for _ in range(2)]
    maskT = [A.alloc("b_maskT", [128, NT, 128], BF16) for _ in range(2)]
    ein = [A.alloc("b_ein", [128, 1024], F32) for _ in range(1)]
    expT = [A.alloc("b_expT", [128, 8, 128], BF16) for _ in range(2)]
    rden = A.alloc("b_rden", [128, 8], F32)
    yv = [A.alloc("b_yv", [128, 512], F32) for _ in range(2)]
    gb = gate_bufs(A)
    psO = [k.psf[4], k.psf[5]]
    pko = [("psf", 4), ("psf", 5)]

    def mload(i):
        p2 = i % 2
        n_i = 128 * (i + 1)
        S.op("sp", lambda e: e.dma_start(out=mbl[p2][:, 0:n_i], in_=k.mask_d[i * 128:(i + 1) * 128, 0:n_i]), r=[("mask_d", i)], w=[("b_mbl", p2)], dma=True)
        for g0 in range(0, i + 1, 8):
            nb_ = min(8, i + 1 - g0)
            blocks = [(mbl[p2][:, (g0 + bi) * 128:(g0 + bi + 1) * 128], ("b_mbl", p2)) for bi in range(nb_)]
            transpose_to(k, blocks, maskT[p2][:, g0:g0 + nb_, :], ("b_maskT", p2, g0 // 8))

    pairs = []
    for t in range(min(2, NT)):
        pairs += [dict(pre=("a", t)), dict(pre=("b", t))]
    pairs += [dict(pre=("mask", 0))]
    for i in range(NT):
        near = [dict(i=i, j=j) for j in (i, i - 1) if j >= 0]
        far = [dict(i=i, j=j) for j in range(0, i - 1)]
        real = near + far
        for p in real:
            p["first"] = p is real[0]
            p["last"] = p is real[-1]
        tile = list(near)
        if i + 1 < NT:
            tile.append(dict(pre=("mask", i + 1)))
        if i + 2 < NT:
            tile.append(dict(pre=("a", i + 2)))
        tile += far
        if i + 2 < NT:
            tile.append(dict(pre=("b", i + 2)))
        pairs += tile
    s1 = {"a": s1a, "b": s1b, "mask": mload}

    def stage1(p, b2):
        if "pre" in p:
            s1[p["pre"][0]](p["pre"][1])
            return
        i, j = p["i"], p["j"]
        dl = min(i - j, 2)
        mT = maskT[i % 2]
        pss = [ps_next(k), ps_next(k)]
        for h in range(8):
            ps, pk = pss[h % 2]
            hp = h % 2
            S.op("pe", lambda e, h=h, hp=hp, ps=ps: e.matmul(ps[:, (h // 2) * 128:(h // 2 + 1) * 128],
                                                           QT[hp * 64:(hp + 1) * 64, 4, j * 128:(j + 1) * 128],
                                                           QT[hp * 64:(hp + 1) * 64, h // 2, i * 128:(i + 1) * 128], start=(h < 2), stop=False,
                                                           skip_group_check=True),
                 r=[("b_QT", i), ("b_QT", j)], w=[pk])
        for hg in range(2):
            ps, pk = pss[hg]
            S.op("pe", lambda e, ps=ps: e.matmul(ps[:, :], k.ident[:], mT[:, j, :].unsqueeze(1).to_broadcast([128, 4, 128]), start=False, stop=True, skip_group_check=True),
                 r=[("b_maskT", i % 2, j // 8), "ident"], w=[pk])
        for hg in range(2):
            ps, pk = pss[hg]
            if dl < 2:
                S.op("dve", lambda e, hg=hg, ps=ps: e.tensor_tensor(
                    out=ein[0][:, hg * 512:(hg + 1) * 512], in0=ps[:, :],
                    in1=bias3[:, dl, hg * 4:(hg + 1) * 4, :].rearrange("p h t -> p (h t)"), op=ALU.add),
                    r=[pk, "b_bias3"], w=[("b_ein", 0, hg)])
                S.op("act", lambda e, hg=hg: e.activation(out=expT[b2][:, hg * 4:(hg + 1) * 4, :].rearrange("p h t -> p (h t)"),
                                                        in_=ein[0][:, hg * 512:(hg + 1) * 512], func=AF.Exp),
                     r=[("b_ein", 0, hg)], w=[("b_expT", b2, hg)])
            else:
                S.op("act", lambda e, hg=hg, ps=ps: e.activation(out=expT[b2][:, hg * 4:(hg + 1) * 4, :].rearrange("p h t -> p (h t)"),
                                                               in_=ps[:, :], func=AF.Exp),
                     r=[pk], w=[("b_expT", b2, hg)])

    def stage2(p, b2):
        if "pre" in p:
            return
        i, j = p["i"], p["j"]
        for h in range(8):
            po = psO[h // 4]
            S.op("pe", lambda e, h=h, po=po: e.matmul(po[:, (h % 4) * 128:(h % 4) * 128 + 65], expT[b2][:, (h % 2) * 4 + h // 2, :], vaug[:, j, 0:65],
                                                    start=(p["first"] and h % 4 == 0), stop=(p["last"] and h % 4 == 3), skip_group_check=True),
                 r=[("b_expT", b2, h % 2), ("b_v", j)], w=[pko[h // 4]])
        if p["last"]:
            b = i % 2
            finalize_heads(k, "b", psO, pko, rden, yv[b], ("b_yv", b), 4, 64, 128)
            gate_and_store(k, l, yv[b][:], ("b_yv", b), c0 + 1224, i, 1, yT, gb, b)

    run_pipelined(pairs, stage1, stage2)
    A.release(m)


def build_program(mode="full", nlayers=DEPTH, mixers="ABCD"):
    nc = bass.Bass("TRN2", target_bir_lowering=False)
    k = K()
    k.nc = nc
    k.S = Sched(nc)
    k.A = Arena(nc)
    k.ps_i = k.pst_i = k.ev_i = k.ps_lo = k.ps_hi = 0
    k.out_dmas = []
    d = {}

    def inp(name, shape, dt=F32):
        d[name] = nc.dram_tensor(name, list(shape), dt, kind="ExternalInput").ap()

    inp("x", [SEQ, D_MODEL])
    inp("w_in", [DEPTH, D_MODEL, IN_COLS])
    inp("w_out", [DEPTH, D_MODEL, D_MODEL])
    inp("norm_g", [DEPTH, D_MODEL])
    inp("ident", [128, 128])
    inp("a_wsT", [DEPTH, 128, 4, 128])
    inp("a_bsT", [DEPTH, 128, 4])
    inp("a_v_gain", [DEPTH, 512])
    inp("b_bias3", [128, 3, 8, 128])
    inp("b_q_gain", [DEPTH, 64])
    inp("b_k_gain", [DEPTH, 64])
    inp("c_w_qb", [DEPTH, 384, 768])
    inp("c_w_kvb", [DEPTH, 128, 1024])
    inp("c_qa_gain", [DEPTH, 384])
    inp("c_kva_gain", [DEPTH, 128])
    inp("c_q_gain", [DEPTH, 192])
    inp("c_k_gain", [DEPTH, 192])
    inp("rope_cs", [128, NT, 64])
    inp("d_bias5", [DEPTH, 128, 5, 8, 128])
    inp("d_q_gain", [DEPTH, 64])
    inp("d_k_gain", [DEPTH, 64])
    k.d = d
    dbg = mode != "full"
    mixonly = mode == "mixonly"
    out_d = nc.dram_tensor("out", [SEQ, D_MODEL], F32, kind="ExternalOutput").ap()
    k.h_d = nc.dram_tensor("h_scr", [SEQ, IN_COLS], BF16, kind="ExternalInput" if mixonly else ("ExternalOutput" if dbg else "Internal")).ap()
    xs_d = nc.dram_tensor("x_scr", [SEQ, D_MODEL], F32).ap()
    k.mask_d = nc.dram_tensor("mask_scr", [SEQ, SEQ], BF16).ap()
    if dbg:
        ydbg = nc.dram_tensor("ydbg", [D_MODEL, SEQ], BF16, kind="ExternalOutput").ap()

    with ExitStack() as es:
        S, A = k.S, k.A
        k.psf = [es.enter_context(nc.psum_tensor("psf%d" % i, [128, 512], F32)) for i in range(6)]
        k.pstt = [es.enter_context(nc.psum_tensor("pst%d" % i, [128, 1024], BF16)) for i in range(2)]
        idf = A.alloc("idf", [128, 128], F32)
        k.ident = A.alloc("ident", [128, 128], BF16)
        S.op("sp", lambda e: e.dma_start(out=idf[:], in_=d["ident"]), w=["idf"], dma=True)
        S.op("dve", lambda e: e.tensor_copy(out=k.ident[:], in_=idf[:]), r=["idf"], w=["ident"])
        k.eps_t = A.alloc("eps_t", [128, 1], F32)
        S.op("dve", lambda e: e.memset(k.eps_t[:], EPS), w=["eps_t"])

        for l in range(nlayers):
            x_src = d["x"] if l == 0 else xs_d
            x_dst = out_d if l == nlayers - 1 else xs_d
            m0 = A.mark()
            use_b = "B" in mixers
            k.b_pre_pool = "all" if mixonly else "hi"
            if not mixonly:
                xnT = A.alloc("xnT", [128, KC, SEQ], BF16)
                pb = proj_bufs(A)
                hsb = [A.alloc("hsb", [128, 512], BF16) for _ in range(4)]
                phase_norm(k, l, x_src, xnT)
                norm_last = [S.ops[e][-1] for e in S.ENGS if S.ops[e] and S.ops[e][-1].fn is not None]
                gen = [None]
                ncb = (IN_COLS + 511) // 512
                order = [4, 5] + [cb for cb in range(ncb) if cb not in (4, 5)]

                n_y = 4 + 8 + sum((1 if t + 4 < NT else 0) + (18 if t >= 2 else 0) + (8 * ((128 * (t + 2) + 511) // 512) if t + 1 < NT else 0) + 1 for t in range(NT))
                quota = -(-n_y // ((ncb - 2) * NT - 8))

                def hook(pos, i):
                    if not use_b or pos < 1 or (pos == 1 and i < NT - 1):
                        return
                    if gen[0] is None:
                        for e in S.ENGS:
                            S.fence(e, norm_last)
                        gen[0] = b_pre_gen(k, l)
                    for _ in range(quota):
                        if next(gen[0], "end") == "end":
                            break

                phase_inproj(k, l, xnT, pb, hsb, order, hook, keep_dve_free=use_b)
                S.barrier()
            elif use_b:
                gen = [b_pre_gen(k, l)]
            A.release(m0)
            yT = A.alloc("yT", [128, KC, SEQ], BF16)
            if mixonly:
                S.op("pool", lambda e: e.memset(yT[:], 0.0), w=[("yT", i) for i in range(NT)])
            done_a = False
            if use_b:
                if "A" in mixers:
                    ga = mixer_a_gen(k, l, yT, ps_pool="lo" if not mixonly else "all")
                    for _ in ga:
                        for _ in range(8):
                            if next(gen[0], "end") == "end":
                                break
                    done_a = True
                for _ in gen[0]:
                    pass
                S.barrier()
            for g, (nm, fn) in enumerate((("A", mixer_a), ("B", mixer_b), ("C", mixer_c), ("D", mixer_d))):
                if nm == "A" and done_a:
                    continue
                if nm in mixers and fn is not None:
                    fn(k, l, yT)
                elif mixonly:
                    continue
                else:
                    mixer_stub(k, l, yT, g)
                S.barrier()
            if dbg and l == nlayers - 1:
                S.op("pool", lambda e: e.dma_start(out=ydbg.rearrange("(fc p) t -> p fc t", p=128), in_=yT[:]),
                     r=[("yT", i) for i in range(NT)], w=["ydbg"], dma=True)
            if not mixonly:
                phase_outproj(k, l, yT, x_src, x_dst)
            S.barrier()
            A.release(m0)
        if not mixonly:
            S.fence("pool", k.out_dmas[-64:] + S.dmas_since_barrier)
        S.emit(es)
    return nc


def host_consts(inputs):
    f = np.float32
    hc = {}
    s_ = np.arange(128)[:, None, None]
    dl = np.arange(5)[None, :, None]
    t_ = np.arange(128)[None, None, :]
    dist = 128 * dl + t_ - s_
    dq = 2 * dl + t_ // 64 - s_ // 64
    valid = (dq >= 0) & (dq <= 8)
    idx = np.clip(dist, -128, 128) + 128
    rb = np.asarray(inputs["d_rel_bias"], dtype=f)
    tab = rb[:, idx]
    tab = np.where(valid[None, ..., None], tab, f(NEGM)).transpose(0, 1, 2, 4, 3)
    tab = tab[:, :, :, [0, 2, 4, 6, 1, 3, 5, 7], :]
    hc["d_bias5"] = np.ascontiguousarray(tab, dtype=f)
    def t5_bucket_np(rel):
        nb, max_exact = 16, 8
        ret = np.where(rel > 0, nb, 0)
        n = np.abs(rel)
        nf = np.maximum(n, 1).astype(np.float32)
        large = max_exact + (np.log(nf / np.float32(max_exact)) / np.float32(np.log(128 / max_exact)) * np.float32(nb - max_exact)).astype(np.int32)
        large = np.minimum(large, nb - 1)
        return ret + np.where(n < max_exact, n, large)
    s2 = np.arange(128)[:, None, None]
    cl = np.arange(3)[None, :, None]
    t2 = np.arange(128)[None, None, :]
    rel = s2 - t2 - 128 * cl - np.where(cl == 2, 4096, 0)
    bk = t5_bucket_np(rel.astype(np.int64))
    t5 = np.asarray(inputs["t5_bias"], dtype=f)[bk]
    t5 = t5.transpose(0, 1, 3, 2)[:, :, [0, 2, 4, 6, 1, 3, 5, 7], :]
    hc["b_bias3"] = np.ascontiguousarray(t5, dtype=f)
    inv = (10000.0 ** (-np.arange(0, 64, 2, dtype=np.float32) / np.float32(64))).astype(f)
    ang = np.arange(SEQ, dtype=f)[:, None] * inv[None, :]
    cs = np.concatenate([np.cos(ang), np.sin(ang)], axis=1).astype(f)
    hc["rope_cs"] = np.ascontiguousarray(cs.reshape(NT, 128, 64).transpose(1, 0, 2))
    return hc


def host_inputs(inputs, b, hc=None):
    f = np.float32
    if hc is None:
        hc = host_consts(inputs)
    hi = {
        "x": np.ascontiguousarray(inputs["x"][b], dtype=f),
        "w_in": np.ascontiguousarray(inputs["w_in"], dtype=f),
        "w_out": np.ascontiguousarray(inputs["w_out"], dtype=f),
        "norm_g": np.ascontiguousarray(inputs["norm_g"], dtype=f),
        "ident": np.eye(128, dtype=f),
        "a_wsT": np.ascontiguousarray(np.asarray(inputs["a_ws"], dtype=f).transpose(0, 3, 1, 2)),
        "a_bsT": np.ascontiguousarray(np.asarray(inputs["a_bs"], dtype=f).transpose(0, 2, 1)),
        "a_v_gain": np.ascontiguousarray(inputs["a_v_gain"], dtype=f),
        "d_q_gain": np.ascontiguousarray(inputs["d_q_gain"], dtype=f),
        "b_q_gain": np.ascontiguousarray(inputs["b_q_gain"], dtype=f),
        "b_k_gain": np.ascontiguousarray(inputs["b_k_gain"], dtype=f),
        "c_w_qb": np.ascontiguousarray(inputs["c_w_qb"], dtype=f),
        "c_w_kvb": np.ascontiguousarray(inputs["c_w_kvb"], dtype=f),
        "c_qa_gain": np.ascontiguousarray(inputs["c_qa_gain"], dtype=f),
        "c_kva_gain": np.ascontiguousarray(inputs["c_kva_gain"], dtype=f),
        "c_q_gain": np.ascontiguousarray(inputs["c_q_gain"], dtype=f),
        "c_k_gain": np.ascontiguousarray(inputs["c_k_gain"], dtype=f),
        "d_k_gain": np.ascontiguousarray(inputs["d_k_gain"], dtype=f),
    }
    hi.update(hc)
    return hi


def kernel(**inputs):
    nc = build_program("full")
    n = 8
    hc = host_consts(inputs)
    in_maps = [host_inputs(inputs, b, hc) for b in range(n)]
    res = run_bass_kernel_spmd(nc, in_maps, core_ids=list(range(n)))
    return np.stack([np.asarray(r["out"]) for r in res.results], axis=0).astype(np.float32)
```

```python
import numpy as np
from contextlib import ExitStack

import concourse.bass as bass
import concourse.mybir as mybir
from concourse.bass_utils import run_bass_kernel_spmd

F32 = mybir.dt.float32
BF16 = mybir.dt.bfloat16
ALU = mybir.AluOpType
AF = mybir.ActivationFunctionType
AX = mybir.AxisListType

D_MODEL = 2048
SEQ = 2048
DEPTH = 2
NT = SEQ // 128
KC = D_MODEL // 128
IN_SIZES = (512, 512, 512, 512, 64, 64, 512, 64, 8, 512, 384, 128, 64, 512, 512, 512, 512, 512)
IN_COLS = sum(IN_SIZES)
OFF = np.concatenate([[0], np.cumsum(IN_SIZES)]).astype(int)
EPS = 1e-6
NEGM = -30000.0


class Op:
    __slots__ = ("eng", "fn", "dma", "deps", "ev", "prev", "signal")

    def __init__(self, eng, fn, dma):
        self.eng, self.fn, self.dma = eng, fn, dma
        self.deps = set()
        self.ev = None
        self.prev = None
        self.signal = False


class Sched:
    ENGS = ("sp", "act", "dve", "pool", "pe")
    NDS = 8

    def __init__(self, nc):
        self.nc = nc
        self.ops = {e: [] for e in self.ENGS}
        self.lastw = {}
        self.readers = {}
        self.dmas_since_barrier = []

    def op(self, eng, fn, r=(), w=(), dma=False):
        o = Op(eng, fn, dma)
        deps = {}

        def add(d, raw):
            if d is None or d is o:
                return
            deps[d] = deps.get(d, False) or raw

        for k in r:
            add(self.lastw.get(k), True)
        for k in w:
            add(self.lastw.get(k), False)
            for rd in self.readers.get(k, ()):
                add(rd, False)
        for d, raw in deps.items():
            if d.eng == eng and not d.dma and not dma:
                if eng == "pe":
                    continue
            o.deps.add(d)
            d.signal = True
        for k in r:
            self.readers.setdefault(k, []).append(o)
        for k in w:
            self.lastw[k] = o
            self.readers[k] = []
        self.ops[eng].append(o)
        if dma:
            self.dmas_since_barrier.append(o)
        return o

    def fence(self, eng, deps):
        o = Op(eng, None, False)
        for d in deps:
            if d is not None:
                o.deps.add(d)
                d.signal = True
        self.ops[eng].append(o)
        return o

    def barrier(self):
        last = [self.ops[e][-1] for e in self.ENGS if self.ops[e] and self.ops[e][-1].fn is not None]
        dm = list(self.dmas_since_barrier)
        self.dmas_since_barrier = []
        for e in self.ENGS:
            self.fence(e, [d for d in last + dm if not (d.eng == e and not d.dma and e == "pe")])

    def emit(self, es):
        nc = self.nc
        sem_eng = {e: es.enter_context(nc.semaphore("s_" + e)) for e in self.ENGS}
        sem_dma = {e: [es.enter_context(nc.semaphore("d_%s%d" % (e, k))) for k in range(self.NDS)]
                   for e in ("sp", "act", "pool")}
        for e in self.ENGS:
            cnt = 0
            dcnt = 0
            for o in self.ops[e]:
                if o.fn is None:
                    continue
                if o.dma:
                    k = dcnt % self.NDS
                    o.ev = (sem_dma[e][k], 16 * (dcnt // self.NDS + 1))
                    o.prev = (sem_dma[e][k], 16 * (dcnt // self.NDS))
                    dcnt += 1
                elif o.signal:
                    cnt += 1
                    o.ev = (sem_eng[e], cnt)
        block = es.enter_context(nc.Block())

        def run(e, eng):
            known = {}
            for o in self.ops[e]:
                waits = {}
                for d in o.deps:
                    s, v = d.ev
                    waits[s] = max(waits.get(s, 0), v)
                if o.dma and o.prev[1] > 0:
                    s, v = o.prev
                    waits[s] = max(waits.get(s, 0), v)
                for s, v in waits.items():
                    if known.get(s, 0) < v:
                        eng.wait_ge(s, v)
                        known[s] = v
                if o.fn is None:
                    continue
                inst = o.fn(eng)
                if o.dma:
                    inst.then_inc(o.ev[0], 16)
                elif o.signal:
                    inst.then_inc(o.ev[0], 1)

        block.sync(lambda eng: run("sp", eng))
        block.scalar(lambda eng: run("act", eng))
        block.vector(lambda eng: run("dve", eng))
        block.gpsimd(lambda eng: run("pool", eng))
        block.tensor(lambda eng: run("pe", eng))


class Arena:
    def __init__(self, nc, base=16640, top=229376 - 128):
        self.nc, self.off, self.top, self.n = nc, base, top, 0
        self.mode = "bottom"

    def alloc(self, name, shape, dtype):
        isz = 2 if dtype == BF16 else 4
        size = isz * int(np.prod(shape[1:]))
        size = (size + 63) // 64 * 64
        assert self.off + size <= self.top, ("SBUF overflow", name, self.off, self.top, size)
        self.n += 1
        if self.mode == "top":
            self.top -= size
            at = self.top
        else:
            at = self.off
            self.off += size
        return self.nc.alloc_sbuf_tensor_at("%s_%d" % (name, self.n), list(shape), dtype, offset=at)

    def mark(self):
        return self.top if self.mode == "top" else self.off

    def release(self, m):
        if self.mode == "top":
            self.top = m
        else:
            self.off = m


class K:
    pass


def ps_next(k, pool="all"):
    if pool == "all":
        i = k.ps_i % 4
        k.ps_i += 1
    elif pool == "lo":
        i = k.ps_lo % 2
        k.ps_lo += 1
    elif pool == "lo4":
        i = (0, 1, 4, 5)[k.ps_lo % 4]
        k.ps_lo += 1
    else:
        i = 2 + k.ps_hi % 2
        k.ps_hi += 1
    return k.psf[i], ("psf", i)


def pst_next(k):
    i = k.pst_i % 2
    k.pst_i += 1
    return k.pstt[i], ("pst", i)


def evac_eng(k):
    k.ev_i += 1
    return "act" if k.ev_i % 2 else "dve"


def copy_op(S, eng, out, in_, r, w):
    if eng == "act":
        return S.op("act", lambda e: e.activation(out=out, in_=in_, func=AF.Copy), r=r, w=w)
    return S.op(eng, lambda e: e.tensor_copy(out=out, in_=in_), r=r, w=w)


def transpose_to(k, blocks, dst, dst_key, eng=None, np_out=128, extra_r=()):
    S = k.S
    n = len(blocks)
    pt, pk = pst_next(k)
    for bi, (ap, key) in enumerate(blocks):
        S.op("pe", lambda e, ap=ap, bi=bi: e.transpose(out=pt[0:np_out, bi * 128:(bi + 1) * 128], in_=ap, identity=k.ident[:]),
             r=[key, "ident"] + list(extra_r), w=[pk])
    src = pt[0:np_out, 0:n * 128].rearrange("p (n t) -> p n t", n=n)
    copy_op(S, eng or evac_eng(k), dst, src, r=[pk], w=[dst_key])


def rstd_op(k, ss, n_feat, out, key_ss, key_out):
    S = k.S
    S.op("act", lambda e: e.activation(out=ss, in_=ss, func=AF.Ln, scale=1.0 / n_feat, bias=k.eps_t[:, 0:1]), r=[key_ss, "eps_t"], w=[key_ss])
    S.op("act", lambda e: e.activation(out=out, in_=ss, func=AF.Exp, scale=-0.5), r=[key_ss], w=[key_out])


def phase_norm(k, l, x_src, xnT):
    S, A = k.S, k.A
    A.mode = "top"
    m = A.mark()
    xt = [A.alloc("xt", [128, D_MODEL], F32) for _ in range(2)]
    xn = [A.alloc("xn", [128, D_MODEL], BF16) for _ in range(2)]
    junk = A.alloc("junk", [128, D_MODEL], BF16)
    ss = A.alloc("ss", [128, NT], F32)
    rs = A.alloc("rs", [128, NT], F32)
    gsb = A.alloc("gsb", [128, D_MODEL], F32)
    S.op("sp", lambda e: e.dma_start(out=gsb[:], in_=k.d["norm_g"][l].partition_broadcast(128)), w=["gsb"], dma=True)
    S.op("dve", lambda e: e.memset(ss[:], 0.0), w=[("ss", i) for i in range(NT)])
    for i in range(NT):
        b = i % 2
        S.op("sp", lambda e, i=i, b=b: e.dma_start(out=xt[b][:], in_=x_src[i * 128:(i + 1) * 128, :]), w=[("xt", b)], dma=True)
        S.op("act", lambda e, i=i, b=b: e.activation(out=junk[:], in_=xt[b][:], func=AF.Square, accum_out=ss[:, i:i + 1]),
             r=[("xt", b)], w=["junk", ("ss", i)])
        rstd_op(k, ss[:, i:i + 1], D_MODEL, rs[:, i:i + 1], ("ss", i), ("rs", i))
        S.op("dve", lambda e, i=i, b=b: e.scalar_tensor_tensor(out=xn[b][:], in0=xt[b][:], scalar=rs[:, i:i + 1], in1=gsb[:], op0=ALU.mult, op1=ALU.mult),
             r=[("xt", b), ("rs", i), "gsb"], w=[("xn", b)])
        for half in range(2):
            pt, pk = pst_next(k)
            for j in range(8):
                kc = half * 8 + j
                S.op("pe", lambda e, kc=kc, j=j, b=b, pt=pt: e.transpose(out=pt[:, j * 128:(j + 1) * 128], in_=xn[b][:, kc * 128:(kc + 1) * 128], identity=k.ident[:]),
                     r=[("xn", b), "ident"], w=[pk])
            src = pt[:, :].rearrange("p (n t) -> p n t", n=8)
            dst = xnT[:, half * 8:(half + 1) * 8, i * 128:(i + 1) * 128]
            copy_op(S, evac_eng(k), dst, src, r=[pk], w=[("xnT", i)])
    A.release(m)
    A.mode = "bottom"


def proj_bufs(A):
    return ([A.alloc("wst", [128, KC, 512], F32)], [A.alloc("wbf", [128, KC, 512], BF16) for _ in range(2)])


def phase_proj(k, lhsT, lhs_key, w_src, ncols, evac, bufs=None, order=None, hook=None, cast_engs=("dve", "act"), ps_pool="all"):
    S, A = k.S, k.A
    m = A.mark()
    wst, wbf = bufs if bufs is not None else proj_bufs(A)
    ncb = (ncols + 511) // 512
    order = list(order) if order is not None else list(range(ncb))
    wv = w_src.rearrange("(kc p) c -> p kc c", p=128)

    def load_w(pos):
        cb = order[pos]
        c0 = cb * 512
        cw = min(512, ncols - c0)
        b = pos % 2
        for q in range(4):
            S.op("sp", lambda e, q=q: e.dma_start(out=wst[0][:, q * 4:(q + 1) * 4, 0:cw], in_=wv[:, q * 4:(q + 1) * 4, c0:c0 + cw]),
                 w=[("wst", q)], dma=True)
            copy_op(S, cast_engs[q % 2], wbf[b][:, q * 4:(q + 1) * 4, 0:cw], wst[0][:, q * 4:(q + 1) * 4, 0:cw],
                    r=[("wst", q)], w=[("wbf", b, q)])

    load_w(0)
    for pos, cb in enumerate(order):
        c0 = cb * 512
        cw = min(512, ncols - c0)
        b = pos % 2
        if pos + 1 < len(order):
            load_w(pos + 1)
        for i in range(NT):
            ps, pk = ps_next(k, ps_pool)
            for kc in range(KC):
                S.op("pe", lambda e, kc=kc, i=i, b=b, cw=cw, ps=ps: e.matmul(ps[:, 0:cw], lhsT[:, kc, i * 128:(i + 1) * 128], wbf[b][:, kc, 0:cw],
                                                                          start=(kc == 0), stop=(kc == KC - 1)),
                     r=[lhs_key(i), ("wbf", b, kc // 4)], w=[pk])
            evac(i, c0, cw, ps, pk)
            if hook is not None:
                hook(pos, i)
    A.release(m)


def phase_inproj(k, l, xnT, bufs=None, hsb=None, order=None, hook=None, keep_dve_free=False):
    S, A = k.S, k.A
    m = A.mark()
    if hsb is None:
        hsb = [A.alloc("hsb", [128, 512], BF16) for _ in range(4)]
    cnt = [0]

    def evac(i, c0, cw, ps, pk):
        b = cnt[0] % 4
        cnt[0] += 1
        copy_op(S, "act" if keep_dve_free else evac_eng(k), hsb[b][:, 0:cw], ps[:, 0:cw], r=[pk], w=[("hsb", b)])
        S.op("pool", lambda e: e.dma_start(out=k.h_d[i * 128:(i + 1) * 128, c0:c0 + cw], in_=hsb[b][:, 0:cw]),
             r=[("hsb", b)], w=[("h_d", i, c0 // 512)], dma=True)

    phase_proj(k, xnT, lambda i: ("xnT", i), k.d["w_in"][l], IN_COLS, evac, bufs, order, hook,
               cast_engs=("act", "act") if keep_dve_free else ("dve", "act"), ps_pool="lo4" if keep_dve_free else "all")
    A.release(m)


def phase_outproj(k, l, yT, x_src, x_dst):
    S, A = k.S, k.A
    m = A.mark()
    xr = [A.alloc("xr", [128, 512], F32) for _ in range(3)]
    cnt = [0]

    def evac(i, c0, cw, ps, pk):
        b = cnt[0] % 3
        cnt[0] += 1
        S.op("sp", lambda e: e.dma_start(out=xr[b][:], in_=x_src[i * 128:(i + 1) * 128, c0:c0 + 512]), w=[("xr", b)], dma=True)
        S.op("dve", lambda e: e.tensor_tensor(out=xr[b][:], in0=ps[:, :], in1=xr[b][:], op=ALU.add), r=[pk, ("xr", b)], w=[("xr", b)])
        o = S.op("pool", lambda e: e.dma_start(out=x_dst[i * 128:(i + 1) * 128, c0:c0 + 512], in_=xr[b][:]),
                 r=[("xr", b)], w=[("xdst", i, c0 // 512)], dma=True)
        k.out_dmas.append(o)

    phase_proj(k, yT, lambda i: ("yT", i), k.d["w_out"][l], D_MODEL, evac)
    A.release(m)


def load_h(k, eng, dst, i, c0, cw, wkey):
    rk = [("h_d", i, cb) for cb in range(c0 // 512, (c0 + cw - 1) // 512 + 1)]
    return k.S.op(eng, lambda e: e.dma_start(out=dst, in_=k.h_d[i * 128:(i + 1) * 128, c0:c0 + cw]), r=rk, w=[wkey], dma=True)


def y_to_yT(k, ysb, ykey, yT, g, i):
    blocks = [(ysb[:, j * 128:(j + 1) * 128], ykey) for j in range(4)]
    transpose_to(k, blocks, yT[:, 4 * g:4 * g + 4, i * 128:(i + 1) * 128], ("yT", i))


def mixer_stub(k, l, yT, g):
    S, A = k.S, k.A
    m = A.mark()
    yb = [A.alloc("yb", [128, 512], BF16) for _ in range(2)]
    for i in range(NT):
        b = i % 2
        load_h(k, "sp", yb[b][:], i, 512 * g, 512, ("yb", b))
        y_to_yT(k, yb[b], ("yb", b), yT, g, i)
    A.release(m)


def mixer_a_gen(k, l, yT, ps_pool="all"):
    S, A, d = k.S, k.A, k.d
    m = A.mark()
    wsf = A.alloc("a_wsf", [128, 4, 128], F32)
    wsb = A.alloc("a_wsb", [128, 4, 128], BF16)
    bsb = A.alloc("a_bsb", [128, 4], F32)
    vg = A.alloc("a_vg", [128, 512], F32)
    ss = A.alloc("a_ss", [128, NT], F32)
    rs = A.alloc("a_rs", [128, NT], F32)
    hin = [A.alloc("a_hin", [128, 1536], BF16) for _ in range(2)]
    guv = [A.alloc("a_guv", [128, 1024], F32) for _ in range(2)]
    junk = A.alloc("a_junk", [128, 512], BF16)
    vn = [A.alloc("a_vn", [128, 512], BF16) for _ in range(2)]
    t1 = [A.alloc("a_t1", [128, 512], F32) for _ in range(2)]
    sz = [A.alloc("a_sz", [128, 512], F32) for _ in range(2)]
    ysb = [A.alloc("a_y", [128, 512], BF16) for _ in range(2)]
    S.op("sp", lambda e: e.dma_start(out=wsf[:], in_=d["a_wsT"][l]), w=["a_wsf"], dma=True)
    S.op("sp", lambda e: e.dma_start(out=bsb[:], in_=d["a_bsT"][l]), w=["a_bsb"], dma=True)
    S.op("sp", lambda e: e.dma_start(out=vg[:], in_=d["a_v_gain"][l].partition_broadcast(128)), w=["a_vg"], dma=True)
    S.op("dve", lambda e: e.tensor_copy(out=wsb[:], in_=wsf[:]), r=["a_wsf"], w=["a_wsb"])
    S.op("dve", lambda e: e.memset(wsb[64:128, :, 0:64], 0.0), w=["a_wsb"])
    S.op("dve", lambda e: e.memset(ss[:], 0.0), w=[("a_ss", i) for i in range(NT)])
    def sA(i):
        b = i % 2
        load_h(k, "sp", hin[b][:], i, 0, 1536, ("a_hin", b))
        S.op("act", lambda e, b=b: e.activation(out=guv[b][:], in_=hin[b][:, 0:1024], func=AF.Gelu_apprx_tanh),
             r=[("a_hin", b)], w=[("a_guv", b)])
        S.op("act", lambda e, b=b, i=i: e.activation(out=junk[:], in_=guv[b][:, 512:1024], func=AF.Square, accum_out=ss[:, i:i + 1]),
             r=[("a_guv", b)], w=["a_junk", ("a_ss", i)])
        rstd_op(k, ss[:, i:i + 1], 512, rs[:, i:i + 1], ("a_ss", i), ("a_rs", i))
        S.op("dve", lambda e, b=b, i=i: e.scalar_tensor_tensor(out=vn[b][:], in0=guv[b][:, 512:1024], scalar=rs[:, i:i + 1], in1=vg[:],
                                                             op0=ALU.mult, op1=ALU.mult),
             r=[("a_guv", b), ("a_rs", i), "a_vg"], w=[("a_vn", b)])

    def sB(i):
        b = i % 2
        ps, pk = ps_next(k, ps_pool)
        for g in range(4):
            S.op("pe", lambda e, g=g, b=b, ps=ps: e.matmul(ps[:, g * 128:(g + 1) * 128], wsb[:, g, :], vn[b][:, g * 128:(g + 1) * 128], start=True, stop=True),
                 r=["a_wsb", ("a_vn", b)], w=[pk])
        for g in range(4):
            S.op("dve", lambda e, g=g, b=b, ps=ps: e.scalar_tensor_tensor(out=t1[b][:, g * 128:(g + 1) * 128], in0=ps[:, g * 128:(g + 1) * 128],
                                                                       scalar=bsb[:, g:g + 1], in1=guv[b][:, g * 128:(g + 1) * 128],
                                                                       op0=ALU.add, op1=ALU.mult),
                 r=[pk, "a_bsb", ("a_guv", b)], w=[("a_t1", b)])
        S.op("act", lambda e, b=b: e.activation(out=sz[b][:], in_=hin[b][:, 1024:1536], func=AF.Silu), r=[("a_hin", b)], w=[("a_sz", b)])
        S.op("pool", lambda e, b=b: e.tensor_tensor(out=ysb[b][:], in0=sz[b][:], in1=t1[b][:], op=ALU.mult), r=[("a_t1", b), ("a_sz", b)], w=[("a_y", b)])

    def sC(i):
        b = i % 2
        y_to_yT(k, ysb[b], ("a_y", b), yT, 0, i)

    for n in range(NT + 2):
        if n < NT:
            sA(n)
        if 0 <= n - 1 < NT:
            sB(n - 1)
        if 0 <= n - 2 < NT:
            sC(n - 2)
        yield
    A.release(m)


def mixer_a(k, l, yT):
    for _ in mixer_a_gen(k, l, yT):
        pass


def head_norm(k, src, nh, hd, gain_full, out, ss, rs, tmp, rkeys, key_ss, key_rs, key_tmp, key_out, sq, gkey="gfull"):
    S = k.S
    S.op("act", lambda e: e.activation(out=sq, in_=src, func=AF.Square), r=rkeys, w=[key_tmp + ("sq",)])
    S.op("dve", lambda e: e.tensor_reduce(out=ss, in_=sq.rearrange("p (h d) -> p h d", d=hd), axis=AX.X, op=ALU.add),
         r=[key_tmp + ("sq",)], w=[key_ss])
    rstd_op(k, ss, hd, rs, key_ss, key_rs)
    S.op("dve", lambda e: e.tensor_tensor(out=tmp.rearrange("p (h d) -> p h d", d=hd), in0=src.rearrange("p (h d) -> p h d", d=hd),
                                          in1=rs.unsqueeze(2).to_broadcast([128, nh, hd]), op=ALU.mult),
         r=rkeys + [key_rs], w=[key_tmp])
    S.op("pool", lambda e: e.tensor_tensor(out=out, in0=tmp, in1=gain_full, op=ALU.mult), r=[key_tmp, gkey], w=[key_out])


def rep_gain(k, gfull, gain, nh, hd, gkey="gfull"):
    for h in range(nh):
        k.S.op("pool", lambda e, h=h: e.tensor_copy(out=gfull[:, h * hd:(h + 1) * hd], in_=gain), r=["gains"], w=[gkey])


def gate_and_store(k, l, yv, yv_key, c0z, i, g, yT, bufs, b):
    S = k.S
    zin, sz, ysb = bufs["zin"][b], bufs["sz"][b], bufs["ysb"][b]
    load_h(k, "sp", zin[:], i, c0z, 512, ("g_zin", b))
    S.op("act", lambda e: e.activation(out=sz[:], in_=zin[:], func=AF.Silu), r=[("g_zin", b)], w=[("g_sz", b)])
    S.op("pool", lambda e: e.tensor_tensor(out=ysb[:], in0=yv, in1=sz[:], op=ALU.mult), r=[yv_key, ("g_sz", b)], w=[("g_y", b)])
    y_to_yT(k, ysb, ("g_y", b), yT, g, i)


def gate_bufs(A):
    return {"zin": [A.alloc("g_zin", [128, 512], BF16) for _ in range(2)],
            "sz": [A.alloc("g_sz", [128, 512], F32) for _ in range(2)],
            "ysb": [A.alloc("g_y", [128, 512], BF16) for _ in range(2)]}


def run_pipelined(pairs, stage1, stage2):
    prev = None
    n = 0
    for p in pairs:
        if "pre" in p:
            stage1(p, 0)
            continue
        stage1(p, n % 2)
        if prev is not None:
            stage2(prev, (n - 1) % 2)
        prev = p
        n += 1
    if prev is not None:
        stage2(prev, (n - 1) % 2)


def finalize_heads(k, pfx, psO, pko, rden, yv_b, yv_key, nh_bank, hd, slot):
    S = k.S
    for hg in range(2):
        pv = psO[hg][:, :].rearrange("p (h c) -> p h c", c=slot)
        S.op("dve", lambda e, hg=hg, pv=pv: e.reciprocal(out=rden[:, hg * nh_bank:(hg + 1) * nh_bank], in_=pv[:, :, hd]), r=[pko[hg]], w=[(pfx + "_rden", hg)])
        S.op("dve", lambda e, hg=hg, pv=pv: e.tensor_tensor(out=yv_b[:, hg * nh_bank * hd:(hg + 1) * nh_bank * hd].rearrange("p (h c) -> p h c", c=hd), in0=pv[:, :, 0:hd],
                                                          in1=rden[:, hg * nh_bank:(hg + 1) * nh_bank].unsqueeze(2).to_broadcast([128, nh_bank, hd]), op=ALU.mult),
             r=[pko[hg], (pfx + "_rden", hg)], w=[yv_key])


def mixer_d(k, l, yT):
    S, A, d = k.S, k.A, k.d
    m = A.mark()
    c0 = int(OFF[14])
    qkT = A.alloc("d_qkT", [128, 8, SEQ], BF16)
    vaug = A.alloc("d_vaug", [128, NT, 8, 80], BF16)
    bias5 = A.alloc("d_bias5", [128, 5, 8, 128], F32)
    gains = A.alloc("d_gains", [128, 2, 64], F32)
    S.op("sp", lambda e: e.dma_start(out=bias5[:], in_=d["d_bias5"][l]), w=["d_bias5"], dma=True)
    S.op("sp", lambda e: e.dma_start(out=gains[:, 0, :], in_=d["d_q_gain"][l].partition_broadcast(128)), w=["gains"], dma=True)
    S.op("sp", lambda e: e.dma_start(out=gains[:, 1, :], in_=d["d_k_gain"][l].partition_broadcast(128)), w=["gains"], dma=True)
    S.op("dve", lambda e: e.tensor_scalar(out=gains[:, 0, :], in0=gains[:, 0, :], scalar1=0.125, scalar2=None, op0=ALU.mult), r=["gains"], w=["gains"])
    S.op("pool", lambda e: e.memset(vaug[:, :, :, 64:65], 1.0), w=[("d_v", j) for j in range(NT)])
    gfull = A.alloc("d_gfull", [128, 2, 512], F32)
    for qk in range(2):
        rep_gain(k, gfull[:, qk, :], gains[:, qk, :], 8, 64)
    m1 = A.mark()
    qkv = [A.alloc("d_qkv", [128, 1536], BF16) for _ in range(2)]
    sq = A.alloc("d_sq", [128, 1024], F32)
    tmp = A.alloc("d_tmp", [128, 1024], F32)
    qkn = [A.alloc("d_qkn", [128, 1024], BF16) for _ in range(2)]
    ss = A.alloc("d_ss", [128, NT, 16], F32)
    rs = A.alloc("d_rs", [128, NT, 16], F32)
    def s1a(i):
        b = i % 2
        load_h(k, "sp", qkv[b][:], i, c0, 1536, ("d_qkv", b))
        for qk in range(2):
            head_norm(k, qkv[b][:, qk * 512:(qk + 1) * 512], 8, 64, gfull[:, qk, :], qkn[b][:, qk * 512:(qk + 1) * 512],
                      ss[:, i, qk * 8:(qk + 1) * 8], rs[:, i, qk * 8:(qk + 1) * 8], tmp[:, qk * 512:(qk + 1) * 512],
                      [("d_qkv", b)], ("d_ss", i, qk), ("d_rs", i, qk), ("d_tmp", qk), ("d_qkn", b, qk), sq[:, qk * 512:(qk + 1) * 512])

    def s1b(i):
        b = i % 2
        blocks = [(qkn[b][:, c * 128:(c + 1) * 128], ("d_qkn", b, c // 4)) for c in range(8)]
        transpose_to(k, blocks, qkT[:, :, i * 128:(i + 1) * 128], ("d_qkT", i))
        S.op("pool", lambda e, i=i, b=b: e.tensor_copy(out=vaug[:, i, :, 0:64], in_=qkv[b][:, 1024:1536].rearrange("p (h d) -> p h d", d=64)),
             r=[("d_qkv", b)], w=[("d_v", i)])
    ein = [A.alloc("d_ein", [128, 1024], F32) for _ in range(2)]
    expT = [A.alloc("d_expT", [128, 8, 128], BF16) for _ in range(2)]
    rden = A.alloc("d_rden", [128, 8], F32)
    yv = [A.alloc("d_yv", [128, 512], F32) for _ in range(2)]
    gb = gate_bufs(A)
    psO = [k.psf[4], k.psf[5]]
    pko = [("psf", 4), ("psf", 5)]
    pairs = [dict(pre=(0, "a")), dict(pre=(0, "b"))]
    for i in range(NT):
        P = [dict(i=i, j=j, first=(j == max(0, i - 4)), last=(j == i)) for j in range(max(0, i - 4), i + 1)]
        n = len(P)
        if i + 1 < NT:
            P = [dict(pre=(i + 1, "a"))] + P[0:(n + 1) // 2] + [dict(pre=(i + 1, "b"))] + P[(n + 1) // 2:]
        pairs += P
    s1 = {"a": s1a, "b": s1b}

    def stage1(p, b2):
        if "pre" in p:
            s1[p["pre"][1]](p["pre"][0])
            return
        i, j = p["i"], p["j"]
        dl = i - j
        pss = [ps_next(k), ps_next(k)]
        for h in range(8):
            ps, pk = pss[h % 2]
            hp = h % 2
            S.op("pe", lambda e, h=h, hp=hp, ps=ps: e.matmul(ps[:, (h // 2) * 128:(h // 2 + 1) * 128],
                                                           qkT[hp * 64:(hp + 1) * 64, 4 + h // 2, j * 128:(j + 1) * 128],
                                                           qkT[hp * 64:(hp + 1) * 64, h // 2, i * 128:(i + 1) * 128], start=True, stop=True),
                 r=[("d_qkT", i), ("d_qkT", j)], w=[pk])
        for hg in range(2):
            ps, pk = pss[hg]
            S.op("dve", lambda e, hg=hg, ps=ps: e.tensor_tensor(
                out=ein[b2][:, hg * 512:(hg + 1) * 512], in0=ps[:, :],
                in1=bias5[:, dl, hg * 4:(hg + 1) * 4, :].rearrange("p h t -> p (h t)"), op=ALU.add),
                r=[pk, "d_bias5"], w=[("d_ein", b2, hg)])
            S.op("act", lambda e, hg=hg: e.activation(out=expT[b2][:, hg * 4:(hg + 1) * 4, :].rearrange("p h t -> p (h t)"),
                                                    in_=ein[b2][:, hg * 512:(hg + 1) * 512], func=AF.Exp),
                 r=[("d_ein", b2, hg)], w=[("d_expT", b2, hg)])

    def stage2(p, b2):
        if "pre" in p:
            return
        i, j = p["i"], p["j"]
        for h in range(8):
            po = psO[h // 4]
            S.op("pe", lambda e, h=h, po=po: e.matmul(po[:, (h % 4) * 128:(h % 4) * 128 + 65], expT[b2][:, (h % 2) * 4 + h // 2, :], vaug[:, j, h, 0:65],
                                                    start=(p["first"] and h % 4 == 0), stop=(p["last"] and h % 4 == 3), skip_group_check=True),
                 r=[("d_expT", b2, h % 2), ("d_v", j)], w=[pko[h // 4]])
        if p["last"]:
            b = i % 2
            finalize_heads(k, "d", psO, pko, rden, yv[b], ("d_yv", b), 4, 64, 128)
            gate_and_store(k, l, yv[b][:], ("d_yv", b), c0 + 1536, i, 3, yT, gb, b)

    run_pipelined(pairs, stage1, stage2)
    A.release(m)


def bcast_load(k, dst, src_vec, key):
    return k.S.op("sp", lambda e: e.dma_start(out=dst, in_=src_vec.partition_broadcast(128)), w=[key], dma=True)


def mixer_c(k, l, yT):
    S, A, d = k.S, k.A, k.d
    m = A.mark()
    c0 = int(OFF[10])
    QKn = A.alloc("c_QKn", [128, 8, SEQ], BF16)
    QKr = A.alloc("c_QKr", [128, 4, SEQ], BF16)
    vaug = A.alloc("c_vaug", [128, NT, 4, 144], BF16)
    S.op("pool", lambda e: e.memset(vaug[:, :, :, 128:129], 1.0), w=[("c_v", j) for j in range(NT)])
    m1 = A.mark()
    wstg = A.alloc("c_wstg", [128, 1024], F32)
    wqb = A.alloc("c_wqb", [128, 3, 768], BF16)
    wkb = A.alloc("c_wkb", [128, 1024], BF16)
    g_qa = A.alloc("c_gqa", [128, 384], F32)
    g_kva = A.alloc("c_gkva", [128, 128], F32)
    g_q = A.alloc("c_gq", [128, 192], F32)
    g_k = A.alloc("c_gk", [128, 192], F32)
    cs = A.alloc("c_cs", [128, NT, 64], F32)
    S.op("sp", lambda e: e.dma_start(out=cs[:], in_=d["rope_cs"]), w=["c_cs"], dma=True)
    bcast_load(k, g_qa[:], d["c_qa_gain"][l], "gains")
    bcast_load(k, g_kva[:], d["c_kva_gain"][l], "gains")
    bcast_load(k, g_q[:], d["c_q_gain"][l], "gains")
    bcast_load(k, g_k[:], d["c_k_gain"][l], "gains")
    gq_full = A.alloc("c_gqf", [128, 768], F32)
    gk_full = A.alloc("c_gkf", [128, 768], F32)
    rep_gain(k, gq_full[:], g_q[:], 4, 192)
    rep_gain(k, gk_full[:], g_k[:], 4, 192)
    for kc in range(3):
        S.op("sp", lambda e, kc=kc: e.dma_start(out=wstg[:, 0:768], in_=d["c_w_qb"][l][kc * 128:(kc + 1) * 128, :]), w=["c_wstg"], dma=True)
        S.op("dve", lambda e, kc=kc: e.tensor_copy(out=wqb[:, kc, :], in_=wstg[:, 0:768]), r=["c_wstg"], w=["c_wqb"])
    S.op("sp", lambda e: e.dma_start(out=wstg[:, :], in_=d["c_w_kvb"][l]), w=["c_wstg"], dma=True)
    S.op("dve", lambda e: e.tensor_copy(out=wkb[:], in_=wstg[:, :]), r=["c_wstg"], w=["c_wkb"])
    hin = [A.alloc("c_hin", [128, 576], BF16) for _ in range(2)]
    sq = A.alloc("c_sq", [128, 768], F32)
    tmp = A.alloc("c_tmp", [128, 768], F32)
    lat = [A.alloc("c_lat", [128, 512], BF16) for _ in range(2)]
    latT = [A.alloc("c_latT", [128, 4, 128], BF16) for _ in range(2)]
    qkf = A.alloc("c_qkf", [128, 8, 192], F32)
    qkn = A.alloc("c_qkn", [128, 8, 192], F32)
    qkb = [A.alloc("c_qkb", [128, 8, 128], BF16) for _ in range(2)]
    qkr = [A.alloc("c_qkr", [128, 8, 64], BF16) for _ in range(2)]
    rt = [A.alloc("c_rt", [128, 8, 32], F32) for _ in range(4)]
    ss = A.alloc("c_ss", [128, NT, 16], F32)
    rs = A.alloc("c_rs", [128, NT, 16], F32)
    def s1a(i):
        b = i % 2
        load_h(k, "sp", hin[b][:], i, c0, 576, ("c_hin", b))
        head_norm(k, hin[b][:, 0:384], 1, 384, g_qa[:], lat[b][:, 0:384], ss[:, i, 0:1], rs[:, i, 0:1], tmp[:, 0:384],
                  [("c_hin", b)], ("c_ss", i, 0), ("c_rs", i, 0), ("c_tmp", 2), ("c_lat", b, 0), sq[:, 0:384], gkey="gains")
        head_norm(k, hin[b][:, 384:512], 1, 128, g_kva[:], lat[b][:, 384:512], ss[:, i, 1:2], rs[:, i, 1:2], tmp[:, 384:512],
                  [("c_hin", b)], ("c_ss", i, 1), ("c_rs", i, 1), ("c_tmp", 2), ("c_lat", b, 1), sq[:, 384:512], gkey="gains")

    def s1b(i):
        b = i % 2
        blocks = [(lat[b][:, c * 128:(c + 1) * 128], ("c_lat", b, 0 if c < 3 else 1)) for c in range(4)]
        transpose_to(k, blocks, latT[b][:, :, :], ("c_latT", b))
        pq = [ps_next(k), ps_next(k)]
        for nh, (n0, n1) in enumerate(((0, 512), (512, 768))):
            ps, pk = pq[nh]
            for kc in range(3):
                S.op("pe", lambda e, ps=ps, kc=kc, n0=n0, n1=n1, b=b: e.matmul(ps[:, 0:n1 - n0], latT[b][:, kc, :], wqb[:, kc, n0:n1], start=(kc == 0), stop=(kc == 2)),
                     r=[("c_latT", b), "c_wqb"], w=[pk])
        pkv = [ps_next(k), ps_next(k)]
        for nh in range(2):
            ps, pk = pkv[nh]
            S.op("pe", lambda e, ps=ps, nh=nh, b=b: e.matmul(ps[:, :], latT[b][:, 3, :], wkb[:, nh * 512:(nh + 1) * 512], start=True, stop=True),
                 r=[("c_latT", b), "c_wkb"], w=[pk])
        qflat = qkf[:, 0:4, :].rearrange("p h c -> p (h c)")
        copy_op(S, "act", qflat[:, 0:512], pq[0][0][:, :], r=[pq[0][1]], w=[("c_qkf", 0)])
        copy_op(S, "act", qflat[:, 512:768], pq[1][0][:, 0:256], r=[pq[1][1]], w=[("c_qkf", 0)])
        for nh in range(2):
            pv = pkv[nh][0][:, :].rearrange("p (h c) -> p h c", c=256)
            copy_op(S, "dve", qkf[:, 4 + 2 * nh:6 + 2 * nh, 0:128], pv[:, :, 0:128], r=[pkv[nh][1]], w=[("c_qkf", 1)])
            copy_op(S, "dve", vaug[:, i, 2 * nh:2 * nh + 2, 0:128], pv[:, :, 128:256], r=[pkv[nh][1]], w=[("c_v", i)])
        for h in range(4):
            S.op("pool", lambda e, b=b, h=h: e.tensor_copy(out=qkf[:, 4 + h, 128:192], in_=hin[b][:, 512:576]),
                 r=[("c_hin", b)], w=[("c_qkf", 1)])
        for qk, gain in ((0, gq_full), (1, gk_full)):
            head_norm(k, qkf[:, 4 * qk:4 * qk + 4, :].rearrange("p h c -> p (h c)"), 4, 192, gain[:],
                      qkn[:, 4 * qk:4 * qk + 4, :].rearrange("p h c -> p (h c)"), ss[:, i, 4 + 4 * qk:8 + 4 * qk], rs[:, i, 4 + 4 * qk:8 + 4 * qk],
                      tmp[:, :], [("c_qkf", qk)], ("c_ss", i, 2 + qk), ("c_rs", i, 2 + qk), ("c_tmp", 2), ("c_qkn", qk), sq[:, :])
        rq = [("c_qkn", 0), ("c_qkn", 1)]
        S.op("pool", lambda e, b=b: e.tensor_copy(out=qkb[b][:, :, :], in_=qkn[:, :, 0:128]), r=rq, w=[("c_qkb", b)])
        x1, x2 = qkn[:, :, 128:160], qkn[:, :, 160:192]
        cc = cs[:, i, 0:32].unsqueeze(1).to_broadcast([128, 8, 32])
        sn = cs[:, i, 32:64].unsqueeze(1).to_broadcast([128, 8, 32])
        for ti, (xa, tb) in enumerate(((x1, cc), (x2, sn), (x1, sn), (x2, cc))):
            S.op("dve", lambda e, ti=ti, xa=xa, tb=tb: e.tensor_tensor(out=rt[ti][:], in0=xa, in1=tb, op=ALU.mult), r=rq + ["c_cs"], w=[("c_rt", ti)])
        S.op("pool", lambda e, b=b: e.tensor_tensor(out=qkr[b][:, :, 0:32], in0=rt[0][:], in1=rt[1][:], op=ALU.subtract),
             r=[("c_rt", 0), ("c_rt", 1)], w=[("c_qkr", b)])
        S.op("pool", lambda e, b=b: e.tensor_tensor(out=qkr[b][:, :, 32:64], in0=rt[2][:], in1=rt[3][:], op=ALU.add),
             r=[("c_rt", 2), ("c_rt", 3)], w=[("c_qkr", b)])

    def s1c(i):
        b = i % 2
        blocks = [(qkb[b][:, c, :], ("c_qkb", b)) for c in range(8)]
        transpose_to(k, blocks, QKn[:, :, i * 128:(i + 1) * 128], ("c_QKn", i))
        blocks = [(qkr[b][:, 2 * c:2 * c + 2, :].rearrange("p h c -> p (h c)"), ("c_qkr", b)) for c in range(4)]
        transpose_to(k, blocks, QKr[:, :, i * 128:(i + 1) * 128], ("c_QKr", i))
    expT = [A.alloc("c_expT", [128, 4, 128], BF16) for _ in range(2)]
    rden = A.alloc("c_rden", [128, 4], F32)
    yv = [A.alloc("c_yv", [128, 512], F32) for _ in range(2)]
    gb = gate_bufs(A)
    psO = [k.psf[4], k.psf[5]]
    pko = [("psf", 4), ("psf", 5)]
    scale = float(192 ** -0.5)
    pairs = [dict(pre=(0, "a")), dict(pre=(0, "b")), dict(pre=(0, "c"))]
    for i in range(NT):
        P = [dict(i=i, j=j, first=(j == 0), last=(j == i)) for j in range(i + 1)]
        n = len(P)
        if i + 1 < NT:
            P = [dict(pre=(i + 1, "a"))] + P[0:n // 3] + [dict(pre=(i + 1, "b"))] + P[n // 3:2 * n // 3] + [dict(pre=(i + 1, "c"))] + P[2 * n // 3:]
        pairs += P
    s1 = {"a": s1a, "b": s1b, "c": s1c}

    def stage1(p, b2):
        if "pre" in p:
            s1[p["pre"][1]](p["pre"][0])
            return
        i, j = p["i"], p["j"]
        pss = [ps_next(k), ps_next(k)]
        for h in (0, 2, 1, 3):
            hp = h % 2
            ps, pk = pss[hp]
            c_ = (h // 2) * 128
            S.op("pe", lambda e, h=h, ps=ps, c_=c_: e.matmul(ps[:, c_:c_ + 128], QKn[:, 4 + h, j * 128:(j + 1) * 128], QKn[:, h, i * 128:(i + 1) * 128],
                                                           start=True, stop=False),
                 r=[("c_QKn", i), ("c_QKn", j)], w=[pk])
            S.op("pe", lambda e, h=h, hp=hp, ps=ps, c_=c_: e.matmul(ps[:, c_:c_ + 128], QKr[hp * 64:(hp + 1) * 64, 2 + h // 2, j * 128:(j + 1) * 128],
                                                                  QKr[hp * 64:(hp + 1) * 64, h // 2, i * 128:(i + 1) * 128], start=False, stop=True),
                 r=[("c_QKr", i), ("c_QKr", j)], w=[pk])
        for hp in range(2):
            ps, pk = pss[hp]
            S.op("act", lambda e, ps=ps, hp=hp: e.activation(out=expT[b2][:, 2 * hp:2 * hp + 2, :].rearrange("p h t -> p (h t)"), in_=ps[:, 0:256], func=AF.Exp, scale=scale),
                 r=[pk], w=[("c_expT", b2)])
        if j == i:
            S.op("pool", lambda e: e.memset(expT[b2][64:128, :, 0:64], 0.0), r=[("c_expT", b2)], w=[("c_expT", b2)])

    def stage2(p, b2):
        if "pre" in p:
            return
        i, j = p["i"], p["j"]
        for h in range(4):
            po = psO[h // 2]
            S.op("pe", lambda e, h=h, po=po: e.matmul(po[:, (h % 2) * 256:(h % 2) * 256 + 129], expT[b2][:, (h % 2) * 2 + h // 2, :], vaug[:, j, h, 0:129],
                                                    start=(j == 0 and h % 2 == 0), stop=(j == i and h % 2 == 1), skip_group_check=True),
                 r=[("c_expT", b2), ("c_v", j)], w=[pko[h // 2]])
        if p["last"]:
            b = i % 2
            finalize_heads(k, "c", psO, pko, rden, yv[b], ("c_yv", b), 2, 128, 256)
            gate_and_store(k, l, yv[b][:], ("c_yv", b), c0 + 576, i, 2, yT, gb, b)

    run_pipelined(pairs, stage1, stage2)
    A.release(m)


def b_pre_gen(k, l):
    S, A, d = k.S, k.A, k.d
    A.mode = "top"
    m = A.mark()
    c0 = int(OFF[3])
    IQT = A.alloc("b_IQT", [128, 5, SEQ], BF16)
    absw = A.alloc("b_absw", [128, NT, 8], F32)
    sgnw = A.alloc("b_sgnw", [128, NT, 8], F32)
    hin = [A.alloc("b_hiq", [128, 584], BF16) for _ in range(2)]
    ikk = [A.alloc("b_ikk", [128, 128], BF16) for _ in range(2)]
    NBIS = 18
    score2 = [A.alloc("b_score", [128, SEQ], F32) for _ in range(2)]
    bs2 = [A.alloc("b_bs", [128, 8], F32) for _ in range(2)]
    W2 = [A.alloc("b_W", [128, NBIS + 1], F32) for _ in range(2)]
    pow2 = A.alloc("b_pow2", [128, NBIS + 1], F32)
    mb = [A.alloc("b_mb", [128, SEQ], BF16) for _ in range(2)]
    tmpr = [A.alloc("b_tmpr", [128, 512], F32) for _ in range(4)]
    tr = [0]
    A.mode = "bottom"
    for kk in range(NBIS + 1):
        S.op("pool", lambda e, kk=kk: e.memset(pow2[:, kk:kk + 1], float(2.0 ** -(kk + 1))), w=["b_pow2"])

    def s1iq(i):
        b = i % 2
        load_h(k, "sp", hin[b][:], i, c0 + 640, 584, ("b_hiq", b))
        for hh in range(2):
            S.op("pool", lambda e, hh=hh: e.tensor_copy(out=ikk[b][:, hh * 64:(hh + 1) * 64], in_=hin[b][:, 512:576]), r=[("b_hiq", b)], w=[("b_ikk", b)])
        S.op("act", lambda e: e.activation(out=absw[:, i, :], in_=hin[b][:, 576:584], func=AF.Abs), r=[("b_hiq", b)], w=[("b_absw", i)])
        S.op("act", lambda e: e.activation(out=sgnw[:, i, :], in_=hin[b][:, 576:584], func=AF.Sign), r=[("b_hiq", b)], w=[("b_sgnw", i)])
        blocks = [(hin[b][:, c * 128:(c + 1) * 128], ("b_hiq", b)) for c in range(4)] + [(ikk[b][:, :], ("b_ikk", b))]
        transpose_to(k, blocks, IQT[:, :, i * 128:(i + 1) * 128], ("b_IQT", i))

    def idx_part(i):
        p2 = i % 2
        score, bs, W = score2[p2], bs2[p2], W2[p2]
        n_i = 128 * (i + 1)
        nch = (n_i + 511) // 512
        for c_ in range(nch):
            cw = min(512, n_i - 512 * c_)
            for h in range(8):
                hp = h % 2
                ps, pk = ps_next(k, k.b_pre_pool)
                S.op("pe", lambda e, ps=ps, h=h, hp=hp, c_=c_, cw=cw: e.matmul(ps[:, 0:cw], IQT[hp * 64:(hp + 1) * 64, h // 2, i * 128:(i + 1) * 128],
                                                                           IQT[hp * 64:(hp + 1) * 64, 4, c_ * 512:c_ * 512 + cw], start=True, stop=True),
                     r=[("b_IQT", i)] + [("b_IQT", jj) for jj in range(4 * c_, min(4 * c_ + 4, i + 1))], w=[pk])
                b3 = tr[0] % 4
                tr[0] += 1
                S.op("act", lambda e, ps=ps, b3=b3, cw=cw, h=h: e.activation(out=tmpr[b3][:, 0:cw], in_=ps[:, 0:cw], func=AF.Relu, scale=absw[:, i, h:h + 1]),
                     r=[pk, ("b_absw", i)], w=[("b_tmpr", b3)])
                sc = score[:, c_ * 512:c_ * 512 + cw]
                if h == 0:
                    S.op("dve", lambda e, sc=sc, b3=b3, cw=cw: e.tensor_scalar(out=sc, in0=tmpr[b3][:, 0:cw], scalar1=sgnw[:, i, 0:1], scalar2=None, op0=ALU.mult),
                         r=[("b_tmpr", b3), ("b_sgnw", i)], w=[("b_score", p2, c_)])
                else:
                    S.op("dve", lambda e, sc=sc, b3=b3, cw=cw, h=h: e.scalar_tensor_tensor(out=sc, in0=tmpr[b3][:, 0:cw], scalar=sgnw[:, i, h:h + 1], in1=sc,
                                                                                       op0=ALU.mult, op1=ALU.add),
                         r=[("b_tmpr", b3), ("b_sgnw", i), ("b_score", p2, c_)], w=[("b_score", p2, c_)])
                yield
        allsc = [("b_score", p2, c_) for c_ in range(nch)]
        bk = ("b_bs", p2)
        lo, hi, w0, mid = (bs[:, c_:c_ + 1] for c_ in range(4))
        if i >= 2:
            S.op("dve", lambda e: e.tensor_reduce(out=lo, in_=score[:, 0:n_i], axis=AX.X, op=ALU.min), r=allsc, w=[bk])
        S.op("dve", lambda e: e.memset(score[0:64, n_i - 64:n_i], -1e30), r=allsc, w=allsc)
        if i >= 2:
            S.op("dve", lambda e: e.tensor_reduce(out=hi, in_=score[:, 0:n_i], axis=AX.X, op=ALU.max), r=allsc, w=[bk])
            S.op("dve", lambda e: e.tensor_tensor(out=w0, in0=hi, in1=lo, op=ALU.subtract), r=[bk], w=[bk])
            S.op("dve", lambda e: e.tensor_scalar(out=W[:, :], in0=pow2[:, :], scalar1=w0, scalar2=None, op0=ALU.mult), r=[bk, "b_pow2"], w=[("b_W", p2)])
            S.op("dve", lambda e: e.tensor_tensor(out=mid, in0=lo, in1=W[:, 0:1], op=ALU.add), r=[bk, ("b_W", p2)], w=[bk])
        else:
            S.op("dve", lambda e: e.memset(lo, -1e29), w=[bk])

    def bis_step(i, kk):
        if i < 2:
            return
        p2 = i % 2
        score, bs, W = score2[p2], bs2[p2], W2[p2]
        n_i = 128 * (i + 1)
        allsc = [("b_score", p2, c_) for c_ in range((n_i + 511) // 512)]
        bk = ("b_bs", p2)
        mid, cnt, gw = bs[:, 3:4], bs[:, 4:5], bs[:, 5:6]
        S.op("dve", lambda e: e.tensor_scalar(out=mb[p2][:, 0:n_i], in0=score[:, 0:n_i], scalar1=mid, scalar2=0.0, op0=ALU.is_ge, op1=ALU.add, accum_out=cnt),
             r=allsc + [bk], w=[bk, ("b_mb", p2)])
        S.op("dve", lambda e: e.tensor_scalar(out=gw, in0=cnt, scalar1=256.0, scalar2=-0.5, op0=ALU.is_ge, op1=ALU.add), r=[bk], w=[bk])
        S.op("dve", lambda e: e.scalar_tensor_tensor(out=mid, in0=gw, scalar=W[:, kk:kk + 1], in1=mid, op0=ALU.mult, op1=ALU.add),
             r=[bk, ("b_W", p2)], w=[bk])

    def mask_part(i):
        p2 = i % 2
        score, bs, W = score2[p2], bs2[p2], W2[p2]
        n_i = 128 * (i + 1)
        allsc = [("b_score", p2, c_) for c_ in range((n_i + 511) // 512)]
        bk = ("b_bs", p2)
        lo, mid = bs[:, 0:1], bs[:, 3:4]
        if i >= 2:
            S.op("dve", lambda e: e.tensor_tensor(out=lo, in0=mid, in1=W[:, NBIS:NBIS + 1], op=ALU.subtract), r=[bk, ("b_W", p2)], w=[bk])
        mbb = mb[p2]
        S.op("dve", lambda e: e.tensor_scalar(out=mbb[:, 0:n_i], in0=score[:, 0:n_i], scalar1=lo, scalar2=NEGM, op0=ALU.is_lt, op1=ALU.mult),
             r=allsc + [bk], w=[("b_mb", p2)])
        S.op("pool", lambda e: e.dma_start(out=k.mask_d[i * 128:(i + 1) * 128, 0:n_i], in_=mbb[:, 0:n_i]), r=[("b_mb", p2)], w=[("mask_d", i)], dma=True)

    for t in range(min(4, NT)):
        s1iq(t)
        yield
    for _ in idx_part(0):
        yield
    for t in range(NT):
        if t + 4 < NT:
            s1iq(t + 4)
            yield
        gi = idx_part(t + 1) if t + 1 < NT else iter(())
        steps = list(range(NBIS)) if t >= 2 else []
        n_idx = 8 * ((128 * (t + 2) + 511) // 512) if t + 1 < NT else 0
        per = max(1, -(-n_idx // max(1, len(steps)))) if steps else n_idx
        for kk in steps:
            bis_step(t, kk)
            yield
            for _ in range(per):
                if next(gi, "end") != "end":
                    yield
        for _ in gi:
            yield
        mask_part(t)
        yield
    A.mode = "top"
    A.release(m)
    A.mode = "bottom"


def mixer_b(k, l, yT):
    S, A, d = k.S, k.A, k.d
    m = A.mark()
    c0 = int(OFF[3])
    QT = A.alloc("b_QT", [128, 5, SEQ], BF16)
    vaug = A.alloc("b_vaug", [128, NT, 80], BF16)
    bias3 = A.alloc("b_bias3", [128, 3, 8, 128], F32)
    gains = A.alloc("b_gains", [128, 2, 64], F32)
    S.op("sp", lambda e: e.dma_start(out=bias3[:], in_=d["b_bias3"]), w=["b_bias3"], dma=True)
    S.op("sp", lambda e: e.dma_start(out=gains[:, 0, :], in_=d["b_q_gain"][l].partition_broadcast(128)), w=["gains"], dma=True)
    S.op("sp", lambda e: e.dma_start(out=gains[:, 1, :], in_=d["b_k_gain"][l].partition_broadcast(128)), w=["gains"], dma=True)
    S.op("dve", lambda e: e.tensor_scalar(out=gains[:, 0, :], in0=gains[:, 0, :], scalar1=0.125, scalar2=None, op0=ALU.mult), r=["gains"], w=["gains"])
    for c in range(2):
        S.op("dve", lambda e, c=c: e.tensor_tensor(out=bias3[:, c, :, :], in0=bias3[:, c, :, :], in1=bias3[:, 2, :, :], op=ALU.subtract),
             r=["b_bias3"], w=["b_bias3"])
    S.op("pool", lambda e: e.memset(vaug[:, :, 64:65], 1.0), w=[("b_v", j) for j in range(NT)])
    gqf = A.alloc("b_gqf", [128, 512], F32)
    rep_gain(k, gqf[:], gains[:, 0, :], 8, 64)
    hin = [A.alloc("b_hin", [128, 640], BF16) for _ in range(2)]
    sq = A.alloc("b_sq", [128, 576], F32)
    tmp = A.alloc("b_tmp", [128, 576], F32)
    qn = [A.alloc("b_qn", [128, 640], BF16) for _ in range(2)]
    ss = A.alloc("b_ss", [128, NT, 16], F32)
    rs = A.alloc("b_rs", [128, NT, 16], F32)

    def s1a(i):
        b = i % 2
        load_h(k, "sp", hin[b][:], i, c0, 640, ("b_hin", b))
        head_norm(k, hin[b][:, 0:512], 8, 64, gqf[:], qn[b][:, 0:512], ss[:, i, 0:8], rs[:, i, 0:8], tmp[:, 0:512],
                  [("b_hin", b)], ("b_ss", i, 0), ("b_rs", i, 0), ("b_tmp", 0), ("b_qn", b, 0), sq[:, 0:512])
        head_norm(k, hin[b][:, 512:576], 1, 64, gains[:, 1, :], qn[b][:, 512:576], ss[:, i, 8:9], rs[:, i, 8:9], tmp[:, 512:576],
                  [("b_hin", b)], ("b_ss", i, 1), ("b_rs", i, 1), ("b_tmp", 1), ("b_qn", b, 1), sq[:, 512:576], gkey="gains")
        S.op("pool", lambda e, b=b: e.tensor_copy(out=qn[b][:, 576:640], in_=qn[b][:, 512:576]), r=[("b_qn", b, 1)], w=[("b_qn", b, 2)])
        S.op("pool", lambda e, b=b, i=i: e.tensor_copy(out=vaug[:, i, 0:64], in_=hin[b][:, 576:640]), r=[("b_hin", b)], w=[("b_v", i)])

    def s1b(i):
        b = i % 2
        blocks = [(qn[b][:, c * 128:(c + 1) * 128], ("b_qn", b, 0)) for c in range(4)] + [(qn[b][:, 512:640], ("b_qn", b, 2))]
        transpose_to(k, blocks, QT[:, :, i * 128:(i + 1) * 128], ("b_QT", i), extra_r=[("b_qn", b, 1)])

    mbl = [A.alloc("b_mbl", [128, SEQ], BF16) for _ in range(2)]
    maskT = [A.alloc("b_maskT", [128, NT, 128], BF16) for _ in range(2)]
    ein = [A.alloc("b_ein", [128, 1024], F32) for _ in range(1)]
    expT = [A.alloc("b_expT", [128, 8, 128], BF16) for _ in range(2)]
    rden = A.alloc("b_rden", [128, 8], F32)
    yv = [A.alloc("b_yv", [128, 512], F32) for _ in range(2)]
    gb = gate_bufs(A)
    psO = [k.psf[4], k.psf[5]]
    pko = [("psf", 4), ("psf", 5)]

    def mload(i):
        p2 = i % 2
        n_i = 128 * (i + 1)
        S.op("sp", lambda e: e.dma_start(out=mbl[p2][:, 0:n_i], in_=k.mask_d[i * 128:(i + 1) * 128, 0:n_i]), r=[("mask_d", i)], w=[("b_mbl", p2)], dma=True)
        for g0 in range(0, i + 1, 8):
            nb_ = min(8, i + 1 - g0)
            blocks = [(mbl[p2][:, (g0 + bi) * 128:(g0 + bi + 1) * 128], ("b_mbl", p2)) for bi in range(nb_)]
            transpose_to(k, blocks, maskT[p2][:, g0:g0 + nb_, :], ("b_maskT", p2, g0 // 8))

    pairs = []
    for t in range(min(2, NT)):
        pairs += [dict(pre=("a", t)), dict(pre=("b", t))]
    pairs += [dict(pre=("mask", 0))]
    for i in range(NT):
        near = [dict(i=i, j=j) for j in (i, i - 1) if j >= 0]
        far = [dict(i=i, j=j) for j in range(0, i - 1)]
        real = near + far
        for p in real:
            p["first"] = p is real[0]
            p["last"] = p is real[-1]
        tile = list(near)
        if i + 1 < NT:
            tile.append(dict(pre=("mask", i + 1)))
        if i + 2 < NT:
            tile.append(dict(pre=("a", i + 2)))
        tile += far
        if i + 2 < NT:
            tile.append(dict(pre=("b", i + 2)))
        pairs += tile
    s1 = {"a": s1a, "b": s1b, "mask": mload}

    def stage1(p, b2):
        if "pre" in p:
            s1[p["pre"][0]](p["pre"][1])
            return
        i, j = p["i"], p["j"]
        dl = min(i - j, 2)
        mT = maskT[i % 2]
        pss = [ps_next(k), ps_next(k)]
        for h in range(8):
            ps, pk = pss[h % 2]
            hp = h % 2
            S.op("pe", lambda e, h=h, hp=hp, ps=ps: e.matmul(ps[:, (h // 2) * 128:(h // 2 + 1) * 128],
                                                           QT[hp * 64:(hp + 1) * 64, 4, j * 128:(j + 1) * 128],
                                                           QT[hp * 64:(hp + 1) * 64, h // 2, i * 128:(i + 1) * 128], start=(h < 2), stop=False,
                                                           skip_group_check=True),
                 r=[("b_QT", i), ("b_QT", j)], w=[pk])
        for hg in range(2):
            ps, pk = pss[hg]
            S.op("pe", lambda e, ps=ps: e.matmul(ps[:, :], k.ident[:], mT[:, j, :].unsqueeze(1).to_broadcast([128, 4, 128]), start=False, stop=True, skip_group_check=True),
                 r=[("b_maskT", i % 2, j // 8), "ident"], w=[pk])
        for hg in range(2):
            ps, pk = pss[hg]
            if dl < 2:
                S.op("dve", lambda e, hg=hg, ps=ps: e.tensor_tensor(
                    out=ein[0][:, hg * 512:(hg + 1) * 512], in0=ps[:, :],
                    in1=bias3[:, dl, hg * 4:(hg + 1) * 4, :].rearrange("p h t -> p (h t)"), op=ALU.add),
                    r=[pk, "b_bias3"], w=[("b_ein", 0, hg)])
                S.op("act", lambda e, hg=hg: e.activation(out=expT[b2][:, hg * 4:(hg + 1) * 4, :].rearrange("p h t -> p (h t)"),
                                                        in_=ein[0][:, hg * 512:(hg + 1) * 512], func=AF.Exp),
                     r=[("b_ein", 0, hg)], w=[("b_expT", b2, hg)])
            else:
                S.op("act", lambda e, hg=hg, ps=ps: e.activation(out=expT[b2][:, hg * 4:(hg + 1) * 4, :].rearrange("p h t -> p (h t)"),
                                                               in_=ps[:, :], func=AF.Exp),
                     r=[pk], w=[("b_expT", b2, hg)])

    def stage2(p, b2):
        if "pre" in p:
            return
        i, j = p["i"], p["j"]
        for h in range(8):
            po = psO[h // 4]
            S.op("pe", lambda e, h=h, po=po: e.matmul(po[:, (h % 4) * 128:(h % 4) * 128 + 65], expT[b2][:, (h % 2) * 4 + h // 2, :], vaug[:, j, 0:65],
                                                    start=(p["first"] and h % 4 == 0), stop=(p["last"] and h % 4 == 3), skip_group_check=True),
                 r=[("b_expT", b2, h % 2), ("b_v", j)], w=[pko[h // 4]])
        if p["last"]:
            b = i % 2
            finalize_heads(k, "b", psO, pko, rden, yv[b], ("b_yv", b), 4, 64, 128)
            gate_and_store(k, l, yv[b][:], ("b_yv", b), c0 + 1224, i, 1, yT, gb, b)

    run_pipelined(pairs, stage1, stage2)
    A.release(m)


def build_program(mode="full", nlayers=DEPTH, mixers="ABCD"):
    nc = bass.Bass("TRN2", target_bir_lowering=False)
    k = K()
    k.nc = nc
    k.S = Sched(nc)
    k.A = Arena(nc)
    k.ps_i = k.pst_i = k.ev_i = k.ps_lo = k.ps_hi = 0
    k.out_dmas = []
    d = {}

    def inp(name, shape, dt=F32):
        d[name] = nc.dram_tensor(name, list(shape), dt, kind="ExternalInput").ap()

    inp("x", [SEQ, D_MODEL])
    inp("w_in", [DEPTH, D_MODEL, IN_COLS])
    inp("w_out", [DEPTH, D_MODEL, D_MODEL])
    inp("norm_g", [DEPTH, D_MODEL])
    inp("ident", [128, 128])
    inp("a_wsT", [DEPTH, 128, 4, 128])
    inp("a_bsT", [DEPTH, 128, 4])
    inp("a_v_gain", [DEPTH, 512])
    inp("b_bias3", [128, 3, 8, 128])
    inp("b_q_gain", [DEPTH, 64])
    inp("b_k_gain", [DEPTH, 64])
    inp("c_w_qb", [DEPTH, 384, 768])
    inp("c_w_kvb", [DEPTH, 128, 1024])
    inp("c_qa_gain", [DEPTH, 384])
    inp("c_kva_gain", [DEPTH, 128])
    inp("c_q_gain", [DEPTH, 192])
    inp("c_k_gain", [DEPTH, 192])
    inp("rope_cs", [128, NT, 64])
    inp("d_bias5", [DEPTH, 128, 5, 8, 128])
    inp("d_q_gain", [DEPTH, 64])
    inp("d_k_gain", [DEPTH, 64])
    k.d = d
    dbg = mode != "full"
    mixonly = mode == "mixonly"
    out_d = nc.dram_tensor("out", [SEQ, D_MODEL], F32, kind="ExternalOutput").ap()
    k.h_d = nc.dram_tensor("h_scr", [SEQ, IN_COLS], BF16, kind="ExternalInput" if mixonly else ("ExternalOutput" if dbg else "Internal")).ap()
    xs_d = nc.dram_tensor("x_scr", [SEQ, D_MODEL], F32).ap()
    k.mask_d = nc.dram_tensor("mask_scr", [SEQ, SEQ], BF16).ap()
    if dbg:
        ydbg = nc.dram_tensor("ydbg", [D_MODEL, SEQ], BF16, kind="ExternalOutput").ap()

    with ExitStack() as es:
        S, A = k.S, k.A
        k.psf = [es.enter_context(nc.psum_tensor("psf%d" % i, [128, 512], F32)) for i in range(6)]
        k.pstt = [es.enter_context(nc.psum_tensor("pst%d" % i, [128, 1024], BF16)) for i in range(2)]
        idf = A.alloc("idf", [128, 128], F32)
        k.ident = A.alloc("ident", [128, 128], BF16)
        S.op("sp", lambda e: e.dma_start(out=idf[:], in_=d["ident"]), w=["idf"], dma=True)
        S.op("dve", lambda e: e.tensor_copy(out=k.ident[:], in_=idf[:]), r=["idf"], w=["ident"])
        k.eps_t = A.alloc("eps_t", [128, 1], F32)
        S.op("dve", lambda e: e.memset(k.eps_t[:], EPS), w=["eps_t"])

        for l in range(nlayers):
            x_src = d["x"] if l == 0 else xs_d
            x_dst = out_d if l == nlayers - 1 else xs_d
            m0 = A.mark()
            use_b = "B" in mixers
            k.b_pre_pool = "all" if mixonly else "hi"
            if not mixonly:
                xnT = A.alloc("xnT", [128, KC, SEQ], BF16)
                pb = proj_bufs(A)
                hsb = [A.alloc("hsb", [128, 512], BF16) for _ in range(4)]
                phase_norm(k, l, x_src, xnT)
                norm_last = [S.ops[e][-1] for e in S.ENGS if S.ops[e] and S.ops[e][-1].fn is not None]
                gen = [None]
                ncb = (IN_COLS + 511) // 512
                order = [4, 5] + [cb for cb in range(ncb) if cb not in (4, 5)]

                n_y = 4 + 8 + sum((1 if t + 4 < NT else 0) + (18 if t >= 2 else 0) + (8 * ((128 * (t + 2) + 511) // 512) if t + 1 < NT else 0) + 1 for t in range(NT))
                quota = -(-n_y // ((ncb - 2) * NT - 8))

                def hook(pos, i):
                    if not use_b or pos < 1 or (pos == 1 and i < NT - 1):
                        return
                    if gen[0] is None:
                        for e in S.ENGS:
                            S.fence(e, norm_last)
                        gen[0] = b_pre_gen(k, l)
                    for _ in range(quota):
                        if next(gen[0], "end") == "end":
                            break

                phase_inproj(k, l, xnT, pb, hsb, order, hook, keep_dve_free=use_b)
                S.barrier()
            elif use_b:
                gen = [b_pre_gen(k, l)]
            A.release(m0)
            yT = A.alloc("yT", [128, KC, SEQ], BF16)
            if mixonly:
                S.op("pool", lambda e: e.memset(yT[:], 0.0), w=[("yT", i) for i in range(NT)])
            done_a = False
            if use_b:
                if "A" in mixers:
                    ga = mixer_a_gen(k, l, yT, ps_pool="lo" if not mixonly else "all")
                    for _ in ga:
                        for _ in range(8):
                            if next(gen[0], "end") == "end":
                                break
                    done_a = True
                for _ in gen[0]:
                    pass
                S.barrier()
            for g, (nm, fn) in enumerate((("A", mixer_a), ("B", mixer_b), ("C", mixer_c), ("D", mixer_d))):
                if nm == "A" and done_a:
                    continue
                if nm in mixers and fn is not None:
                    fn(k, l, yT)
                elif mixonly:
                    continue
                else:
                    mixer_stub(k, l, yT, g)
                S.barrier()
            if dbg and l == nlayers - 1:
                S.op("pool", lambda e: e.dma_start(out=ydbg.rearrange("(fc p) t -> p fc t", p=128), in_=yT[:]),
                     r=[("yT", i) for i in range(NT)], w=["ydbg"], dma=True)
            if not mixonly:
                phase_outproj(k, l, yT, x_src, x_dst)
            S.barrier()
            A.release(m0)
        if not mixonly:
            S.fence("pool", k.out_dmas[-64:] + S.dmas_since_barrier)
        S.emit(es)
    return nc


def host_consts(inputs):
    f = np.float32
    hc = {}
    s_ = np.arange(128)[:, None, None]
    dl = np.arange(5)[None, :, None]
    t_ = np.arange(128)[None, None, :]
    dist = 128 * dl + t_ - s_
    dq = 2 * dl + t_ // 64 - s_ // 64
    valid = (dq >= 0) & (dq <= 8)
    idx = np.clip(dist, -128, 128) + 128
    rb = np.asarray(inputs["d_rel_bias"], dtype=f)
    tab = rb[:, idx]
    tab = np.where(valid[None, ..., None], tab, f(NEGM)).transpose(0, 1, 2, 4, 3)
    tab = tab[:, :, :, [0, 2, 4, 6, 1, 3, 5, 7], :]
    hc["d_bias5"] = np.ascontiguousarray(tab, dtype=f)
    def t5_bucket_np(rel):
        nb, max_exact = 16, 8
        ret = np.where(rel > 0, nb, 0)
        n = np.abs(rel)
        nf = np.maximum(n, 1).astype(np.float32)
        large = max_exact + (np.log(nf / np.float32(max_exact)) / np.float32(np.log(128 / max_exact)) * np.float32(nb - max_exact)).astype(np.int32)
        large = np.minimum(large, nb - 1)
        return ret + np.where(n < max_exact, n, large)
    s2 = np.arange(128)[:, None, None]
    cl = np.arange(3)[None, :, None]
    t2 = np.arange(128)[None, None, :]
    rel = s2 - t2 - 128 * cl - np.where(cl == 2, 4096, 0)
    bk = t5_bucket_np(rel.astype(np.int64))
    t5 = np.asarray(inputs["t5_bias"], dtype=f)[bk]
    t5 = t5.transpose(0, 1, 3, 2)[:, :, [0, 2, 4, 6, 1, 3, 5, 7], :]
    hc["b_bias3"] = np.ascontiguousarray(t5, dtype=f)
    inv = (10000.0 ** (-np.arange(0, 64, 2, dtype=np.float32) / np.float32(64))).astype(f)
    ang = np.arange(SEQ, dtype=f)[:, None] * inv[None, :]
    cs = np.concatenate([np.cos(ang), np.sin(ang)], axis=1).astype(f)
    hc["rope_cs"] = np.ascontiguousarray(cs.reshape(NT, 128, 64).transpose(1, 0, 2))
    return hc


def host_inputs(inputs, b, hc=None):
    f = np.float32
    if hc is None:
        hc = host_consts(inputs)
    hi = {
        "x": np.ascontiguousarray(inputs["x"][b], dtype=f),
        "w_in": np.ascontiguousarray(inputs["w_in"], dtype=f),
        "w_out": np.ascontiguousarray(inputs["w_out"], dtype=f),
        "norm_g": np.ascontiguousarray(inputs["norm_g"], dtype=f),
        "ident": np.eye(128, dtype=f),
        "a_wsT": np.ascontiguousarray(np.asarray(inputs["a_ws"], dtype=f).transpose(0, 3, 1, 2)),
        "a_bsT": np.ascontiguousarray(np.asarray(inputs["a_bs"], dtype=f).transpose(0, 2, 1)),
        "a_v_gain": np.ascontiguousarray(inputs["a_v_gain"], dtype=f),
        "d_q_gain": np.ascontiguousarray(inputs["d_q_gain"], dtype=f),
        "b_q_gain": np.ascontiguousarray(inputs["b_q_gain"], dtype=f),
        "b_k_gain": np.ascontiguousarray(inputs["b_k_gain"], dtype=f),
        "c_w_qb": np.ascontiguousarray(inputs["c_w_qb"], dtype=f),
        "c_w_kvb": np.ascontiguousarray(inputs["c_w_kvb"], dtype=f),
        "c_qa_gain": np.ascontiguousarray(inputs["c_qa_gain"], dtype=f),
        "c_kva_gain": np.ascontiguousarray(inputs["c_kva_gain"], dtype=f),
        "c_q_gain": np.ascontiguousarray(inputs["c_q_gain"], dtype=f),
        "c_k_gain": np.ascontiguousarray(inputs["c_k_gain"], dtype=f),
        "d_k_gain": np.ascontiguousarray(inputs["d_k_gain"], dtype=f),
    }
    hi.update(hc)
    return hi


def kernel(**inputs):
    nc = build_program("full")
    n = 8
    hc = host_consts(inputs)
    in_maps = [host_inputs(inputs, b, hc) for b in range(n)]
    res = run_bass_kernel_spmd(nc, in_maps, core_ids=list(range(n)))
    return np.stack([np.asarray(r["out"]) for r in res.results], axis=0).astype(np.float32)
```

```python
import numpy as np
from contextlib import ExitStack

import concourse.bass as bass
import concourse.mybir as mybir
from concourse.bass_utils import run_bass_kernel_spmd

F32 = mybir.dt.float32
BF16 = mybir.dt.bfloat16
ALU = mybir.AluOpType
AF = mybir.ActivationFunctionType
AX = mybir.AxisListType

D_MODEL = 2048
SEQ = 2048
DEPTH = 2
NT = SEQ // 128
KC = D_MODEL // 128
IN_SIZES = (512, 512, 512, 512, 64, 64, 512, 64, 8, 512, 384, 128, 64, 512, 512, 512, 512, 512)
IN_COLS = sum(IN_SIZES)
OFF = np.concatenate([[0], np.cumsum(IN_SIZES)]).astype(int)
EPS = 1e-6
NEGM = -30000.0


class Op:
    __slots__ = ("eng", "fn", "dma", "deps", "ev", "prev", "signal")

    def __init__(self, eng, fn, dma):
        self.eng, self.fn, self.dma = eng, fn, dma
        self.deps = set()
        self.ev = None
        self.prev = None
        self.signal = False


class Sched:
    ENGS = ("sp", "act", "dve", "pool", "pe")
    NDS = 8

    def __init__(self, nc):
        self.nc = nc
        self.ops = {e: [] for e in self.ENGS}
        self.lastw = {}
        self.readers = {}
        self.dmas_since_barrier = []

    def op(self, eng, fn, r=(), w=(), dma=False):
        o = Op(eng, fn, dma)
        deps = {}

        def add(d, raw):
            if d is None or d is o:
                return
            deps[d] = deps.get(d, False) or raw

        for k in r:
            add(self.lastw.get(k), True)
        for k in w:
            add(self.lastw.get(k), False)
            for rd in self.readers.get(k, ()):
                add(rd, False)
        for d, raw in deps.items():
            if d.eng == eng and not d.dma and not dma:
                if eng == "pe":
                    continue
            o.deps.add(d)
            d.signal = True
        for k in r:
            self.readers.setdefault(k, []).append(o)
        for k in w:
            self.lastw[k] = o
            self.readers[k] = []
        self.ops[eng].append(o)
        if dma:
            self.dmas_since_barrier.append(o)
        return o

    def fence(self, eng, deps):
        o = Op(eng, None, False)
        for d in deps:
            if d is not None:
                o.deps.add(d)
                d.signal = True
        self.ops[eng].append(o)
        return o

    def barrier(self):
        last = [self.ops[e][-1] for e in self.ENGS if self.ops[e] and self.ops[e][-1].fn is not None]
        dm = list(self.dmas_since_barrier)
        self.dmas_since_barrier = []
        for e in self.ENGS:
            self.fence(e, [d for d in last + dm if not (d.eng == e and not d.dma and e == "pe")])

    def emit(self, es):
        nc = self.nc
        sem_eng = {e: es.enter_context(nc.semaphore("s_" + e)) for e in self.ENGS}
        sem_dma = {e: [es.enter_context(nc.semaphore("d_%s%d" % (e, k))) for k in range(self.NDS)]
                   for e in ("sp", "act", "pool")}
        for e in self.ENGS:
            cnt = 0
            dcnt = 0
            for o in self.ops[e]:
                if o.fn is None:
                    continue
                if o.dma:
                    k = dcnt % self.NDS
                    o.ev = (sem_dma[e][k], 16 * (dcnt // self.NDS + 1))
                    o.prev = (sem_dma[e][k], 16 * (dcnt // self.NDS))
                    dcnt += 1
                elif o.signal:
                    cnt += 1
                    o.ev = (sem_eng[e], cnt)
        block = es.enter_context(nc.Block())

        def run(e, eng):
            known = {}
            for o in self.ops[e]:
                waits = {}
                for d in o.deps:
                    s, v = d.ev
                    waits[s] = max(waits.get(s, 0), v)
                if o.dma and o.prev[1] > 0:
                    s, v = o.prev
                    waits[s] = max(waits.get(s, 0), v)
                for s, v in waits.items():
                    if known.get(s, 0) < v:
                        eng.wait_ge(s, v)
                        known[s] = v
                if o.fn is None:
                    continue
                inst = o.fn(eng)
                if o.dma:
                    inst.then_inc(o.ev[0], 16)
                elif o.signal:
                    inst.then_inc(o.ev[0], 1)

        block.sync(lambda eng: run("sp", eng))
        block.scalar(lambda eng: run("act", eng))
        block.vector(lambda eng: run("dve", eng))
        block.gpsimd(lambda eng: run("pool", eng))
        block.tensor(lambda eng: run("pe", eng))


class Arena:
    def __init__(self, nc, base=16640, top=229376 - 128):
        self.nc, self.off, self.top, self.n = nc, base, top, 0
        self.mode = "bottom"

    def alloc(self, name, shape, dtype):
        isz = 2 if dtype == BF16 else 4
        size = isz * int(np.prod(shape[1:]))
        size = (size + 63) // 64 * 64
        assert self.off + size <= self.top, ("SBUF overflow", name, self.off, self.top, size)
        self.n += 1
        if self.mode == "top":
            self.top -= size
            at = self.top
        else:
            at = self.off
            self.off += size
        return self.nc.alloc_sbuf_tensor_at("%s_%d" % (name, self.n), list(shape), dtype, offset=at)

    def mark(self):
        return self.top if self.mode == "top" else self.off

    def release(self, m):
        if self.mode == "top":
            self.top = m
        else:
            self.off = m


class K:
    pass


def ps_next(k, pool="all"):
    if pool == "all":
        i = k.ps_i % 4
        k.ps_i += 1
    elif pool == "lo":
        i = k.ps_lo % 2
        k.ps_lo += 1
    elif pool == "lo4":
        i = (0, 1, 4, 5)[k.ps_lo % 4]
        k.ps_lo += 1
    else:
        i = 2 + k.ps_hi % 2
        k.ps_hi += 1
    return k.psf[i], ("psf", i)


def pst_next(k):
    i = k.pst_i % 2
    k.pst_i += 1
    return k.pstt[i], ("pst", i)


def evac_eng(k):
    k.ev_i += 1
    return "act" if k.ev_i % 2 else "dve"


def copy_op(S, eng, out, in_, r, w):
    if eng == "act":
        return S.op("act", lambda e: e.activation(out=out, in_=in_, func=AF.Copy), r=r, w=w)
    return S.op(eng, lambda e: e.tensor_copy(out=out, in_=in_), r=r, w=w)


def transpose_to(k, blocks, dst, dst_key, eng=None, np_out=128, extra_r=()):
    S = k.S
    n = len(blocks)
    pt, pk = pst_next(k)
    for bi, (ap, key) in enumerate(blocks):
        S.op("pe", lambda e, ap=ap, bi=bi: e.transpose(out=pt[0:np_out, bi * 128:(bi + 1) * 128], in_=ap, identity=k.ident[:]),
             r=[key, "ident"] + list(extra_r), w=[pk])
    src = pt[0:np_out, 0:n * 128].rearrange("p (n t) -> p n t", n=n)
    copy_op(S, eng or evac_eng(k), dst, src, r=[pk], w=[dst_key])


def rstd_op(k, ss, n_feat, out, key_ss, key_out):
    S = k.S
    S.op("act", lambda e: e.activation(out=ss, in_=ss, func=AF.Ln, scale=1.0 / n_feat, bias=k.eps_t[:, 0:1]), r=[key_ss, "eps_t"], w=[key_ss])
    S.op("act", lambda e: e.activation(out=out, in_=ss, func=AF.Exp, scale=-0.5), r=[key_ss], w=[key_out])


def phase_norm(k, l, x_src, xnT):
    S, A = k.S, k.A
    A.mode = "top"
    m = A.mark()
    xt = [A.alloc("xt", [128, D_MODEL], F32) for _ in range(2)]
    xn = [A.alloc("xn", [128, D_MODEL], BF16) for _ in range(2)]
    junk = A.alloc("junk", [128, D_MODEL], BF16)
    ss = A.alloc("ss", [128, NT], F32)
    rs = A.alloc("rs", [128, NT], F32)
    gsb = A.alloc("gsb", [128, D_MODEL], F32)
    S.op("sp", lambda e: e.dma_start(out=gsb[:], in_=k.d["norm_g"][l].partition_broadcast(128)), w=["gsb"], dma=True)
    S.op("dve", lambda e: e.memset(ss[:], 0.0), w=[("ss", i) for i in range(NT)])
    for i in range(NT):
        b = i % 2
        S.op("sp", lambda e, i=i, b=b: e.dma_start(out=xt[b][:], in_=x_src[i * 128:(i + 1) * 128, :]), w=[("xt", b)], dma=True)
        S.op("act", lambda e, i=i, b=b: e.activation(out=junk[:], in_=xt[b][:], func=AF.Square, accum_out=ss[:, i:i + 1]),
             r=[("xt", b)], w=["junk", ("ss", i)])
        rstd_op(k, ss[:, i:i + 1], D_MODEL, rs[:, i:i + 1], ("ss", i), ("rs", i))
        S.op("dve", lambda e, i=i, b=b: e.scalar_tensor_tensor(out=xn[b][:], in0=xt[b][:], scalar=rs[:, i:i + 1], in1=gsb[:], op0=ALU.mult, op1=ALU.mult),
             r=[("xt", b), ("rs", i), "gsb"], w=[("xn", b)])
        for half in range(2):
            pt, pk = pst_next(k)
            for j in range(8):
                kc = half * 8 + j
                S.op("pe", lambda e, kc=kc, j=j, b=b, pt=pt: e.transpose(out=pt[:, j * 128:(j + 1) * 128], in_=xn[b][:, kc * 128:(kc + 1) * 128], identity=k.ident[:]),
                     r=[("xn", b), "ident"], w=[pk])
            src = pt[:, :].rearrange("p (n t) -> p n t", n=8)
            dst = xnT[:, half * 8:(half + 1) * 8, i * 128:(i + 1) * 128]
            copy_op(S, evac_eng(k), dst, src, r=[pk], w=[("xnT", i)])
    A.release(m)
    A.mode = "bottom"


def proj_bufs(A):
    return ([A.alloc("wst", [128, KC, 512], F32)], [A.alloc("wbf", [128, KC, 512], BF16) for _ in range(2)])


def phase_proj(k, lhsT, lhs_key, w_src, ncols, evac, bufs=None, order=None, hook=None, cast_engs=("dve", "act"), ps_pool="all"):
    S, A = k.S, k.A
    m = A.mark()
    wst, wbf = bufs if bufs is not None else proj_bufs(A)
    ncb = (ncols + 511) // 512
    order = list(order) if order is not None else list(range(ncb))
    wv = w_src.rearrange("(kc p) c -> p kc c", p=128)

    def load_w(pos):
        cb = order[pos]
        c0 = cb * 512
        cw = min(512, ncols - c0)
        b = pos % 2
        for q in range(4):
            S.op("sp", lambda e, q=q: e.dma_start(out=wst[0][:, q * 4:(q + 1) * 4, 0:cw], in_=wv[:, q * 4:(q + 1) * 4, c0:c0 + cw]),
                 w=[("wst", q)], dma=True)
            copy_op(S, cast_engs[q % 2], wbf[b][:, q * 4:(q + 1) * 4, 0:cw], wst[0][:, q * 4:(q + 1) * 4, 0:cw],
                    r=[("wst", q)], w=[("wbf", b, q)])

    load_w(0)
    for pos, cb in enumerate(order):
        c0 = cb * 512
        cw = min(512, ncols - c0)
        b = pos % 2
        if pos + 1 < len(order):
            load_w(pos + 1)
        for i in range(NT):
            ps, pk = ps_next(k, ps_pool)
            for kc in range(KC):
                S.op("pe", lambda e, kc=kc, i=i, b=b, cw=cw, ps=ps: e.matmul(ps[:, 0:cw], lhsT[:, kc, i * 128:(i + 1) * 128], wbf[b][:, kc, 0:cw],
                                                                          start=(kc == 0), stop=(kc == KC - 1)),
                     r=[lhs_key(i), ("wbf", b, kc // 4)], w=[pk])
            evac(i, c0, cw, ps, pk)
            if hook is not None:
                hook(pos, i)
    A.release(m)


def phase_inproj(k, l, xnT, bufs=None, hsb=None, order=None, hook=None, keep_dve_free=False):
    S, A = k.S, k.A
    m = A.mark()
    if hsb is None:
        hsb = [A.alloc("hsb", [128, 512], BF16) for _ in range(4)]
    cnt = [0]

    def evac(i, c0, cw, ps, pk):
        b = cnt[0] % 4
        cnt[0] += 1
        copy_op(S, "act" if keep_dve_free else evac_eng(k), hsb[b][:, 0:cw], ps[:, 0:cw], r=[pk], w=[("hsb", b)])
        S.op("pool", lambda e: e.dma_start(out=k.h_d[i * 128:(i + 1) * 128, c0:c0 + cw], in_=hsb[b][:, 0:cw]),
             r=[("hsb", b)], w=[("h_d", i, c0 // 512)], dma=True)

    phase_proj(k, xnT, lambda i: ("xnT", i), k.d["w_in"][l], IN_COLS, evac, bufs, order, hook,
               cast_engs=("act", "act") if keep_dve_free else ("dve", "act"), ps_pool="lo4" if keep_dve_free else "all")
    A.release(m)


def phase_outproj(k, l, yT, x_src, x_dst):
    S, A = k.S, k.A
    m = A.mark()
    xr = [A.alloc("xr", [128, 512], F32) for _ in range(3)]
    cnt = [0]

    def evac(i, c0, cw, ps, pk):
        b = cnt[0] % 3
        cnt[0] += 1
        S.op("sp", lambda e: e.dma_start(out=xr[b][:], in_=x_src[i * 128:(i + 1) * 128, c0:c0 + 512]), w=[("xr", b)], dma=True)
        S.op("dve", lambda e: e.tensor_tensor(out=xr[b][:], in0=ps[:, :], in1=xr[b][:], op=ALU.add), r=[pk, ("xr", b)], w=[("xr", b)])
        o = S.op("pool", lambda e: e.dma_start(out=x_dst[i * 128:(i + 1) * 128, c0:c0 + 512], in_=xr[b][:]),
                 r=[("xr", b)], w=[("xdst", i, c0 // 512)], dma=True)
        k.out_dmas.append(o)

    phase_proj(k, yT, lambda i: ("yT", i), k.d["w_out"][l], D_MODEL, evac)
    A.release(m)


def load_h(k, eng, dst, i, c0, cw, wkey):
    rk = [("h_d", i, cb) for cb in range(c0 // 512, (c0 + cw - 1) // 512 + 1)]
    return k.S.op(eng, lambda e: e.dma_start(out=dst, in_=k.h_d[i * 128:(i + 1) * 128, c0:c0 + cw]), r=rk, w=[wkey], dma=True)


def y_to_yT(k, ysb, ykey, yT, g, i):
    blocks = [(ysb[:, j * 128:(j + 1) * 128], ykey) for j in range(4)]
    transpose_to(k, blocks, yT[:, 4 * g:4 * g + 4, i * 128:(i + 1) * 128], ("yT", i))


def mixer_stub(k, l, yT, g):
    S, A = k.S, k.A
    m = A.mark()
    yb = [A.alloc("yb", [128, 512], BF16) for _ in range(2)]
    for i in range(NT):
        b = i % 2
        load_h(k, "sp", yb[b][:], i, 512 * g, 512, ("yb", b))
        y_to_yT(k, yb[b], ("yb", b), yT, g, i)
    A.release(m)


def mixer_a_gen(k, l, yT, ps_pool="all"):
    S, A, d = k.S, k.A, k.d
    m = A.mark()
    wsf = A.alloc("a_wsf", [128, 4, 128], F32)
    wsb = A.alloc("a_wsb", [128, 4, 128], BF16)
    bsb = A.alloc("a_bsb", [128, 4], F32)
    vg = A.alloc("a_vg", [128, 512], F32)
    ss = A.alloc("a_ss", [128, NT], F32)
    rs = A.alloc("a_rs", [128, NT], F32)
    hin = [A.alloc("a_hin", [128, 1536], BF16) for _ in range(2)]
    guv = [A.alloc("a_guv", [128, 1024], F32) for _ in range(2)]
    junk = A.alloc("a_junk", [128, 512], BF16)
    vn = [A.alloc("a_vn", [128, 512], BF16) for _ in range(2)]
    t1 = [A.alloc("a_t1", [128, 512], F32) for _ in range(2)]
    sz = [A.alloc("a_sz", [128, 512], F32) for _ in range(2)]
    ysb = [A.alloc("a_y", [128, 512], BF16) for _ in range(2)]
    S.op("sp", lambda e: e.dma_start(out=wsf[:], in_=d["a_wsT"][l]), w=["a_wsf"], dma=True)
    S.op("sp", lambda e: e.dma_start(out=bsb[:], in_=d["a_bsT"][l]), w=["a_bsb"], dma=True)
    S.op("sp", lambda e: e.dma_start(out=vg[:], in_=d["a_v_gain"][l].partition_broadcast(128)), w=["a_vg"], dma=True)
    S.op("dve", lambda e: e.tensor_copy(out=wsb[:], in_=wsf[:]), r=["a_wsf"], w=["a_wsb"])
    S.op("dve", lambda e: e.memset(wsb[64:128, :, 0:64], 0.0), w=["a_wsb"])
    S.op("dve", lambda e: e.memset(ss[:], 0.0), w=[("a_ss", i) for i in range(NT)])
    def sA(i):
        b = i % 2
        load_h(k, "sp", hin[b][:], i, 0, 1536, ("a_hin", b))
        S.op("act", lambda e, b=b: e.activation(out=guv[b][:], in_=hin[b][:, 0:1024], func=AF.Gelu_apprx_tanh),
             r=[("a_hin", b)], w=[("a_guv", b)])
        S.op("act", lambda e, b=b, i=i: e.activation(out=junk[:], in_=guv[b][:, 512:1024], func=AF.Square, accum_out=ss[:, i:i + 1]),
             r=[("a_guv", b)], w=["a_junk", ("a_ss", i)])
        rstd_op(k, ss[:, i:i + 1], 512, rs[:, i:i + 1], ("a_ss", i), ("a_rs", i))
        S.op("dve", lambda e, b=b, i=i: e.scalar_tensor_tensor(out=vn[b][:], in0=guv[b][:, 512:1024], scalar=rs[:, i:i + 1], in1=vg[:],
                                                             op0=ALU.mult, op1=ALU.mult),
             r=[("a_guv", b), ("a_rs", i), "a_vg"], w=[("a_vn", b)])

    def sB(i):
        b = i % 2
        ps, pk = ps_next(k, ps_pool)
        for g in range(4):
            S.op("pe", lambda e, g=g, b=b, ps=ps: e.matmul(ps[:, g * 128:(g + 1) * 128], wsb[:, g, :], vn[b][:, g * 128:(g + 1) * 128], start=True, stop=True),
                 r=["a_wsb", ("a_vn", b)], w=[pk])
        for g in range(4):
            S.op("dve", lambda e, g=g, b=b, ps=ps: e.scalar_tensor_tensor(out=t1[b][:, g * 128:(g + 1) * 128], in0=ps[:, g * 128:(g + 1) * 128],
                                                                       scalar=bsb[:, g:g + 1], in1=guv[b][:, g * 128:(g + 1) * 128],
                                                                       op0=ALU.add, op1=ALU.mult),
                 r=[pk, "a_bsb", ("a_guv", b)], w=[("a_t1", b)])
        S.op("act", lambda e, b=b: e.activation(out=sz[b][:], in_=hin[b][:, 1024:1536], func=AF.Silu), r=[("a_hin", b)], w=[("a_sz", b)])
        S.op("pool", lambda e, b=b: e.tensor_tensor(out=ysb[b][:], in0=sz[b][:], in1=t1[b][:], op=ALU.mult), r=[("a_t1", b), ("a_sz", b)], w=[("a_y", b)])

    def sC(i):
        b = i % 2
        y_to_yT(k, ysb[b], ("a_y", b), yT, 0, i)

    for n in range(NT + 2):
        if n < NT:
            sA(n)
        if 0 <= n - 1 < NT:
            sB(n - 1)
        if 0 <= n - 2 < NT:
            sC(n - 2)
        yield
    A.release(m)


def mixer_a(k, l, yT):
    for _ in mixer_a_gen(k, l, yT):
        pass


def head_norm(k, src, nh, hd, gain_full, out, ss, rs, tmp, rkeys, key_ss, key_rs, key_tmp, key_out, sq, gkey="gfull"):
    S = k.S
    S.op("act", lambda e: e.activation(out=sq, in_=src, func=AF.Square), r=rkeys, w=[key_tmp + ("sq",)])
    S.op("dve", lambda e: e.tensor_reduce(out=ss, in_=sq.rearrange("p (h d) -> p h d", d=hd), axis=AX.X, op=ALU.add),
         r=[key_tmp + ("sq",)], w=[key_ss])
    rstd_op(k, ss, hd, rs, key_ss, key_rs)
    S.op("dve", lambda e: e.tensor_tensor(out=tmp.rearrange("p (h d) -> p h d", d=hd), in0=src.rearrange("p (h d) -> p h d", d=hd),
                                          in1=rs.unsqueeze(2).to_broadcast([128, nh, hd]), op=ALU.mult),
         r=rkeys + [key_rs], w=[key_tmp])
    S.op("pool", lambda e: e.tensor_tensor(out=out, in0=tmp, in1=gain_full, op=ALU.mult), r=[key_tmp, gkey], w=[key_out])


def rep_gain(k, gfull, gain, nh, hd, gkey="gfull"):
    for h in range(nh):
        k.S.op("pool", lambda e, h=h: e.tensor_copy(out=gfull[:, h * hd:(h + 1) * hd], in_=gain), r=["gains"], w=[gkey])


def gate_and_store(k, l, yv, yv_key, c0z, i, g, yT, bufs, b):
    S = k.S
    zin, sz, ysb = bufs["zin"][b], bufs["sz"][b], bufs["ysb"][b]
    load_h(k, "sp", zin[:], i, c0z, 512, ("g_zin", b))
    S.op("act", lambda e: e.activation(out=sz[:], in_=zin[:], func=AF.Silu), r=[("g_zin", b)], w=[("g_sz", b)])
    S.op("pool", lambda e: e.tensor_tensor(out=ysb[:], in0=yv, in1=sz[:], op=ALU.mult), r=[yv_key, ("g_sz", b)], w=[("g_y", b)])
    y_to_yT(k, ysb, ("g_y", b), yT, g, i)


def gate_bufs(A):
    return {"zin": [A.alloc("g_zin", [128, 512], BF16) for _ in range(2)],
            "sz": [A.alloc("g_sz", [128, 512], F32) for _ in range(2)],
            "ysb": [A.alloc("g_y", [128, 512], BF16) for _ in range(2)]}


def run_pipelined(pairs, stage1, stage2):
    prev = None
    n = 0
    for p in pairs:
        if "pre" in p:
            stage1(p, 0)
            continue
        stage1(p, n % 2)
        if prev is not None:
            stage2(prev, (n - 1) % 2)
        prev = p
        n += 1
    if prev is not None:
        stage2(prev, (n - 1) % 2)


def finalize_heads(k, pfx, psO, pko, rden, yv_b, yv_key, nh_bank, hd, slot):
    S = k.S
    for hg in range(2):
        pv = psO[hg][:, :].rearrange("p (h c) -> p h c", c=slot)
        S.op("dve", lambda e, hg=hg, pv=pv: e.reciprocal(out=rden[:, hg * nh_bank:(hg + 1) * nh_bank], in_=pv[:, :, hd]), r=[pko[hg]], w=[(pfx + "_rden", hg)])
        S.op("dve", lambda e, hg=hg, pv=pv: e.tensor_tensor(out=yv_b[:, hg * nh_bank * hd:(hg + 1) * nh_bank * hd].rearrange("p (h c) -> p h c", c=hd), in0=pv[:, :, 0:hd],
                                                          in1=rden[:, hg * nh_bank:(hg + 1) * nh_bank].unsqueeze(2).to_broadcast([128, nh_bank, hd]), op=ALU.mult),
             r=[pko[hg], (pfx + "_rden", hg)], w=[yv_key])


def mixer_d(k, l, yT):
    S, A, d = k.S, k.A, k.d
    m = A.mark()
    c0 = int(OFF[14])
    qkT = A.alloc("d_qkT", [128, 8, SEQ], BF16)
    vaug = A.alloc("d_vaug", [128, NT, 8, 80], BF16)
    bias5 = A.alloc("d_bias5", [128, 5, 8, 128], F32)
    gains = A.alloc("d_gains", [128, 2, 64], F32)
    S.op("sp", lambda e: e.dma_start(out=bias5[:], in_=d["d_bias5"][l]), w=["d_bias5"], dma=True)
    S.op("sp", lambda e: e.dma_start(out=gains[:, 0, :], in_=d["d_q_gain"][l].partition_broadcast(128)), w=["gains"], dma=True)
    S.op("sp", lambda e: e.dma_start(out=gains[:, 1, :], in_=d["d_k_gain"][l].partition_broadcast(128)), w=["gains"], dma=True)
    S.op("dve", lambda e: e.tensor_scalar(out=gains[:, 0, :], in0=gains[:, 0, :], scalar1=0.125, scalar2=None, op0=ALU.mult), r=["gains"], w=["gains"])
    S.op("pool", lambda e: e.memset(vaug[:, :, :, 64:65], 1.0), w=[("d_v", j) for j in range(NT)])
    gfull = A.alloc("d_gfull", [128, 2, 512], F32)
    for qk in range(2):
        rep_gain(k, gfull[:, qk, :], gains[:, qk, :], 8, 64)
    m1 = A.mark()
    qkv = [A.alloc("d_qkv", [128, 1536], BF16) for _ in range(2)]
    sq = A.alloc("d_sq", [128, 1024], F32)
    tmp = A.alloc("d_tmp", [128, 1024], F32)
    qkn = [A.alloc("d_qkn", [128, 1024], BF16) for _ in range(2)]
    ss = A.alloc("d_ss", [128, NT, 16], F32)
    rs = A.alloc("d_rs", [128, NT, 16], F32)
    def s1a(i):
        b = i % 2
        load_h(k, "sp", qkv[b][:], i, c0, 1536, ("d_qkv", b))
        for qk in range(2):
            head_norm(k, qkv[b][:, qk * 512:(qk + 1) * 512], 8, 64, gfull[:, qk, :], qkn[b][:, qk * 512:(qk + 1) * 512],
                      ss[:, i, qk * 8:(qk + 1) * 8], rs[:, i, qk * 8:(qk + 1) * 8], tmp[:, qk * 512:(qk + 1) * 512],
                      [("d_qkv", b)], ("d_ss", i, qk), ("d_rs", i, qk), ("d_tmp", qk), ("d_qkn", b, qk), sq[:, qk * 512:(qk + 1) * 512])

    def s1b(i):
        b = i % 2
        blocks = [(qkn[b][:, c * 128:(c + 1) * 128], ("d_qkn", b, c // 4)) for c in range(8)]
        transpose_to(k, blocks, qkT[:, :, i * 128:(i + 1) * 128], ("d_qkT", i))
        S.op("pool", lambda e, i=i, b=b: e.tensor_copy(out=vaug[:, i, :, 0:64], in_=qkv[b][:, 1024:1536].rearrange("p (h d) -> p h d", d=64)),
             r=[("d_qkv", b)], w=[("d_v", i)])
    ein = [A.alloc("d_ein", [128, 1024], F32) for _ in range(2)]
    expT = [A.alloc("d_expT", [128, 8, 128], BF16) for _ in range(2)]
    rden = A.alloc("d_rden", [128, 8], F32)
    yv = [A.alloc("d_yv", [128, 512], F32) for _ in range(2)]
    gb = gate_bufs(A)
    psO = [k.psf[4], k.psf[5]]
    pko = [("psf", 4), ("psf", 5)]
    pairs = [dict(pre=(0, "a")), dict(pre=(0, "b"))]
    for i in range(NT):
        P = [dict(i=i, j=j, first=(j == max(0, i - 4)), last=(j == i)) for j in range(max(0, i - 4), i + 1)]
        n = len(P)
        if i + 1 < NT:
            P = [dict(pre=(i + 1, "a"))] + P[0:(n + 1) // 2] + [dict(pre=(i + 1, "b"))] + P[(n + 1) // 2:]
        pairs += P
    s1 = {"a": s1a, "b": s1b}

    def stage1(p, b2):
        if "pre" in p:
            s1[p["pre"][1]](p["pre"][0])
            return
        i, j = p["i"], p["j"]
        dl = i - j
        pss = [ps_next(k), ps_next(k)]
        for h in range(8):
            ps, pk = pss[h % 2]
            hp = h % 2
            S.op("pe", lambda e, h=h, hp=hp, ps=ps: e.matmul(ps[:, (h // 2) * 128:(h // 2 + 1) * 128],
                                                           qkT[hp * 64:(hp + 1) * 64, 4 + h // 2, j * 128:(j + 1) * 128],
                                                           qkT[hp * 64:(hp + 1) * 64, h // 2, i * 128:(i + 1) * 128], start=True, stop=True),
                 r=[("d_qkT", i), ("d_qkT", j)], w=[pk])
        for hg in range(2):
            ps, pk = pss[hg]
            S.op("dve", lambda e, hg=hg, ps=ps: e.tensor_tensor(
                out=ein[b2][:, hg * 512:(hg + 1) * 512], in0=ps[:, :],
                in1=bias5[:, dl, hg * 4:(hg + 1) * 4, :].rearrange("p h t -> p (h t)"), op=ALU.add),
                r=[pk, "d_bias5"], w=[("d_ein", b2, hg)])
            S.op("act", lambda e, hg=hg: e.activation(out=expT[b2][:, hg * 4:(hg + 1) * 4, :].rearrange("p h t -> p (h t)"),
                                                    in_=ein[b2][:, hg * 512:(hg + 1) * 512], func=AF.Exp),
                 r=[("d_ein", b2, hg)], w=[("d_expT", b2, hg)])

    def stage2(p, b2):
        if "pre" in p:
            return
        i, j = p["i"], p["j"]
        for h in range(8):
            po = psO[h // 4]
            S.op("pe", lambda e, h=h, po=po: e.matmul(po[:, (h % 4) * 128:(h % 4) * 128 + 65], expT[b2][:, (h % 2) * 4 + h // 2, :], vaug[:, j, h, 0:65],
                                                    start=(p["first"] and h % 4 == 0), stop=(p["last"] and h % 4 == 3), skip_group_check=True),
                 r=[("d_expT", b2, h % 2), ("d_v", j)], w=[pko[h // 4]])
        if p["last"]:
            b = i % 2
            finalize_heads(k, "d", psO, pko, rden, yv[b], ("d_yv", b), 4, 64, 128)
            gate_and_store(k, l, yv[b][:], ("d_yv", b), c0 + 1536, i, 3, yT, gb, b)

    run_pipelined(pairs, stage1, stage2)
    A.release(m)


def bcast_load(k, dst, src_vec, key):
    return k.S.op("sp", lambda e: e.dma_start(out=dst, in_=src_vec.partition_broadcast(128)), w=[key], dma=True)


def mixer_c(k, l, yT):
    S, A, d = k.S, k.A, k.d
    m = A.mark()
    c0 = int(OFF[10])
    QKn = A.alloc("c_QKn", [128, 8, SEQ], BF16)
    QKr = A.alloc("c_QKr", [128, 4, SEQ], BF16)
    vaug = A.alloc("c_vaug", [128, NT, 4, 144], BF16)
    S.op("pool", lambda e: e.memset(vaug[:, :, :, 128:129], 1.0), w=[("c_v", j) for j in range(NT)])
    m1 = A.mark()
    wstg = A.alloc("c_wstg", [128, 1024], F32)
    wqb = A.alloc("c_wqb", [128, 3, 768], BF16)
    wkb = A.alloc("c_wkb", [128, 1024], BF16)
    g_qa = A.alloc("c_gqa", [128, 384], F32)
    g_kva = A.alloc("c_gkva", [128, 128], F32)
    g_q = A.alloc("c_gq", [128, 192], F32)
    g_k = A.alloc("c_gk", [128, 192], F32)
    cs = A.alloc("c_cs", [128, NT, 64], F32)
    S.op("sp", lambda e: e.dma_start(out=cs[:], in_=d["rope_cs"]), w=["c_cs"], dma=True)
    bcast_load(k, g_qa[:], d["c_qa_gain"][l], "gains")
    bcast_load(k, g_kva[:], d["c_kva_gain"][l], "gains")
    bcast_load(k, g_q[:], d["c_q_gain"][l], "gains")
    bcast_load(k, g_k[:], d["c_k_gain"][l], "gains")
    gq_full = A.alloc("c_gqf", [128, 768], F32)
    gk_full = A.alloc("c_gkf", [128, 768], F32)
    rep_gain(k, gq_full[:], g_q[:], 4, 192)
    rep_gain(k, gk_full[:], g_k[:], 4, 192)
    for kc in range(3):
        S.op("sp", lambda e, kc=kc: e.dma_start(out=wstg[:, 0:768], in_=d["c_w_qb"][l][kc * 128:(kc + 1) * 128, :]), w=["c_wstg"], dma=True)
        S.op("dve", lambda e, kc=kc: e.tensor_copy(out=wqb[:, kc, :], in_=wstg[:, 0:768]), r=["c_wstg"], w=["c_wqb"])
    S.op("sp", lambda e: e.dma_start(out=wstg[:, :], in_=d["c_w_kvb"][l]), w=["c_wstg"], dma=True)
    S.op("dve", lambda e: e.tensor_copy(out=wkb[:], in_=wstg[:, :]), r=["c_wstg"], w=["c_wkb"])
    hin = [A.alloc("c_hin", [128, 576], BF16) for _ in range(2)]
    sq = A.alloc("c_sq", [128, 768], F32)
    tmp = A.alloc("c_tmp", [128, 768], F32)
    lat = [A.alloc("c_lat", [128, 512], BF16) for _ in range(2)]
    latT = [A.alloc("c_latT", [128, 4, 128], BF16) for _ in range(2)]
    qkf = A.alloc("c_qkf", [128, 8, 192], F32)
    qkn = A.alloc("c_qkn", [128, 8, 192], F32)
    qkb = [A.alloc("c_qkb", [128, 8, 128], BF16) for _ in range(2)]
    qkr = [A.alloc("c_qkr", [128, 8, 64], BF16) for _ in range(2)]
    rt = [A.alloc("c_rt", [128, 8, 32], F32) for _ in range(4)]
    ss = A.alloc("c_ss", [128, NT, 16], F32)
    rs = A.alloc("c_rs", [128, NT, 16], F32)
    def s1a(i):
        b = i % 2
        load_h(k, "sp", hin[b][:], i, c0, 576, ("c_hin", b))
        head_norm(k, hin[b][:, 0:384], 1, 384, g_qa[:], lat[b][:, 0:384], ss[:, i, 0:1], rs[:, i, 0:1], tmp[:, 0:384],
                  [("c_hin", b)], ("c_ss", i, 0), ("c_rs", i, 0), ("c_tmp", 2), ("c_lat", b, 0), sq[:, 0:384], gkey="gains")
        head_norm(k, hin[b][:, 384:512], 1, 128, g_kva[:], lat[b][:, 384:512], ss[:, i, 1:2], rs[:, i, 1:2], tmp[:, 384:512],
                  [("c_hin", b)], ("c_ss", i, 1), ("c_rs", i, 1), ("c_tmp", 2), ("c_lat", b, 1), sq[:, 384:512], gkey="gains")

    def s1b(i):
        b = i % 2
        blocks = [(lat[b][:, c * 128:(c + 1) * 128], ("c_lat", b, 0 if c < 3 else 1)) for c in range(4)]
        transpose_to(k, blocks, latT[b][:, :, :], ("c_latT", b))
        pq = [ps_next(k), ps_next(k)]
        for nh, (n0, n1) in enumerate(((0, 512), (512, 768))):
            ps, pk = pq[nh]
            for kc in range(3):
                S.op("pe", lambda e, ps=ps, kc=kc, n0=n0, n1=n1, b=b: e.matmul(ps[:, 0:n1 - n0], latT[b][:, kc, :], wqb[:, kc, n0:n1], start=(kc == 0), stop=(kc == 2)),
                     r=[("c_latT", b), "c_wqb"], w=[pk])
        pkv = [ps_next(k), ps_next(k)]
        for nh in range(2):
            ps, pk = pkv[nh]
            S.op("pe", lambda e, ps=ps, nh=nh, b=b: e.matmul(ps[:, :], latT[b][:, 3, :], wkb[:, nh * 512:(nh + 1) * 512], start=True, stop=True),
                 r=[("c_latT", b), "c_wkb"], w=[pk])
        qflat = qkf[:, 0:4, :].rearrange("p h c -> p (h c)")
        copy_op(S, "act", qflat[:, 0:512], pq[0][0][:, :], r=[pq[0][1]], w=[("c_qkf", 0)])
        copy_op(S, "act", qflat[:, 512:768], pq[1][0][:, 0:256], r=[pq[1][1]], w=[("c_qkf", 0)])
        for nh in range(2):
            pv = pkv[nh][0][:, :].rearrange("p (h c) -> p h c", c=256)
            copy_op(S, "dve", qkf[:, 4 + 2 * nh:6 + 2 * nh, 0:128], pv[:, :, 0:128], r=[pkv[nh][1]], w=[("c_qkf", 1)])
            copy_op(S, "dve", vaug[:, i, 2 * nh:2 * nh + 2, 0:128], pv[:, :, 128:256], r=[pkv[nh][1]], w=[("c_v", i)])
        for h in range(4):
            S.op("pool", lambda e, b=b, h=h: e.tensor_copy(out=qkf[:, 4 + h, 128:192], in_=hin[b][:, 512:576]),
                 r=[("c_hin", b)], w=[("c_qkf", 1)])
        for qk, gain in ((0, gq_full), (1, gk_full)):
            head_norm(k, qkf[:, 4 * qk:4 * qk + 4, :].rearrange("p h c -> p (h c)"), 4, 192, gain[:],
                      qkn[:, 4 * qk:4 * qk + 4, :].rearrange("p h c -> p (h c)"), ss[:, i, 4 + 4 * qk:8 + 4 * qk], rs[:, i, 4 + 4 * qk:8 + 4 * qk],
                      tmp[:, :], [("c_qkf", qk)], ("c_ss", i, 2 + qk), ("c_rs", i, 2 + qk), ("c_tmp", 2), ("c_qkn", qk), sq[:, :])
        rq = [("c_qkn", 0), ("c_qkn", 1)]
        S.op("pool", lambda e, b=b: e.tensor_copy(out=qkb[b][:, :, :], in_=qkn[:, :, 0:128]), r=rq, w=[("c_qkb", b)])
        x1, x2 = qkn[:, :, 128:160], qkn[:, :, 160:192]
        cc = cs[:, i, 0:32].unsqueeze(1).to_broadcast([128, 8, 32])
        sn = cs[:, i, 32:64].unsqueeze(1).to_broadcast([128, 8, 32])
        for ti, (xa, tb) in enumerate(((x1, cc), (x2, sn), (x1, sn), (x2, cc))):
            S.op("dve", lambda e, ti=ti, xa=xa, tb=tb: e.tensor_tensor(out=rt[ti][:], in0=xa, in1=tb, op=ALU.mult), r=rq + ["c_cs"], w=[("c_rt", ti)])
        S.op("pool", lambda e, b=b: e.tensor_tensor(out=qkr[b][:, :, 0:32], in0=rt[0][:], in1=rt[1][:], op=ALU.subtract),
             r=[("c_rt", 0), ("c_rt", 1)], w=[("c_qkr", b)])
        S.op("pool", lambda e, b=b: e.tensor_tensor(out=qkr[b][:, :, 32:64], in0=rt[2][:], in1=rt[3][:], op=ALU.add),
             r=[("c_rt", 2), ("c_rt", 3)], w=[("c_qkr", b)])

    def s1c(i):
        b = i % 2
        blocks = [(qkb[b][:, c, :], ("c_qkb", b)) for c in range(8)]
        transpose_to(k, blocks, QKn[:, :, i * 128:(i + 1) * 128], ("c_QKn", i))
        blocks = [(qkr[b][:, 2 * c:2 * c + 2, :].rearrange("p h c -> p (h c)"), ("c_qkr", b)) for c in range(4)]
        transpose_to(k, blocks, QKr[:, :, i * 128:(i + 1) * 128], ("c_QKr", i))
    expT = [A.alloc("c_expT", [128, 4, 128], BF16) for _ in range(2)]
    rden = A.alloc("c_rden", [128, 4], F32)
    yv = [A.alloc("c_yv", [128, 512], F32) for _ in range(2)]
    gb = gate_bufs(A)
    psO = [k.psf[4], k.psf[5]]
    pko = [("psf", 4), ("psf", 5)]
    scale = float(192 ** -0.5)
    pairs = [dict(pre=(0, "a")), dict(pre=(0, "b")), dict(pre=(0, "c"))]
    for i in range(NT):
        P = [dict(i=i, j=j, first=(j == 0), last=(j == i)) for j in range(i + 1)]
        n = len(P)
        if i + 1 < NT:
            P = [dict(pre=(i + 1, "a"))] + P[0:n // 3] + [dict(pre=(i + 1, "b"))] + P[n // 3:2 * n // 3] + [dict(pre=(i + 1, "c"))] + P[2 * n // 3:]
        pairs += P
    s1 = {"a": s1a, "b": s1b, "c": s1c}

    def stage1(p, b2):
        if "pre" in p:
            s1[p["pre"][1]](p["pre"][0])
            return
        i, j = p["i"], p["j"]
        pss = [ps_next(k), ps_next(k)]
        for h in (0, 2, 1, 3):
            hp = h % 2
            ps, pk = pss[hp]
            c_ = (h // 2) * 128
            S.op("pe", lambda e, h=h, ps=ps, c_=c_: e.matmul(ps[:, c_:c_ + 128], QKn[:, 4 + h, j * 128:(j + 1) * 128], QKn[:, h, i * 128:(i + 1) * 128],
                                                           start=True, stop=False),
                 r=[("c_QKn", i), ("c_QKn", j)], w=[pk])
            S.op("pe", lambda e, h=h, hp=hp, ps=ps, c_=c_: e.matmul(ps[:, c_:c_ + 128], QKr[hp * 64:(hp + 1) * 64, 2 + h // 2, j * 128:(j + 1) * 128],
                                                                  QKr[hp * 64:(hp + 1) * 64, h // 2, i * 128:(i + 1) * 128], start=False, stop=True),
                 r=[("c_QKr", i), ("c_QKr", j)], w=[pk])
        for hp in range(2):
            ps, pk = pss[hp]
            S.op("act", lambda e, ps=ps, hp=hp: e.activation(out=expT[b2][:, 2 * hp:2 * hp + 2, :].rearrange("p h t -> p (h t)"), in_=ps[:, 0:256], func=AF.Exp, scale=scale),
                 r=[pk], w=[("c_expT", b2)])
        if j == i:
            S.op("pool", lambda e: e.memset(expT[b2][64:128, :, 0:64], 0.0), r=[("c_expT", b2)], w=[("c_expT", b2)])

    def stage2(p, b2):
        if "pre" in p:
            return
        i, j = p["i"], p["j"]
        for h in range(4):
            po = psO[h // 2]
            S.op("pe", lambda e, h=h, po=po: e.matmul(po[:, (h % 2) * 256:(h % 2) * 256 + 129], expT[b2][:, (h % 2) * 2 + h // 2, :], vaug[:, j, h, 0:129],
                                                    start=(j == 0 and h % 2 == 0), stop=(j == i and h % 2 == 1), skip_group_check=True),
                 r=[("c_expT", b2), ("c_v", j)], w=[pko[h // 2]])
        if p["last"]:
            b = i % 2
            finalize_heads(k, "c", psO, pko, rden, yv[b], ("c_yv", b), 2, 128, 256)
            gate_and_store(k, l, yv[b][:], ("c_yv", b), c0 + 576, i, 2, yT, gb, b)

    run_pipelined(pairs, stage1, stage2)
    A.release(m)


def b_pre_gen(k, l):
    S, A, d = k.S, k.A, k.d
    A.mode = "top"
    m = A.mark()
    c0 = int(OFF[3])
    IQT = A.alloc("b_IQT", [128, 5, SEQ], BF16)
    absw = A.alloc("b_absw", [128, NT, 8], F32)
    sgnw = A.alloc("b_sgnw", [128, NT, 8], F32)
    hin = [A.alloc("b_hiq", [128, 584], BF16) for _ in range(2)]
    ikk = [A.alloc("b_ikk", [128, 128], BF16) for _ in range(2)]
    NBIS = 18
    score2 = [A.alloc("b_score", [128, SEQ], F32) for _ in range(2)]
    bs2 = [A.alloc("b_bs", [128, 8], F32) for _ in range(2)]
    W2 = [A.alloc("b_W", [128, NBIS + 1], F32) for _ in range(2)]
    pow2 = A.alloc("b_pow2", [128, NBIS + 1], F32)
    mb = [A.alloc("b_mb", [128, SEQ], BF16) for _ in range(2)]
    tmpr = [A.alloc("b_tmpr", [128, 512], F32) for _ in range(4)]
    tr = [0]
    A.mode = "bottom"
    for kk in range(NBIS + 1):
        S.op("pool", lambda e, kk=kk: e.memset(pow2[:, kk:kk + 1], float(2.0 ** -(kk + 1))), w=["b_pow2"])

    def s1iq(i):
        b = i % 2
        load_h(k, "sp", hin[b][:], i, c0 + 640, 584, ("b_hiq", b))
        for hh in range(2):
            S.op("pool", lambda e, hh=hh: e.tensor_copy(out=ikk[b][:, hh * 64:(hh + 1) * 64], in_=hin[b][:, 512:576]), r=[("b_hiq", b)], w=[("b_ikk", b)])
        S.op("act", lambda e: e.activation(out=absw[:, i, :], in_=hin[b][:, 576:584], func=AF.Abs), r=[("b_hiq", b)], w=[("b_absw", i)])
        S.op("act", lambda e: e.activation(out=sgnw[:, i, :], in_=hin[b][:, 576:584], func=AF.Sign), r=[("b_hiq", b)], w=[("b_sgnw", i)])
        blocks = [(hin[b][:, c * 128:(c + 1) * 128], ("b_hiq", b)) for c in range(4)] + [(ikk[b][:, :], ("b_ikk", b))]
        transpose_to(k, blocks, IQT[:, :, i * 128:(i + 1) * 128], ("b_IQT", i))

    def idx_part(i):
        p2 = i % 2
        score, bs, W = score2[p2], bs2[p2], W2[p2]
        n_i = 128 * (i + 1)
        nch = (n_i + 511) // 512
        for c_ in range(nch):
            cw = min(512, n_i - 512 * c_)
            for h in range(8):
                hp = h % 2
                ps, pk = ps_next(k, k.b_pre_pool)
                S.op("pe", lambda e, ps=ps, h=h, hp=hp, c_=c_, cw=cw: e.matmul(ps[:, 0:cw], IQT[hp * 64:(hp + 1) * 64, h // 2, i * 128:(i + 1) * 128],
                                                                           IQT[hp * 64:(hp + 1) * 64, 4, c_ * 512:c_ * 512 + cw], start=True, stop=True),
                     r=[("b_IQT", i)] + [("b_IQT", jj) for jj in range(4 * c_, min(4 * c_ + 4, i + 1))], w=[pk])
                b3 = tr[0] % 4
                tr[0] += 1
                S.op("act", lambda e, ps=ps, b3=b3, cw=cw, h=h: e.activation(out=tmpr[b3][:, 0:cw], in_=ps[:, 0:cw], func=AF.Relu, scale=absw[:, i, h:h + 1]),
                     r=[pk, ("b_absw", i)], w=[("b_tmpr", b3)])
                sc = score[:, c_ * 512:c_ * 512 + cw]
                if h == 0:
                    S.op("dve", lambda e, sc=sc, b3=b3, cw=cw: e.tensor_scalar(out=sc, in0=tmpr[b3][:, 0:cw], scalar1=sgnw[:, i, 0:1], scalar2=None, op0=ALU.mult),
                         r=[("b_tmpr", b3), ("b_sgnw", i)], w=[("b_score", p2, c_)])
                else:
                    S.op("dve", lambda e, sc=sc, b3=b3, cw=cw, h=h: e.scalar_tensor_tensor(out=sc, in0=tmpr[b3][:, 0:cw], scalar=sgnw[:, i, h:h + 1], in1=sc,
                                                                                       op0=ALU.mult, op1=ALU.add),
                         r=[("b_tmpr", b3), ("b_sgnw", i), ("b_score", p2, c_)], w=[("b_score", p2, c_)])
                yield
        allsc = [("b_score", p2, c_) for c_ in range(nch)]
        bk = ("b_bs", p2)
        lo, hi, w0, mid = (bs[:, c_:c_ + 1] for c_ in range(4))
        if i >= 2:
            S.op("dve", lambda e: e.tensor_reduce(out=lo, in_=score[:, 0:n_i], axis=AX.X, op=ALU.min), r=allsc, w=[bk])
        S.op("dve", lambda e: e.memset(score[0:64, n_i - 64:n_i], -1e30), r=allsc, w=allsc)
        if i >= 2:
            S.op("dve", lambda e: e.tensor_reduce(out=hi, in_=score[:, 0:n_i], axis=AX.X, op=ALU.max), r=allsc, w=[bk])
            S.op("dve", lambda e: e.tensor_tensor(out=w0, in0=hi, in1=lo, op=ALU.subtract), r=[bk], w=[bk])
            S.op("dve", lambda e: e.tensor_scalar(out=W[:, :], in0=pow2[:, :], scalar1=w0, scalar2=None, op0=ALU.mult), r=[bk, "b_pow2"], w=[("b_W", p2)])
            S.op("dve", lambda e: e.tensor_tensor(out=mid, in0=lo, in1=W[:, 0:1], op=ALU.add), r=[bk, ("b_W", p2)], w=[bk])
        else:
            S.op("dve", lambda e: e.memset(lo, -1e29), w=[bk])

    def bis_step(i, kk):
        if i < 2:
            return
        p2 = i % 2
        score, bs, W = score2[p2], bs2[p2], W2[p2]
        n_i = 128 * (i + 1)
        allsc = [("b_score", p2, c_) for c_ in range((n_i + 511) // 512)]
        bk = ("b_bs", p2)
        mid, cnt, gw = bs[:, 3:4], bs[:, 4:5], bs[:, 5:6]
        S.op("dve", lambda e: e.tensor_scalar(out=mb[p2][:, 0:n_i], in0=score[:, 0:n_i], scalar1=mid, scalar2=0.0, op0=ALU.is_ge, op1=ALU.add, accum_out=cnt),
             r=allsc + [bk], w=[bk, ("b_mb", p2)])
        S.op("dve", lambda e: e.tensor_scalar(out=gw, in0=cnt, scalar1=256.0, scalar2=-0.5, op0=ALU.is_ge, op1=ALU.add), r=[bk], w=[bk])
        S.op("dve", lambda e: e.scalar_tensor_tensor(out=mid, in0=gw, scalar=W[:, kk:kk + 1], in1=mid, op0=ALU.mult, op1=ALU.add),
             r=[bk, ("b_W", p2)], w=[bk])

    def mask_part(i):
        p2 = i % 2
        score, bs, W = score2[p2], bs2[p2], W2[p2]
        n_i = 128 * (i + 1)
        allsc = [("b_score", p2, c_) for c_ in range((n_i + 511) // 512)]
        bk = ("b_bs", p2)
        lo, mid = bs[:, 0:1], bs[:, 3:4]
        if i >= 2:
            S.op("dve", lambda e: e.tensor_tensor(out=lo, in0=mid, in1=W[:, NBIS:NBIS + 1], op=ALU.subtract), r=[bk, ("b_W", p2)], w=[bk])
        mbb = mb[p2]
        S.op("dve", lambda e: e.tensor_scalar(out=mbb[:, 0:n_i], in0=score[:, 0:n_i], scalar1=lo, scalar2=NEGM, op0=ALU.is_lt, op1=ALU.mult),
             r=allsc + [bk], w=[("b_mb", p2)])
        S.op("pool", lambda e: e.dma_start(out=k.mask_d[i * 128:(i + 1) * 128, 0:n_i], in_=mbb[:, 0:n_i]), r=[("b_mb", p2)], w=[("mask_d", i)], dma=True)

    for t in range(min(4, NT)):
        s1iq(t)
        yield
    for _ in idx_part(0):
        yield
    for t in range(NT):
        if t + 4 < NT:
            s1iq(t + 4)
            yield
        gi = idx_part(t + 1) if t + 1 < NT else iter(())
        steps = list(range(NBIS)) if t >= 2 else []
        n_idx = 8 * ((128 * (t + 2) + 511) // 512) if t + 1 < NT else 0
        per = max(1, -(-n_idx // max(1, len(steps)))) if steps else n_idx
        for kk in steps:
            bis_step(t, kk)
            yield
            for _ in range(per):
                if next(gi, "end") != "end":
                    yield
        for _ in gi:
            yield
        mask_part(t)
        yield
    A.mode = "top"
    A.release(m)
    A.mode = "bottom"


def mixer_b(k, l, yT):
    S, A, d = k.S, k.A, k.d
    m = A.mark()
    c0 = int(OFF[3])
    QT = A.alloc("b_QT", [128, 5, SEQ], BF16)
    vaug = A.alloc("b_vaug", [128, NT, 80], BF16)
    bias3 = A.alloc("b_bias3", [128, 3, 8, 128], F32)
    gains = A.alloc("b_gains", [128, 2, 64], F32)
    S.op("sp", lambda e: e.dma_start(out=bias3[:], in_=d["b_bias3"]), w=["b_bias3"], dma=True)
    S.op("sp", lambda e: e.dma_start(out=gains[:, 0, :], in_=d["b_q_gain"][l].partition_broadcast(128)), w=["gains"], dma=True)
    S.op("sp", lambda e: e.dma_start(out=gains[:, 1, :], in_=d["b_k_gain"][l].partition_broadcast(128)), w=["gains"], dma=True)
    S.op("dve", lambda e: e.tensor_scalar(out=gains[:, 0, :], in0=gains[:, 0, :], scalar1=0.125, scalar2=None, op0=ALU.mult), r=["gains"], w=["gains"])
    for c in range(2):
        S.op("dve", lambda e, c=c: e.tensor_tensor(out=bias3[:, c, :, :], in0=bias3[:, c, :, :], in1=bias3[:, 2, :, :], op=ALU.subtract),
             r=["b_bias3"], w=["b_bias3"])
    S.op("pool", lambda e: e.memset(vaug[:, :, 64:65], 1.0), w=[("b_v", j) for j in range(NT)])
    gqf = A.alloc("b_gqf", [128, 512], F32)
    rep_gain(k, gqf[:], gains[:, 0, :], 8, 64)
    hin = [A.alloc("b_hin", [128, 640], BF16) for _ in range(2)]
    sq = A.alloc("b_sq", [128, 576], F32)
    tmp = A.alloc("b_tmp", [128, 576], F32)
    qn = [A.alloc("b_qn", [128, 640], BF16) for _ in range(2)]
    ss = A.alloc("b_ss", [128, NT, 16], F32)
    rs = A.alloc("b_rs", [128, NT, 16], F32)

    def s1a(i):
        b = i % 2
        load_h(k, "sp", hin[b][:], i, c0, 640, ("b_hin", b))
        head_norm(k, hin[b][:, 0:512], 8, 64, gqf[:], qn[b][:, 0:512], ss[:, i, 0:8], rs[:, i, 0:8], tmp[:, 0:512],
                  [("b_hin", b)], ("b_ss", i, 0), ("b_rs", i, 0), ("b_tmp", 0), ("b_qn", b, 0), sq[:, 0:512])
        head_norm(k, hin[b][:, 512:576], 1, 64, gains[:, 1, :], qn[b][:, 512:576], ss[:, i, 8:9], rs[:, i, 8:9], tmp[:, 512:576],
                  [("b_hin", b)], ("b_ss", i, 1), ("b_rs", i, 1), ("b_tmp", 1), ("b_qn", b, 1), sq[:, 512:576], gkey="gains")
        S.op("pool", lambda e, b=b: e.tensor_copy(out=qn[b][:, 576:640], in_=qn[b][:, 512:576]), r=[("b_qn", b, 1)], w=[("b_qn", b, 2)])
        S.op("pool", lambda e, b=b, i=i: e.tensor_copy(out=vaug[:, i, 0:64], in_=hin[b][:, 576:640]), r=[("b_hin", b)], w=[("b_v", i)])

    def s1b(i):
        b = i % 2
        blocks = [(qn[b][:, c * 128:(c + 1) * 128], ("b_qn", b, 0)) for c in range(4)] + [(qn[b][:, 512:640], ("b_qn", b, 2))]
        transpose_to(k, blocks, QT[:, :, i * 128:(i + 1) * 128], ("b_QT", i), extra_r=[("b_qn", b, 1)])

    mbl = [A.alloc("b_mbl", [128, SEQ], BF16) for _ in range(2)]
    maskT = [A.alloc("b_maskT", [128, NT, 128], BF16) for _ in range(2)]
    ein = [A.alloc("b_ein", [128, 1024], F32) for _ in range(1)]
    expT = [A.alloc("b_expT", [128, 8, 128], BF16) for _ in range(2)]
    rden = A.alloc("b_rden", [128, 8], F32)
    yv = [A.alloc("b_yv", [128, 512], F32) for _ in range(2)]
    gb = gate_bufs(A)
    psO = [k.psf[4], k.psf[5]]
    pko = [("psf", 4), ("psf", 5)]

    def mload(i):
        p2 = i % 2
        n_i = 128 * (i + 1)
        S.op("sp", lambda e: e.dma_start(out=mbl[p2][:, 0:n_i], in_=k.mask_d[i * 128:(i + 1) * 128, 0:n_i]), r=[("mask_d", i)], w=[("b_mbl", p2)], dma=True)
        for g0 in range(0, i + 1, 8):
            nb_ = min(8, i + 1 - g0)
            blocks = [(mbl[p2][:, (g0 + bi) * 128:(g0 + bi + 1) * 128], ("b_mbl", p2)) for bi in range(nb_)]
            transpose_to(k, blocks, maskT[p2][:, g0:g0 + nb_, :], ("b_maskT", p2, g0 // 8))

    pairs = []
    for t in range(min(2, NT)):
        pairs += [dict(pre=("a", t)), dict(pre=("b", t))]
    pairs += [dict(pre=("mask", 0))]
    for i in range(NT):
        near = [dict(i=i, j=j) for j in (i, i - 1) if j >= 0]
        far = [dict(i=i, j=j) for j in range(0, i - 1)]
        real = near + far
        for p in real:
            p["first"] = p is real[0]
            p["last"] = p is real[-1]
        tile = list(near)
        if i + 1 < NT:
            tile.append(dict(pre=("mask", i + 1)))
        if i + 2 < NT:
            tile.append(dict(pre=("a", i + 2)))
        tile += far
        if i + 2 < NT:
            tile.append(dict(pre=("b", i + 2)))
        pairs += tile
    s1 = {"a": s1a, "b": s1b, "mask": mload}

    def stage1(p, b2):
        if "pre" in p:
            s1[p["pre"][0]](p["pre"][1])
            return
        i, j = p["i"], p["j"]
        dl = min(i - j, 2)
        mT = maskT[i % 2]
        pss = [ps_next(k), ps_next(k)]
        for hp in range(2):
            ps, pk = pss[hp]
            S.op("pe", lambda e, hp=hp, ps=ps: e.matmul(ps[:, :], QT[hp * 64:(hp + 1) * 64, 4, j * 128:(j + 1) * 128],
                                                      QT[hp * 64:(hp + 1) * 64, 0:4, i * 128:(i + 1) * 128], start=True, stop=False,
                                                      skip_group_check=True),
                 r=[("b_QT", i), ("b_QT", j)], w=[pk])
        for hg in range(2):
            ps, pk = pss[hg]
            S.op("pe", lambda e, ps=ps: e.matmul(ps[:, :], k.ident[:], mT[:, j, :].unsqueeze(1).to_broadcast([128, 4, 128]), start=False, stop=True, skip_group_check=True),
                 r=[("b_maskT", i % 2, j // 8), "ident"], w=[pk])
        for hg in range(2):
            ps, pk = pss[hg]
            if dl < 2:
                S.op("dve", lambda e, hg=hg, ps=ps: e.tensor_tensor(
                    out=ein[0][:, hg * 512:(hg + 1) * 512], in0=ps[:, :],
                    in1=bias3[:, dl, hg * 4:(hg + 1) * 4, :].rearrange("p h t -> p (h t)"), op=ALU.add),
                    r=[pk, "b_bias3"], w=[("b_ein", 0, hg)])
                S.op("act", lambda e, hg=hg: e.activation(out=expT[b2][:, hg * 4:(hg + 1) * 4, :].rearrange("p h t -> p (h t)"),
                                                        in_=ein[0][:, hg * 512:(hg + 1) * 512], func=AF.Exp),
                     r=[("b_ein", 0, hg)], w=[("b_expT", b2, hg)])
            else:
                S.op("act", lambda e, hg=hg, ps=ps: e.activation(out=expT[b2][:, hg * 4:(hg + 1) * 4, :].rearrange("p h t -> p (h t)"),
                                                               in_=ps[:, :], func=AF.Exp),
                     r=[pk], w=[("b_expT", b2, hg)])

    def stage2(p, b2):
        if "pre" in p:
            return
        i, j = p["i"], p["j"]
        for h in range(8):
            po = psO[h // 4]
            S.op("pe", lambda e, h=h, po=po: e.matmul(po[:, (h % 4) * 128:(h % 4) * 128 + 65], expT[b2][:, (h % 2) * 4 + h // 2, :], vaug[:, j, 0:65],
                                                    start=(p["first"] and h % 4 == 0), stop=(p["last"] and h % 4 == 3), skip_group_check=True),
                 r=[("b_expT", b2, h % 2), ("b_v", j)], w=[pko[h // 4]])
        if p["last"]:
            b = i % 2
            finalize_heads(k, "b", psO, pko, rden, yv[b], ("b_yv", b), 4, 64, 128)
            gate_and_store(k, l, yv[b][:], ("b_yv", b), c0 + 1224, i, 1, yT, gb, b)

    run_pipelined(pairs, stage1, stage2)
    A.release(m)


def build_program(mode="full", nlayers=DEPTH, mixers="ABCD"):
    nc = bass.Bass("TRN2", target_bir_lowering=False)
    k = K()
    k.nc = nc
    k.S = Sched(nc)
    k.A = Arena(nc)
    k.ps_i = k.pst_i = k.ev_i = k.ps_lo = k.ps_hi = 0
    k.out_dmas = []
    d = {}

    def inp(name, shape, dt=F32):
        d[name] = nc.dram_tensor(name, list(shape), dt, kind="ExternalInput").ap()

    inp("x", [SEQ, D_MODEL])
    inp("w_in", [DEPTH, D_MODEL, IN_COLS])
    inp("w_out", [DEPTH, D_MODEL, D_MODEL])
    inp("norm_g", [DEPTH, D_MODEL])
    inp("ident", [128, 128])
    inp("a_wsT", [DEPTH, 128, 4, 128])
    inp("a_bsT", [DEPTH, 128, 4])
    inp("a_v_gain", [DEPTH, 512])
    inp("b_bias3", [128, 3, 8, 128])
    inp("b_q_gain", [DEPTH, 64])
    inp("b_k_gain", [DEPTH, 64])
    inp("c_w_qb", [DEPTH, 384, 768])
    inp("c_w_kvb", [DEPTH, 128, 1024])
    inp("c_qa_gain", [DEPTH, 384])
    inp("c_kva_gain", [DEPTH, 128])
    inp("c_q_gain", [DEPTH, 192])
    inp("c_k_gain", [DEPTH, 192])
    inp("rope_cs", [128, NT, 64])
    inp("d_bias5", [DEPTH, 128, 5, 8, 128])
    inp("d_q_gain", [DEPTH, 64])
    inp("d_k_gain", [DEPTH, 64])
    k.d = d
    dbg = mode != "full"
    mixonly = mode == "mixonly"
    out_d = nc.dram_tensor("out", [SEQ, D_MODEL], F32, kind="ExternalOutput").ap()
    k.h_d = nc.dram_tensor("h_scr", [SEQ, IN_COLS], BF16, kind="ExternalInput" if mixonly else ("ExternalOutput" if dbg else "Internal")).ap()
    xs_d = nc.dram_tensor("x_scr", [SEQ, D_MODEL], F32).ap()
    k.mask_d = nc.dram_tensor("mask_scr", [SEQ, SEQ], BF16).ap()
    if dbg:
        ydbg = nc.dram_tensor("ydbg", [D_MODEL, SEQ], BF16, kind="ExternalOutput").ap()

    with ExitStack() as es:
        S, A = k.S, k.A
        k.psf = [es.enter_context(nc.psum_tensor("psf%d" % i, [128, 512], F32)) for i in range(6)]
        k.pstt = [es.enter_context(nc.psum_tensor("pst%d" % i, [128, 1024], BF16)) for i in range(2)]
        idf = A.alloc("idf", [128, 128], F32)
        k.ident = A.alloc("ident", [128, 128], BF16)
        S.op("sp", lambda e: e.dma_start(out=idf[:], in_=d["ident"]), w=["idf"], dma=True)
        S.op("dve", lambda e: e.tensor_copy(out=k.ident[:], in_=idf[:]), r=["idf"], w=["ident"])
        k.eps_t = A.alloc("eps_t", [128, 1], F32)
        S.op("dve", lambda e: e.memset(k.eps_t[:], EPS), w=["eps_t"])

        for l in range(nlayers):
            x_src = d["x"] if l == 0 else xs_d
            x_dst = out_d if l == nlayers - 1 else xs_d
            m0 = A.mark()
            use_b = "B" in mixers
            k.b_pre_pool = "all" if mixonly else "hi"
            if not mixonly:
                xnT = A.alloc("xnT", [128, KC, SEQ], BF16)
                pb = proj_bufs(A)
                hsb = [A.alloc("hsb", [128, 512], BF16) for _ in range(4)]
                phase_norm(k, l, x_src, xnT)
                norm_last = [S.ops[e][-1] for e in S.ENGS if S.ops[e] and S.ops[e][-1].fn is not None]
                gen = [None]
                ncb = (IN_COLS + 511) // 512
                order = [4, 5] + [cb for cb in range(ncb) if cb not in (4, 5)]

                n_y = 4 + 8 + sum((1 if t + 4 < NT else 0) + (18 if t >= 2 else 0) + (8 * ((128 * (t + 2) + 511) // 512) if t + 1 < NT else 0) + 1 for t in range(NT))
                quota = -(-n_y // ((ncb - 2) * NT - 8))

                def hook(pos, i):
                    if not use_b or pos < 1 or (pos == 1 and i < NT - 1):
                        return
                    if gen[0] is None:
                        for e in S.ENGS:
                            S.fence(e, norm_last)
                        gen[0] = b_pre_gen(k, l)
                    for _ in range(quota):
                        if next(gen[0], "end") == "end":
                            break

                phase_inproj(k, l, xnT, pb, hsb, order, hook, keep_dve_free=use_b)
                S.barrier()
            elif use_b:
                gen = [b_pre_gen(k, l)]
            A.release(m0)
            yT = A.alloc("yT", [128, KC, SEQ], BF16)
            if mixonly:
                S.op("pool", lambda e: e.memset(yT[:], 0.0), w=[("yT", i) for i in range(NT)])
            done_a = False
            if use_b:
                if "A" in mixers:
                    ga = mixer_a_gen(k, l, yT, ps_pool="lo" if not mixonly else "all")
                    for _ in ga:
                        for _ in range(8):
                            if next(gen[0], "end") == "end":
                                break
                    done_a = True
                for _ in gen[0]:
                    pass
                S.barrier()
            for g, (nm, fn) in enumerate((("A", mixer_a), ("B", mixer_b), ("C", mixer_c), ("D", mixer_d))):
                if nm == "A" and done_a:
                    continue
                if nm in mixers and fn is not None:
                    fn(k, l, yT)
                elif mixonly:
                    continue
                else:
                    mixer_stub(k, l, yT, g)
                S.barrier()
            if dbg and l == nlayers - 1:
                S.op("pool", lambda e: e.dma_start(out=ydbg.rearrange("(fc p) t -> p fc t", p=128), in_=yT[:]),
                     r=[("yT", i) for i in range(NT)], w=["ydbg"], dma=True)
            if not mixonly:
                phase_outproj(k, l, yT, x_src, x_dst)
            S.barrier()
            A.release(m0)
        if not mixonly:
            S.fence("pool", k.out_dmas[-64:] + S.dmas_since_barrier)
        S.emit(es)
    return nc


def host_consts(inputs):
    f = np.float32
    hc = {}
    s_ = np.arange(128)[:, None, None]
    dl = np.arange(5)[None, :, None]
    t_ = np.arange(128)[None, None, :]
    dist = 128 * dl + t_ - s_
    dq = 2 * dl + t_ // 64 - s_ // 64
    valid = (dq >= 0) & (dq <= 8)
    idx = np.clip(dist, -128, 128) + 128
    rb = np.asarray(inputs["d_rel_bias"], dtype=f)
    tab = rb[:, idx]
    tab = np.where(valid[None, ..., None], tab, f(NEGM)).transpose(0, 1, 2, 4, 3)
    tab = tab[:, :, :, [0, 2, 4, 6, 1, 3, 5, 7], :]
    hc["d_bias5"] = np.ascontiguousarray(tab, dtype=f)
    def t5_bucket_np(rel):
        nb, max_exact = 16, 8
        ret = np.where(rel > 0, nb, 0)
        n = np.abs(rel)
        nf = np.maximum(n, 1).astype(np.float32)
        large = max_exact + (np.log(nf / np.float32(max_exact)) / np.float32(np.log(128 / max_exact)) * np.float32(nb - max_exact)).astype(np.int32)
        large = np.minimum(large, nb - 1)
        return ret + np.where(n < max_exact, n, large)
    s2 = np.arange(128)[:, None, None]
    cl = np.arange(3)[None, :, None]
    t2 = np.arange(128)[None, None, :]
    rel = s2 - t2 - 128 * cl - np.where(cl == 2, 4096, 0)
    bk = t5_bucket_np(rel.astype(np.int64))
    t5 = np.asarray(inputs["t5_bias"], dtype=f)[bk]
    t5 = t5.transpose(0, 1, 3, 2)[:, :, [0, 2, 4, 6, 1, 3, 5, 7], :]
    hc["b_bias3"] = np.ascontiguousarray(t5, dtype=f)
    inv = (10000.0 ** (-np.arange(0, 64, 2, dtype=np.float32) / np.float32(64))).astype(f)
    ang = np.arange(SEQ, dtype=f)[:, None] * inv[None, :]
    cs = np.concatenate([np.cos(ang), np.sin(ang)], axis=1).astype(f)
    hc["rope_cs"] = np.ascontiguousarray(cs.reshape(NT, 128, 64).transpose(1, 0, 2))
    return hc


def host_inputs(inputs, b, hc=None):
    f = np.float32
    if hc is None:
        hc = host_consts(inputs)
    hi = {
        "x": np.ascontiguousarray(inputs["x"][b], dtype=f),
        "w_in": np.ascontiguousarray(inputs["w_in"], dtype=f),
        "w_out": np.ascontiguousarray(inputs["w_out"], dtype=f),
        "norm_g": np.ascontiguousarray(inputs["norm_g"], dtype=f),
        "ident": np.eye(128, dtype=f),
        "a_wsT": np.ascontiguousarray(np.asarray(inputs["a_ws"], dtype=f).transpose(0, 3, 1, 2)),
        "a_bsT": np.ascontiguousarray(np.asarray(inputs["a_bs"], dtype=f).transpose(0, 2, 1)),
        "a_v_gain": np.ascontiguousarray(inputs["a_v_gain"], dtype=f),
        "d_q_gain": np.ascontiguousarray(inputs["d_q_gain"], dtype=f),
        "b_q_gain": np.ascontiguousarray(inputs["b_q_gain"], dtype=f),
        "b_k_gain": np.ascontiguousarray(inputs["b_k_gain"], dtype=f),
        "c_w_qb": np.ascontiguousarray(inputs["c_w_qb"], dtype=f),
        "c_w_kvb": np.ascontiguousarray(inputs["c_w_kvb"], dtype=f),
        "c_qa_gain": np.ascontiguousarray(inputs["c_qa_gain"], dtype=f),
        "c_kva_gain": np.ascontiguousarray(inputs["c_kva_gain"], dtype=f),
        "c_q_gain": np.ascontiguousarray(inputs["c_q_gain"], dtype=f),
        "c_k_gain": np.ascontiguousarray(inputs["c_k_gain"], dtype=f),
        "d_k_gain": np.ascontiguousarray(inputs["d_k_gain"], dtype=f),
    }
    hi.update(hc)
    return hi


def kernel(**inputs):
    nc = build_program("full")
    n = 8
    hc = host_consts(inputs)
    in_maps = [host_inputs(inputs, b, hc) for b in range(n)]
    res = run_bass_kernel_spmd(nc, in_maps, core_ids=list(range(n)))
    return np.stack([np.asarray(r["out"]) for r in res.results], axis=0).astype(np.float32)
```

```python
import numpy as np
from contextlib import ExitStack

import concourse.bass as bass
import concourse.mybir as mybir
from concourse.bass_utils import run_bass_kernel_spmd

F32 = mybir.dt.float32
BF16 = mybir.dt.bfloat16
ALU = mybir.AluOpType
AF = mybir.ActivationFunctionType
AX = mybir.AxisListType

D_MODEL = 2048
SEQ = 2048
DEPTH = 2
NT = SEQ // 128
KC = D_MODEL // 128
IN_SIZES = (512, 512, 512, 512, 64, 64, 512, 64, 8, 512, 384, 128, 64, 512, 512, 512, 512, 512)
IN_COLS = sum(IN_SIZES)
OFF = np.concatenate([[0], np.cumsum(IN_SIZES)]).astype(int)
EPS = 1e-6
NEGM = -30000.0


class Op:
    __slots__ = ("eng", "fn", "dma", "deps", "ev", "prev", "signal")

    def __init__(self, eng, fn, dma):
        self.eng, self.fn, self.dma = eng, fn, dma
        self.deps = set()
        self.ev = None
        self.prev = None
        self.signal = False


class Sched:
    ENGS = ("sp", "act", "dve", "pool", "pe")
    NDS = 8

    def __init__(self, nc):
        self.nc = nc
        self.ops = {e: [] for e in self.ENGS}
        self.lastw = {}
        self.readers = {}
        self.dmas_since_barrier = []

    def op(self, eng, fn, r=(), w=(), dma=False):
        o = Op(eng, fn, dma)
        deps = {}

        def add(d, raw):
            if d is None or d is o:
                return
            deps[d] = deps.get(d, False) or raw

        for k in r:
            add(self.lastw.get(k), True)
        for k in w:
            add(self.lastw.get(k), False)
            for rd in self.readers.get(k, ()):
                add(rd, False)
        for d, raw in deps.items():
            if d.eng == eng and not d.dma and not dma:
                if eng == "pe":
                    continue
            o.deps.add(d)
            d.signal = True
        for k in r:
            self.readers.setdefault(k, []).append(o)
        for k in w:
            self.lastw[k] = o
            self.readers[k] = []
        self.ops[eng].append(o)
        if dma:
            self.dmas_since_barrier.append(o)
        return o

    def fence(self, eng, deps):
        o = Op(eng, None, False)
        for d in deps:
            if d is not None:
                o.deps.add(d)
                d.signal = True
        self.ops[eng].append(o)
        return o

    def barrier(self):
        last = [self.ops[e][-1] for e in self.ENGS if self.ops[e] and self.ops[e][-1].fn is not None]
        dm = list(self.dmas_since_barrier)
        self.dmas_since_barrier = []
        for e in self.ENGS:
            self.fence(e, [d for d in last + dm if not (d.eng == e and not d.dma and e == "pe")])

    def emit(self, es):
        nc = self.nc
        sem_eng = {e: es.enter_context(nc.semaphore("s_" + e)) for e in self.ENGS}
        sem_dma = {e: [es.enter_context(nc.semaphore("d_%s%d" % (e, k))) for k in range(self.NDS)]
                   for e in ("sp", "act", "pool")}
        for e in self.ENGS:
            cnt = 0
            dcnt = 0
            for o in self.ops[e]:
                if o.fn is None:
                    continue
                if o.dma:
                    k = dcnt % self.NDS
                    o.ev = (sem_dma[e][k], 16 * (dcnt // self.NDS + 1))
                    o.prev = (sem_dma[e][k], 16 * (dcnt // self.NDS))
                    dcnt += 1
                elif o.signal:
                    cnt += 1
                    o.ev = (sem_eng[e], cnt)
        block = es.enter_context(nc.Block())

        def run(e, eng):
            known = {}
            for o in self.ops[e]:
                waits = {}
                for d in o.deps:
                    s, v = d.ev
                    waits[s] = max(waits.get(s, 0), v)
                if o.dma and o.prev[1] > 0:
                    s, v = o.prev
                    waits[s] = max(waits.get(s, 0), v)
                for s, v in waits.items():
                    if known.get(s, 0) < v:
                        eng.wait_ge(s, v)
                        known[s] = v
                if o.fn is None:
                    continue
                inst = o.fn(eng)
                if o.dma:
                    inst.then_inc(o.ev[0], 16)
                elif o.signal:
                    inst.then_inc(o.ev[0], 1)

        block.sync(lambda eng: run("sp", eng))
        block.scalar(lambda eng: run("act", eng))
        block.vector(lambda eng: run("dve", eng))
        block.gpsimd(lambda eng: run("pool", eng))
        block.tensor(lambda eng: run("pe", eng))


class Arena:
    def __init__(self, nc, base=16640, top=229376 - 128):
        self.nc, self.off, self.top, self.n = nc, base, top, 0
        self.mode = "bottom"

    def alloc(self, name, shape, dtype):
        isz = 2 if dtype == BF16 else 4
        size = isz * int(np.prod(shape[1:]))
        size = (size + 63) // 64 * 64
        assert self.off + size <= self.top, ("SBUF overflow", name, self.off, self.top, size)
        self.n += 1
        if self.mode == "top":
            self.top -= size
            at = self.top
        else:
            at = self.off
            self.off += size
        return self.nc.alloc_sbuf_tensor_at("%s_%d" % (name, self.n), list(shape), dtype, offset=at)

    def mark(self):
        return self.top if self.mode == "top" else self.off

    def release(self, m):
        if self.mode == "top":
            self.top = m
        else:
            self.off = m


class K:
    pass


def ps_next(k, pool="all"):
    if pool == "all":
        i = k.ps_i % 4
        k.ps_i += 1
    elif pool == "lo":
        i = k.ps_lo % 2
        k.ps_lo += 1
    elif pool == "lo4":
        i = (0, 1, 4, 5)[k.ps_lo % 4]
        k.ps_lo += 1
    else:
        i = 2 + k.ps_hi % 2
        k.ps_hi += 1
    return k.psf[i], ("psf", i)


def pst_next(k):
    i = k.pst_i % 2
    k.pst_i += 1
    return k.pstt[i], ("pst", i)


def evac_eng(k):
    k.ev_i += 1
    return "act" if k.ev_i % 2 else "dve"


def copy_op(S, eng, out, in_, r, w):
    if eng == "act":
        return S.op("act", lambda e: e.activation(out=out, in_=in_, func=AF.Copy), r=r, w=w)
    return S.op(eng, lambda e: e.tensor_copy(out=out, in_=in_), r=r, w=w)


def transpose_to(k, blocks, dst, dst_key, eng=None, np_out=128, extra_r=()):
    S = k.S
    n = len(blocks)
    pt, pk = pst_next(k)
    for bi, (ap, key) in enumerate(blocks):
        S.op("pe", lambda e, ap=ap, bi=bi: e.transpose(out=pt[0:np_out, bi * 128:(bi + 1) * 128], in_=ap, identity=k.ident[:]),
             r=[key, "ident"] + list(extra_r), w=[pk])
    src = pt[0:np_out, 0:n * 128].rearrange("p (n t) -> p n t", n=n)
    copy_op(S, eng or evac_eng(k), dst, src, r=[pk], w=[dst_key])


def rstd_op(k, ss, n_feat, out, key_ss, key_out):
    S = k.S
    S.op("act", lambda e: e.activation(out=ss, in_=ss, func=AF.Ln, scale=1.0 / n_feat, bias=k.eps_t[:, 0:1]), r=[key_ss, "eps_t"], w=[key_ss])
    S.op("act", lambda e: e.activation(out=out, in_=ss, func=AF.Exp, scale=-0.5), r=[key_ss], w=[key_out])


def phase_norm(k, l, x_src, xnT):
    S, A = k.S, k.A
    A.mode = "top"
    m = A.mark()
    xt = [A.alloc("xt", [128, D_MODEL], F32) for _ in range(2)]
    xn = [A.alloc("xn", [128, D_MODEL], BF16) for _ in range(2)]
    junk = A.alloc("junk", [128, D_MODEL], BF16)
    ss = A.alloc("ss", [128, NT], F32)
    rs = A.alloc("rs", [128, NT], F32)
    gsb = A.alloc("gsb", [128, D_MODEL], F32)
    S.op("sp", lambda e: e.dma_start(out=gsb[:], in_=k.d["norm_g"][l].partition_broadcast(128)), w=["gsb"], dma=True)
    S.op("dve", lambda e: e.memset(ss[:], 0.0), w=[("ss", i) for i in range(NT)])
    for i in range(NT):
        b = i % 2
        S.op("sp", lambda e, i=i, b=b: e.dma_start(out=xt[b][:], in_=x_src[i * 128:(i + 1) * 128, :]), w=[("xt", b)], dma=True)
        S.op("act", lambda e, i=i, b=b: e.activation(out=junk[:], in_=xt[b][:], func=AF.Square, accum_out=ss[:, i:i + 1]),
             r=[("xt", b)], w=["junk", ("ss", i)])
        rstd_op(k, ss[:, i:i + 1], D_MODEL, rs[:, i:i + 1], ("ss", i), ("rs", i))
        S.op("dve", lambda e, i=i, b=b: e.scalar_tensor_tensor(out=xn[b][:], in0=xt[b][:], scalar=rs[:, i:i + 1], in1=gsb[:], op0=ALU.mult, op1=ALU.mult),
             r=[("xt", b), ("rs", i), "gsb"], w=[("xn", b)])
        for half in range(2):
            pt, pk = pst_next(k)
            for j in range(8):
                kc = half * 8 + j
                S.op("pe", lambda e, kc=kc, j=j, b=b, pt=pt: e.transpose(out=pt[:, j * 128:(j + 1) * 128], in_=xn[b][:, kc * 128:(kc + 1) * 128], identity=k.ident[:]),
                     r=[("xn", b), "ident"], w=[pk])
            src = pt[:, :].rearrange("p (n t) -> p n t", n=8)
            dst = xnT[:, half * 8:(half + 1) * 8, i * 128:(i + 1) * 128]
            copy_op(S, evac_eng(k), dst, src, r=[pk], w=[("xnT", i)])
    A.release(m)
    A.mode = "bottom"


def proj_bufs(A):
    return ([A.alloc("wst", [128, KC, 512], F32)], [A.alloc("wbf", [128, KC, 512], BF16) for _ in range(2)])


def phase_proj(k, lhsT, lhs_key, w_src, ncols, evac, bufs=None, order=None, hook=None, cast_engs=("dve", "act"), ps_pool="all"):
    S, A = k.S, k.A
    m = A.mark()
    wst, wbf = bufs if bufs is not None else proj_bufs(A)
    ncb = (ncols + 511) // 512
    order = list(order) if order is not None else list(range(ncb))
    wv = w_src.rearrange("(kc p) c -> p kc c", p=128)

    def load_w(pos):
        cb = order[pos]
        c0 = cb * 512
        cw = min(512, ncols - c0)
        b = pos % 2
        for q in range(4):
            S.op("sp", lambda e, q=q: e.dma_start(out=wst[0][:, q * 4:(q + 1) * 4, 0:cw], in_=wv[:, q * 4:(q + 1) * 4, c0:c0 + cw]),
                 w=[("wst", q)], dma=True)
            copy_op(S, cast_engs[q % 2], wbf[b][:, q * 4:(q + 1) * 4, 0:cw], wst[0][:, q * 4:(q + 1) * 4, 0:cw],
                    r=[("wst", q)], w=[("wbf", b, q)])

    load_w(0)
    for pos, cb in enumerate(order):
        c0 = cb * 512
        cw = min(512, ncols - c0)
        b = pos % 2
        if pos + 1 < len(order):
            load_w(pos + 1)
        for i in range(NT):
            ps, pk = ps_next(k, ps_pool)
            for kc in range(KC):
                S.op("pe", lambda e, kc=kc, i=i, b=b, cw=cw, ps=ps: e.matmul(ps[:, 0:cw], lhsT[:, kc, i * 128:(i + 1) * 128], wbf[b][:, kc, 0:cw],
                                                                          start=(kc == 0), stop=(kc == KC - 1)),
                     r=[lhs_key(i), ("wbf", b, kc // 4)], w=[pk])
            evac(i, c0, cw, ps, pk)
            if hook is not None:
                hook(pos, i)
    A.release(m)


def phase_inproj(k, l, xnT, bufs=None, hsb=None, order=None, hook=None, keep_dve_free=False):
    S, A = k.S, k.A
    m = A.mark()
    if hsb is None:
        hsb = [A.alloc("hsb", [128, 512], BF16) for _ in range(4)]
    cnt = [0]

    def evac(i, c0, cw, ps, pk):
        b = cnt[0] % 4
        cnt[0] += 1
        copy_op(S, "act" if keep_dve_free else evac_eng(k), hsb[b][:, 0:cw], ps[:, 0:cw], r=[pk], w=[("hsb", b)])
        S.op("pool", lambda e: e.dma_start(out=k.h_d[i * 128:(i + 1) * 128, c0:c0 + cw], in_=hsb[b][:, 0:cw]),
             r=[("hsb", b)], w=[("h_d", i, c0 // 512)], dma=True)

    phase_proj(k, xnT, lambda i: ("xnT", i), k.d["w_in"][l], IN_COLS, evac, bufs, order, hook,
               cast_engs=("act", "act") if keep_dve_free else ("dve", "act"), ps_pool="lo4" if keep_dve_free else "all")
    A.release(m)


def phase_outproj(k, l, yT, x_src, x_dst):
    S, A = k.S, k.A
    m = A.mark()
    xr = [A.alloc("xr", [128, 512], F32) for _ in range(3)]
    cnt = [0]

    def evac(i, c0, cw, ps, pk):
        b = cnt[0] % 3
        cnt[0] += 1
        S.op("sp", lambda e: e.dma_start(out=xr[b][:], in_=x_src[i * 128:(i + 1) * 128, c0:c0 + 512]), w=[("xr", b)], dma=True)
        S.op("dve", lambda e: e.tensor_tensor(out=xr[b][:], in0=ps[:, :], in1=xr[b][:], op=ALU.add), r=[pk, ("xr", b)], w=[("xr", b)])
        o = S.op("pool", lambda e: e.dma_start(out=x_dst[i * 128:(i + 1) * 128, c0:c0 + 512], in_=xr[b][:]),
                 r=[("xr", b)], w=[("xdst", i, c0 // 512)], dma=True)
        k.out_dmas.append(o)

    phase_proj(k, yT, lambda i: ("yT", i), k.d["w_out"][l], D_MODEL, evac)
    A.release(m)


def load_h(k, eng, dst, i, c0, cw, wkey):
    rk = [("h_d", i, cb) for cb in range(c0 // 512, (c0 + cw - 1) // 512 + 1)]
    return k.S.op(eng, lambda e: e.dma_start(out=dst, in_=k.h_d[i * 128:(i + 1) * 128, c0:c0 + cw]), r=rk, w=[wkey], dma=True)


def y_to_yT(k, ysb, ykey, yT, g, i):
    blocks = [(ysb[:, j * 128:(j + 1) * 128], ykey) for j in range(4)]
    transpose_to(k, blocks, yT[:, 4 * g:4 * g + 4, i * 128:(i + 1) * 128], ("yT", i))


def mixer_stub(k, l, yT, g):
    S, A = k.S, k.A
    m = A.mark()
    yb = [A.alloc("yb", [128, 512], BF16) for _ in range(2)]
    for i in range(NT):
        b = i % 2
        load_h(k, "sp", yb[b][:], i, 512 * g, 512, ("yb", b))
        y_to_yT(k, yb[b], ("yb", b), yT, g, i)
    A.release(m)


def mixer_a_gen(k, l, yT, ps_pool="all"):
    S, A, d = k.S, k.A, k.d
    m = A.mark()
    wsf = A.alloc("a_wsf", [128, 4, 128], F32)
    wsb = A.alloc("a_wsb", [128, 4, 128], BF16)
    bsb = A.alloc("a_bsb", [128, 4], F32)
    vg = A.alloc("a_vg", [128, 512], F32)
    ss = A.alloc("a_ss", [128, NT], F32)
    rs = A.alloc("a_rs", [128, NT], F32)
    hin = [A.alloc("a_hin", [128, 1536], BF16) for _ in range(2)]
    guv = [A.alloc("a_guv", [128, 1024], F32) for _ in range(2)]
    junk = A.alloc("a_junk", [128, 512], BF16)
    vn = [A.alloc("a_vn", [128, 512], BF16) for _ in range(2)]
    t1 = [A.alloc("a_t1", [128, 512], F32) for _ in range(2)]
    sz = [A.alloc("a_sz", [128, 512], F32) for _ in range(2)]
    ysb = [A.alloc("a_y", [128, 512], BF16) for _ in range(2)]
    S.op("sp", lambda e: e.dma_start(out=wsf[:], in_=d["a_wsT"][l]), w=["a_wsf"], dma=True)
    S.op("sp", lambda e: e.dma_start(out=bsb[:], in_=d["a_bsT"][l]), w=["a_bsb"], dma=True)
    S.op("sp", lambda e: e.dma_start(out=vg[:], in_=d["a_v_gain"][l].partition_broadcast(128)), w=["a_vg"], dma=True)
    S.op("dve", lambda e: e.tensor_copy(out=wsb[:], in_=wsf[:]), r=["a_wsf"], w=["a_wsb"])
    S.op("dve", lambda e: e.memset(wsb[64:128, :, 0:64], 0.0), w=["a_wsb"])
    S.op("dve", lambda e: e.memset(ss[:], 0.0), w=[("a_ss", i) for i in range(NT)])
    def sA(i):
        b = i % 2
        load_h(k, "sp", hin[b][:], i, 0, 1536, ("a_hin", b))
        S.op("act", lambda e, b=b: e.activation(out=guv[b][:], in_=hin[b][:, 0:1024], func=AF.Gelu_apprx_tanh),
             r=[("a_hin", b)], w=[("a_guv", b)])
        S.op("act", lambda e, b=b, i=i: e.activation(out=junk[:], in_=guv[b][:, 512:1024], func=AF.Square, accum_out=ss[:, i:i + 1]),
             r=[("a_guv", b)], w=["a_junk", ("a_ss", i)])
        rstd_op(k, ss[:, i:i + 1], 512, rs[:, i:i + 1], ("a_ss", i), ("a_rs", i))
        S.op("dve", lambda e, b=b, i=i: e.scalar_tensor_tensor(out=vn[b][:], in0=guv[b][:, 512:1024], scalar=rs[:, i:i + 1], in1=vg[:],
                                                             op0=ALU.mult, op1=ALU.mult),
             r=[("a_guv", b), ("a_rs", i), "a_vg"], w=[("a_vn", b)])

    def sB(i):
        b = i % 2
        ps, pk = ps_next(k, ps_pool)
        for g in range(4):
            S.op("pe", lambda e, g=g, b=b, ps=ps: e.matmul(ps[:, g * 128:(g + 1) * 128], wsb[:, g, :], vn[b][:, g * 128:(g + 1) * 128], start=True, stop=True),
                 r=["a_wsb", ("a_vn", b)], w=[pk])
        for g in range(4):
            S.op("dve", lambda e, g=g, b=b, ps=ps: e.scalar_tensor_tensor(out=t1[b][:, g * 128:(g + 1) * 128], in0=ps[:, g * 128:(g + 1) * 128],
                                                                       scalar=bsb[:, g:g + 1], in1=guv[b][:, g * 128:(g + 1) * 128],
                                                                       op0=ALU.add, op1=ALU.mult),
                 r=[pk, "a_bsb", ("a_guv", b)], w=[("a_t1", b)])
        S.op("act", lambda e, b=b: e.activation(out=sz[b][:], in_=hin[b][:, 1024:1536], func=AF.Silu), r=[("a_hin", b)], w=[("a_sz", b)])
        S.op("pool", lambda e, b=b: e.tensor_tensor(out=ysb[b][:], in0=sz[b][:], in1=t1[b][:], op=ALU.mult), r=[("a_t1", b), ("a_sz", b)], w=[("a_y", b)])

    def sC(i):
        b = i % 2
        y_to_yT(k, ysb[b], ("a_y", b), yT, 0, i)

    for n in range(NT + 2):
        if n < NT:
            sA(n)
        if 0 <= n - 1 < NT:
            sB(n - 1)
        if 0 <= n - 2 < NT:
            sC(n - 2)
        yield
    A.release(m)


def mixer_a(k, l, yT):
    for _ in mixer_a_gen(k, l, yT):
        pass


def head_norm(k, src, nh, hd, gain_full, out, ss, rs, tmp, rkeys, key_ss, key_rs, key_tmp, key_out, sq, gkey="gfull"):
    S = k.S
    S.op("act", lambda e: e.activation(out=sq, in_=src, func=AF.Square), r=rkeys, w=[key_tmp + ("sq",)])
    S.op("dve", lambda e: e.tensor_reduce(out=ss, in_=sq.rearrange("p (h d) -> p h d", d=hd), axis=AX.X, op=ALU.add),
         r=[key_tmp + ("sq",)], w=[key_ss])
    rstd_op(k, ss, hd, rs, key_ss, key_rs)
    S.op("dve", lambda e: e.tensor_tensor(out=tmp.rearrange("p (h d) -> p h d", d=hd), in0=src.rearrange("p (h d) -> p h d", d=hd),
                                          in1=rs.unsqueeze(2).to_broadcast([128, nh, hd]), op=ALU.mult),
         r=rkeys + [key_rs], w=[key_tmp])
    S.op("pool", lambda e: e.tensor_tensor(out=out, in0=tmp, in1=gain_full, op=ALU.mult), r=[key_tmp, gkey], w=[key_out])


def rep_gain(k, gfull, gain, nh, hd, gkey="gfull"):
    for h in range(nh):
        k.S.op("pool", lambda e, h=h: e.tensor_copy(out=gfull[:, h * hd:(h + 1) * hd], in_=gain), r=["gains"], w=[gkey])


def gate_and_store(k, l, yv, yv_key, c0z, i, g, yT, bufs, b):
    S = k.S
    zin, sz, ysb = bufs["zin"][b], bufs["sz"][b], bufs["ysb"][b]
    load_h(k, "sp", zin[:], i, c0z, 512, ("g_zin", b))
    S.op("act", lambda e: e.activation(out=sz[:], in_=zin[:], func=AF.Silu), r=[("g_zin", b)], w=[("g_sz", b)])
    S.op("pool", lambda e: e.tensor_tensor(out=ysb[:], in0=yv, in1=sz[:], op=ALU.mult), r=[yv_key, ("g_sz", b)], w=[("g_y", b)])
    y_to_yT(k, ysb, ("g_y", b), yT, g, i)


def gate_bufs(A):
    return {"zin": [A.alloc("g_zin", [128, 512], BF16) for _ in range(2)],
            "sz": [A.alloc("g_sz", [128, 512], F32) for _ in range(2)],
            "ysb": [A.alloc("g_y", [128, 512], BF16) for _ in range(2)]}


def run_pipelined(pairs, stage1, stage2):
    prev = None
    n = 0
    for p in pairs:
        if "pre" in p:
            stage1(p, 0)
            continue
        stage1(p, n % 2)
        if prev is not None:
            stage2(prev, (n - 1) % 2)
        prev = p
        n += 1
    if prev is not None:
        stage2(prev, (n - 1) % 2)


def finalize_heads(k, pfx, psO, pko, rden, yv_b, yv_key, nh_bank, hd, slot):
    S = k.S
    for hg in range(2):
        pv = psO[hg][:, :].rearrange("p (h c) -> p h c", c=slot)
        S.op("dve", lambda e, hg=hg, pv=pv: e.reciprocal(out=rden[:, hg * nh_bank:(hg + 1) * nh_bank], in_=pv[:, :, hd]), r=[pko[hg]], w=[(pfx + "_rden", hg)])
        S.op("dve", lambda e, hg=hg, pv=pv: e.tensor_tensor(out=yv_b[:, hg * nh_bank * hd:(hg + 1) * nh_bank * hd].rearrange("p (h c) -> p h c", c=hd), in0=pv[:, :, 0:hd],
                                                          in1=rden[:, hg * nh_bank:(hg + 1) * nh_bank].unsqueeze(2).to_broadcast([128, nh_bank, hd]), op=ALU.mult),
             r=[pko[hg], (pfx + "_rden", hg)], w=[yv_key])


def mixer_d(k, l, yT):
    S, A, d = k.S, k.A, k.d
    m = A.mark()
    c0 = int(OFF[14])
    qkT = A.alloc("d_qkT", [128, 8, SEQ], BF16)
    vaug = A.alloc("d_vaug", [128, NT, 8, 80], BF16)
    bias5 = A.alloc("d_bias5", [128, 5, 8, 128], F32)
    gains = A.alloc("d_gains", [128, 2, 64], F32)
    S.op("sp", lambda e: e.dma_start(out=bias5[:], in_=d["d_bias5"][l]), w=["d_bias5"], dma=True)
    S.op("sp", lambda e: e.dma_start(out=gains[:, 0, :], in_=d["d_q_gain"][l].partition_broadcast(128)), w=["gains"], dma=True)
    S.op("sp", lambda e: e.dma_start(out=gains[:, 1, :], in_=d["d_k_gain"][l].partition_broadcast(128)), w=["gains"], dma=True)
    S.op("dve", lambda e: e.tensor_scalar(out=gains[:, 0, :], in0=gains[:, 0, :], scalar1=0.125, scalar2=None, op0=ALU.mult), r=["gains"], w=["gains"])
    S.op("pool", lambda e: e.memset(vaug[:, :, :, 64:65], 1.0), w=[("d_v", j) for j in range(NT)])
    gfull = A.alloc("d_gfull", [128, 2, 512], F32)
    for qk in range(2):
        rep_gain(k, gfull[:, qk, :], gains[:, qk, :], 8, 64)
    m1 = A.mark()
    qkv = [A.alloc("d_qkv", [128, 1536], BF16) for _ in range(2)]
    sq = A.alloc("d_sq", [128, 1024], F32)
    tmp = A.alloc("d_tmp", [128, 1024], F32)
    qkn = [A.alloc("d_qkn", [128, 1024], BF16) for _ in range(2)]
    ss = A.alloc("d_ss", [128, NT, 16], F32)
    rs = A.alloc("d_rs", [128, NT, 16], F32)
    def s1a(i):
        b = i % 2
        load_h(k, "sp", qkv[b][:], i, c0, 1536, ("d_qkv", b))
        for qk in range(2):
            head_norm(k, qkv[b][:, qk * 512:(qk + 1) * 512], 8, 64, gfull[:, qk, :], qkn[b][:, qk * 512:(qk + 1) * 512],
                      ss[:, i, qk * 8:(qk + 1) * 8], rs[:, i, qk * 8:(qk + 1) * 8], tmp[:, qk * 512:(qk + 1) * 512],
                      [("d_qkv", b)], ("d_ss", i, qk), ("d_rs", i, qk), ("d_tmp", qk), ("d_qkn", b, qk), sq[:, qk * 512:(qk + 1) * 512])

    def s1b(i):
        b = i % 2
        blocks = [(qkn[b][:, c * 128:(c + 1) * 128], ("d_qkn", b, c // 4)) for c in range(8)]
        transpose_to(k, blocks, qkT[:, :, i * 128:(i + 1) * 128], ("d_qkT", i))
        S.op("pool", lambda e, i=i, b=b: e.tensor_copy(out=vaug[:, i, :, 0:64], in_=qkv[b][:, 1024:1536].rearrange("p (h d) -> p h d", d=64)),
             r=[("d_qkv", b)], w=[("d_v", i)])
    ein = [A.alloc("d_ein", [128, 1024], F32) for _ in range(2)]
    expT = [A.alloc("d_expT", [128, 8, 128], BF16) for _ in range(2)]
    rden = A.alloc("d_rden", [128, 8], F32)
    yv = [A.alloc("d_yv", [128, 512], F32) for _ in range(2)]
    gb = gate_bufs(A)
    psO = [k.psf[4], k.psf[5]]
    pko = [("psf", 4), ("psf", 5)]
    pairs = [dict(pre=(0, "a")), dict(pre=(0, "b"))]
    for i in range(NT):
        P = [dict(i=i, j=j, first=(j == max(0, i - 4)), last=(j == i)) for j in range(max(0, i - 4), i + 1)]
        n = len(P)
        if i + 1 < NT:
            P = [dict(pre=(i + 1, "a"))] + P[0:(n + 1) // 2] + [dict(pre=(i + 1, "b"))] + P[(n + 1) // 2:]
        pairs += P
    s1 = {"a": s1a, "b": s1b}

    def stage1(p, b2):
        if "pre" in p:
            s1[p["pre"][1]](p["pre"][0])
            return
        i, j = p["i"], p["j"]
        dl = i - j
        pss = [ps_next(k), ps_next(k)]
        for h in range(8):
            ps, pk = pss[h % 2]
            hp = h % 2
            S.op("pe", lambda e, h=h, hp=hp, ps=ps: e.matmul(ps[:, (h // 2) * 128:(h // 2 + 1) * 128],
                                                           qkT[hp * 64:(hp + 1) * 64, 4 + h // 2, j * 128:(j + 1) * 128],
                                                           qkT[hp * 64:(hp + 1) * 64, h // 2, i * 128:(i + 1) * 128], start=True, stop=True),
                 r=[("d_qkT", i), ("d_qkT", j)], w=[pk])
        for hg in range(2):
            ps, pk = pss[hg]
            S.op("dve", lambda e, hg=hg, ps=ps: e.tensor_tensor(
                out=ein[b2][:, hg * 512:(hg + 1) * 512], in0=ps[:, :],
                in1=bias5[:, dl, hg * 4:(hg + 1) * 4, :].rearrange("p h t -> p (h t)"), op=ALU.add),
                r=[pk, "d_bias5"], w=[("d_ein", b2, hg)])
            S.op("act", lambda e, hg=hg: e.activation(out=expT[b2][:, hg * 4:(hg + 1) * 4, :].rearrange("p h t -> p (h t)"),
                                                    in_=ein[b2][:, hg * 512:(hg + 1) * 512], func=AF.Exp),
                 r=[("d_ein", b2, hg)], w=[("d_expT", b2, hg)])

    def stage2(p, b2):
        if "pre" in p:
            return
        i, j = p["i"], p["j"]
        for h in range(8):
            po = psO[h // 4]
            S.op("pe", lambda e, h=h, po=po: e.matmul(po[:, (h % 4) * 128:(h % 4) * 128 + 65], expT[b2][:, (h % 2) * 4 + h // 2, :], vaug[:, j, h, 0:65],
                                                    start=(p["first"] and h % 4 == 0), stop=(p["last"] and h % 4 == 3), skip_group_check=True),
                 r=[("d_expT", b2, h % 2), ("d_v", j)], w=[pko[h // 4]])
        if p["last"]:
            b = i % 2
            finalize_heads(k, "d", psO, pko, rden, yv[b], ("d_yv", b), 4, 64, 128)
            gate_and_store(k, l, yv[b][:], ("d_yv", b), c0 + 1536, i, 3, yT, gb, b)

    run_pipelined(pairs, stage1, stage2)
    A.release(m)


def bcast_load(k, dst, src_vec, key):
    return k.S.op("sp", lambda e: e.dma_start(out=dst, in_=src_vec.partition_broadcast(128)), w=[key], dma=True)


def mixer_c(k, l, yT):
    S, A, d = k.S, k.A, k.d
    m = A.mark()
    c0 = int(OFF[10])
    QKn = A.alloc("c_QKn", [128, 8, SEQ], BF16)
    QKr = A.alloc("c_QKr", [128, 4, SEQ], BF16)
    vaug = A.alloc("c_vaug", [128, NT, 4, 144], BF16)
    S.op("pool", lambda e: e.memset(vaug[:, :, :, 128:129], 1.0), w=[("c_v", j) for j in range(NT)])
    m1 = A.mark()
    wstg = A.alloc("c_wstg", [128, 1024], F32)
    wqb = A.alloc("c_wqb", [128, 3, 768], BF16)
    wkb = A.alloc("c_wkb", [128, 1024], BF16)
    g_qa = A.alloc("c_gqa", [128, 384], F32)
    g_kva = A.alloc("c_gkva", [128, 128], F32)
    g_q = A.alloc("c_gq", [128, 192], F32)
    g_k = A.alloc("c_gk", [128, 192], F32)
    cs = A.alloc("c_cs", [128, NT, 64], F32)
    S.op("sp", lambda e: e.dma_start(out=cs[:], in_=d["rope_cs"]), w=["c_cs"], dma=True)
    bcast_load(k, g_qa[:], d["c_qa_gain"][l], "gains")
    bcast_load(k, g_kva[:], d["c_kva_gain"][l], "gains")
    bcast_load(k, g_q[:], d["c_q_gain"][l], "gains")
    bcast_load(k, g_k[:], d["c_k_gain"][l], "gains")
    gq_full = A.alloc("c_gqf", [128, 768], F32)
    gk_full = A.alloc("c_gkf", [128, 768], F32)
    rep_gain(k, gq_full[:], g_q[:], 4, 192)
    rep_gain(k, gk_full[:], g_k[:], 4, 192)
    for kc in range(3):
        S.op("sp", lambda e, kc=kc: e.dma_start(out=wstg[:, 0:768], in_=d["c_w_qb"][l][kc * 128:(kc + 1) * 128, :]), w=["c_wstg"], dma=True)
        S.op("dve", lambda e, kc=kc: e.tensor_copy(out=wqb[:, kc, :], in_=wstg[:, 0:768]), r=["c_wstg"], w=["c_wqb"])
    S.op("sp", lambda e: e.dma_start(out=wstg[:, :], in_=d["c_w_kvb"][l]), w=["c_wstg"], dma=True)
    S.op("dve", lambda e: e.tensor_copy(out=wkb[:], in_=wstg[:, :]), r=["c_wstg"], w=["c_wkb"])
    hin = [A.alloc("c_hin", [128, 576], BF16) for _ in range(2)]
    sq = A.alloc("c_sq", [128, 768], F32)
    tmp = A.alloc("c_tmp", [128, 768], F32)
    lat = [A.alloc("c_lat", [128, 512], BF16) for _ in range(2)]
    latT = [A.alloc("c_latT", [128, 4, 128], BF16) for _ in range(2)]
    qkf = A.alloc("c_qkf", [128, 8, 192], F32)
    qkn = A.alloc("c_qkn", [128, 8, 192], F32)
    qkb = [A.alloc("c_qkb", [128, 8, 128], BF16) for _ in range(2)]
    qkr = [A.alloc("c_qkr", [128, 8, 64], BF16) for _ in range(2)]
    rt = [A.alloc("c_rt", [128, 8, 32], F32) for _ in range(4)]
    ss = A.alloc("c_ss", [128, NT, 16], F32)
    rs = A.alloc("c_rs", [128, NT, 16], F32)
    def s1a(i):
        b = i % 2
        load_h(k, "sp", hin[b][:], i, c0, 576, ("c_hin", b))
        head_norm(k, hin[b][:, 0:384], 1, 384, g_qa[:], lat[b][:, 0:384], ss[:, i, 0:1], rs[:, i, 0:1], tmp[:, 0:384],
                  [("c_hin", b)], ("c_ss", i, 0), ("c_rs", i, 0), ("c_tmp", 2), ("c_lat", b, 0), sq[:, 0:384], gkey="gains")
        head_norm(k, hin[b][:, 384:512], 1, 128, g_kva[:], lat[b][:, 384:512], ss[:, i, 1:2], rs[:, i, 1:2], tmp[:, 384:512],
                  [("c_hin", b)], ("c_ss", i, 1), ("c_rs", i, 1), ("c_tmp", 2), ("c_lat", b, 1), sq[:, 384:512], gkey="gains")

    def s1b(i):
        b = i % 2
        blocks = [(lat[b][:, c * 128:(c + 1) * 128], ("c_lat", b, 0 if c < 3 else 1)) for c in range(4)]
        transpose_to(k, blocks, latT[b][:, :, :], ("c_latT", b))
        pq = [ps_next(k), ps_next(k)]
        for nh, (n0, n1) in enumerate(((0, 512), (512, 768))):
            ps, pk = pq[nh]
            for kc in range(3):
                S.op("pe", lambda e, ps=ps, kc=kc, n0=n0, n1=n1, b=b: e.matmul(ps[:, 0:n1 - n0], latT[b][:, kc, :], wqb[:, kc, n0:n1], start=(kc == 0), stop=(kc == 2)),
                     r=[("c_latT", b), "c_wqb"], w=[pk])
        pkv = [ps_next(k), ps_next(k)]
        for nh in range(2):
            ps, pk = pkv[nh]
            S.op("pe", lambda e, ps=ps, nh=nh, b=b: e.matmul(ps[:, :], latT[b][:, 3, :], wkb[:, nh * 512:(nh + 1) * 512], start=True, stop=True),
                 r=[("c_latT", b), "c_wkb"], w=[pk])
        qflat = qkf[:, 0:4, :].rearrange("p h c -> p (h c)")
        copy_op(S, "act", qflat[:, 0:512], pq[0][0][:, :], r=[pq[0][1]], w=[("c_qkf", 0)])
        copy_op(S, "act", qflat[:, 512:768], pq[1][0][:, 0:256], r=[pq[1][1]], w=[("c_qkf", 0)])
        for nh in range(2):
            pv = pkv[nh][0][:, :].rearrange("p (h c) -> p h c", c=256)
            copy_op(S, "dve", qkf[:, 4 + 2 * nh:6 + 2 * nh, 0:128], pv[:, :, 0:128], r=[pkv[nh][1]], w=[("c_qkf", 1)])
            copy_op(S, "dve", vaug[:, i, 2 * nh:2 * nh + 2, 0:128], pv[:, :, 128:256], r=[pkv[nh][1]], w=[("c_v", i)])
        for h in range(4):
            S.op("pool", lambda e, b=b, h=h: e.tensor_copy(out=qkf[:, 4 + h, 128:192], in_=hin[b][:, 512:576]),
                 r=[("c_hin", b)], w=[("c_qkf", 1)])
        for qk, gain in ((0, gq_full), (1, gk_full)):
            head_norm(k, qkf[:, 4 * qk:4 * qk + 4, :].rearrange("p h c -> p (h c)"), 4, 192, gain[:],
                      qkn[:, 4 * qk:4 * qk + 4, :].rearrange("p h c -> p (h c)"), ss[:, i, 4 + 4 * qk:8 + 4 * qk], rs[:, i, 4 + 4 * qk:8 + 4 * qk],
                      tmp[:, :], [("c_qkf", qk)], ("c_ss", i, 2 + qk), ("c_rs", i, 2 + qk), ("c_tmp", 2), ("c_qkn", qk), sq[:, :])
        rq = [("c_qkn", 0), ("c_qkn", 1)]
        S.op("pool", lambda e, b=b: e.tensor_copy(out=qkb[b][:, :, :], in_=qkn[:, :, 0:128]), r=rq, w=[("c_qkb", b)])
        x1, x2 = qkn[:, :, 128:160], qkn[:, :, 160:192]
        cc = cs[:, i, 0:32].unsqueeze(1).to_broadcast([128, 8, 32])
        sn = cs[:, i, 32:64].unsqueeze(1).to_broadcast([128, 8, 32])
        for ti, (xa, tb) in enumerate(((x1, cc), (x2, sn), (x1, sn), (x2, cc))):
            S.op("dve", lambda e, ti=ti, xa=xa, tb=tb: e.tensor_tensor(out=rt[ti][:], in0=xa, in1=tb, op=ALU.mult), r=rq + ["c_cs"], w=[("c_rt", ti)])
        S.op("pool", lambda e, b=b: e.tensor_tensor(out=qkr[b][:, :, 0:32], in0=rt[0][:], in1=rt[1][:], op=ALU.subtract),
             r=[("c_rt", 0), ("c_rt", 1)], w=[("c_qkr", b)])
        S.op("pool", lambda e, b=b: e.tensor_tensor(out=qkr[b][:, :, 32:64], in0=rt[2][:], in1=rt[3][:], op=ALU.add),
             r=[("c_rt", 2), ("c_rt", 3)], w=[("c_qkr", b)])

    def s1c(i):
        b = i % 2
        blocks = [(qkb[b][:, c, :], ("c_qkb", b)) for c in range(8)]
        transpose_to(k, blocks, QKn[:, :, i * 128:(i + 1) * 128], ("c_QKn", i))
        blocks = [(qkr[b][:, 2 * c:2 * c + 2, :].rearrange("p h c -> p (h c)"), ("c_qkr", b)) for c in range(4)]
        transpose_to(k, blocks, QKr[:, :, i * 128:(i + 1) * 128], ("c_QKr", i))
    expT = [A.alloc("c_expT", [128, 4, 128], BF16) for _ in range(2)]
    rden = A.alloc("c_rden", [128, 4], F32)
    yv = [A.alloc("c_yv", [128, 512], F32) for _ in range(2)]
    gb = gate_bufs(A)
    psO = [k.psf[4], k.psf[5]]
    pko = [("psf", 4), ("psf", 5)]
    scale = float(192 ** -0.5)
    pairs = [dict(pre=(0, "a")), dict(pre=(0, "b")), dict(pre=(0, "c"))]
    for i in range(NT):
        P = [dict(i=i, j=j, first=(j == 0), last=(j == i)) for j in range(i + 1)]
        n = len(P)
        if i + 1 < NT:
            P = [dict(pre=(i + 1, "a"))] + P[0:n // 3] + [dict(pre=(i + 1, "b"))] + P[n // 3:2 * n // 3] + [dict(pre=(i + 1, "c"))] + P[2 * n // 3:]
        pairs += P
    s1 = {"a": s1a, "b": s1b, "c": s1c}

    def stage1(p, b2):
        if "pre" in p:
            s1[p["pre"][1]](p["pre"][0])
            return
        i, j = p["i"], p["j"]
        pss = [ps_next(k), ps_next(k)]
        for h in (0, 2, 1, 3):
            hp = h % 2
            ps, pk = pss[hp]
            c_ = (h // 2) * 128
            S.op("pe", lambda e, h=h, ps=ps, c_=c_: e.matmul(ps[:, c_:c_ + 128], QKn[:, 4 + h, j * 128:(j + 1) * 128], QKn[:, h, i * 128:(i + 1) * 128],
                                                           start=True, stop=False),
                 r=[("c_QKn", i), ("c_QKn", j)], w=[pk])
            S.op("pe", lambda e, h=h, hp=hp, ps=ps, c_=c_: e.matmul(ps[:, c_:c_ + 128], QKr[hp * 64:(hp + 1) * 64, 2 + h // 2, j * 128:(j + 1) * 128],
                                                                  QKr[hp * 64:(hp + 1) * 64, h // 2, i * 128:(i + 1) * 128], start=False, stop=True),
                 r=[("c_QKr", i), ("c_QKr", j)], w=[pk])
        for hp in range(2):
            ps, pk = pss[hp]
            S.op("act", lambda e, ps=ps, hp=hp: e.activation(out=expT[b2][:, 2 * hp:2 * hp + 2, :].rearrange("p h t -> p (h t)"), in_=ps[:, 0:256], func=AF.Exp, scale=scale),
                 r=[pk], w=[("c_expT", b2)])
        if j == i:
            S.op("pool", lambda e: e.memset(expT[b2][64:128, :, 0:64], 0.0), r=[("c_expT", b2)], w=[("c_expT", b2)])

    def stage2(p, b2):
        if "pre" in p:
            return
        i, j = p["i"], p["j"]
        for h in range(4):
            po = psO[h // 2]
            S.op("pe", lambda e, h=h, po=po: e.matmul(po[:, (h % 2) * 256:(h % 2) * 256 + 129], expT[b2][:, (h % 2) * 2 + h // 2, :], vaug[:, j, h, 0:129],
                                                    start=(j == 0 and h % 2 == 0), stop=(j == i and h % 2 == 1), skip_group_check=True),
                 r=[("c_expT", b2), ("c_v", j)], w=[pko[h // 2]])
        if p["last"]:
            b = i % 2
            finalize_heads(k, "c", psO, pko, rden, yv[b], ("c_yv", b), 2, 128, 256)
            gate_and_store(k, l, yv[b][:], ("c_yv", b), c0 + 576, i, 2, yT, gb, b)

    run_pipelined(pairs, stage1, stage2)
    A.release(m)


def b_pre_gen(k, l):
    S, A, d = k.S, k.A, k.d
    A.mode = "top"
    m = A.mark()
    c0 = int(OFF[3])
    IQT = A.alloc("b_IQT", [128, 5, SEQ], BF16)
    absw = A.alloc("b_absw", [128, NT, 8], F32)
    sgnw = A.alloc("b_sgnw", [128, NT, 8], F32)
    hin = [A.alloc("b_hiq", [128, 584], BF16) for _ in range(2)]
    ikk = [A.alloc("b_ikk", [128, 128], BF16) for _ in range(2)]
    NBIS = 18
    score2 = [A.alloc("b_score", [128, SEQ], F32) for _ in range(2)]
    bs2 = [A.alloc("b_bs", [128, 8], F32) for _ in range(2)]
    W2 = [A.alloc("b_W", [128, NBIS + 1], F32) for _ in range(2)]
    pow2 = A.alloc("b_pow2", [128, NBIS + 1], F32)
    mb = [A.alloc("b_mb", [128, SEQ], BF16) for _ in range(2)]
    tmpr = [A.alloc("b_tmpr", [128, 512], F32) for _ in range(4)]
    tr = [0]
    A.mode = "bottom"
    for kk in range(NBIS + 1):
        S.op("pool", lambda e, kk=kk: e.memset(pow2[:, kk:kk + 1], float(2.0 ** -(kk + 1))), w=["b_pow2"])

    def s1iq(i):
        b = i % 2
        load_h(k, "sp", hin[b][:], i, c0 + 640, 584, ("b_hiq", b))
        for hh in range(2):
            S.op("pool", lambda e, hh=hh: e.tensor_copy(out=ikk[b][:, hh * 64:(hh + 1) * 64], in_=hin[b][:, 512:576]), r=[("b_hiq", b)], w=[("b_ikk", b)])
        S.op("act", lambda e: e.activation(out=absw[:, i, :], in_=hin[b][:, 576:584], func=AF.Abs), r=[("b_hiq", b)], w=[("b_absw", i)])
        S.op("act", lambda e: e.activation(out=sgnw[:, i, :], in_=hin[b][:, 576:584], func=AF.Sign), r=[("b_hiq", b)], w=[("b_sgnw", i)])
        blocks = [(hin[b][:, c * 128:(c + 1) * 128], ("b_hiq", b)) for c in range(4)] + [(ikk[b][:, :], ("b_ikk", b))]
        transpose_to(k, blocks, IQT[:, :, i * 128:(i + 1) * 128], ("b_IQT", i))

    def idx_part(i):
        p2 = i % 2
        score, bs, W = score2[p2], bs2[p2], W2[p2]
        n_i = 128 * (i + 1)
        nch = (n_i + 511) // 512
        for c_ in range(nch):
            cw = min(512, n_i - 512 * c_)
            for h in range(8):
                hp = h % 2
                ps, pk = ps_next(k, k.b_pre_pool)
                S.op("pe", lambda e, ps=ps, h=h, hp=hp, c_=c_, cw=cw: e.matmul(ps[:, 0:cw], IQT[hp * 64:(hp + 1) * 64, h // 2, i * 128:(i + 1) * 128],
                                                                           IQT[hp * 64:(hp + 1) * 64, 4, c_ * 512:c_ * 512 + cw], start=True, stop=True),
                     r=[("b_IQT", i)] + [("b_IQT", jj) for jj in range(4 * c_, min(4 * c_ + 4, i + 1))], w=[pk])
                b3 = tr[0] % 4
                tr[0] += 1
                S.op("act", lambda e, ps=ps, b3=b3, cw=cw, h=h: e.activation(out=tmpr[b3][:, 0:cw], in_=ps[:, 0:cw], func=AF.Relu, scale=absw[:, i, h:h + 1]),
                     r=[pk, ("b_absw", i)], w=[("b_tmpr", b3)])
                sc = score[:, c_ * 512:c_ * 512 + cw]
                if h == 0:
                    S.op("dve", lambda e, sc=sc, b3=b3, cw=cw: e.tensor_scalar(out=sc, in0=tmpr[b3][:, 0:cw], scalar1=sgnw[:, i, 0:1], scalar2=None, op0=ALU.mult),
                         r=[("b_tmpr", b3), ("b_sgnw", i)], w=[("b_score", p2, c_)])
                else:
                    S.op("dve", lambda e, sc=sc, b3=b3, cw=cw, h=h: e.scalar_tensor_tensor(out=sc, in0=tmpr[b3][:, 0:cw], scalar=sgnw[:, i, h:h + 1], in1=sc,
                                                                                       op0=ALU.mult, op1=ALU.add),
                         r=[("b_tmpr", b3), ("b_sgnw", i), ("b_score", p2, c_)], w=[("b_score", p2, c_)])
                yield
        allsc = [("b_score", p2, c_) for c_ in range(nch)]
        bk = ("b_bs", p2)
        lo, hi, w0, mid = (bs[:, c_:c_ + 1] for c_ in range(4))
        if i >= 2:
            S.op("dve", lambda e: e.tensor_reduce(out=lo, in_=score[:, 0:n_i], axis=AX.X, op=ALU.min), r=allsc, w=[bk])
        S.op("dve", lambda e: e.memset(score[0:64, n_i - 64:n_i], -1e30), r=allsc, w=allsc)
        if i >= 2:
            S.op("dve", lambda e: e.tensor_reduce(out=hi, in_=score[:, 0:n_i], axis=AX.X, op=ALU.max), r=allsc, w=[bk])
            S.op("dve", lambda e: e.tensor_tensor(out=w0, in0=hi, in1=lo, op=ALU.subtract), r=[bk], w=[bk])
            S.op("dve", lambda e: e.tensor_scalar(out=W[:, :], in0=pow2[:, :], scalar1=w0, scalar2=None, op0=ALU.mult), r=[bk, "b_pow2"], w=[("b_W", p2)])
            S.op("dve", lambda e: e.tensor_tensor(out=mid, in0=lo, in1=W[:, 0:1], op=ALU.add), r=[bk, ("b_W", p2)], w=[bk])
        else:
            S.op("dve", lambda e: e.memset(lo, -1e29), w=[bk])

    def bis_step(i, kk):
        if i < 2:
            return
        p2 = i % 2
        score, bs, W = score2[p2], bs2[p2], W2[p2]
        n_i = 128 * (i + 1)
        allsc = [("b_score", p2, c_) for c_ in range((n_i + 511) // 512)]
        bk = ("b_bs", p2)
        mid, cnt, gw = bs[:, 3:4], bs[:, 4:5], bs[:, 5:6]
        S.op("dve", lambda e: e.tensor_scalar(out=mb[p2][:, 0:n_i], in0=score[:, 0:n_i], scalar1=mid, scalar2=0.0, op0=ALU.is_ge, op1=ALU.add, accum_out=cnt),
             r=allsc + [bk], w=[bk, ("b_mb", p2)])
        S.op("dve", lambda e: e.tensor_scalar(out=gw, in0=cnt, scalar1=256.0, scalar2=-0.5, op0=ALU.is_ge, op1=ALU.add), r=[bk], w=[bk])
        S.op("dve", lambda e: e.scalar_tensor_tensor(out=mid, in0=gw, scalar=W[:, kk:kk + 1], in1=mid, op0=ALU.mult, op1=ALU.add),
             r=[bk, ("b_W", p2)], w=[bk])

    def mask_part(i):
        p2 = i % 2
        score, bs, W = score2[p2], bs2[p2], W2[p2]
        n_i = 128 * (i + 1)
        allsc = [("b_score", p2, c_) for c_ in range((n_i + 511) // 512)]
        bk = ("b_bs", p2)
        lo, mid = bs[:, 0:1], bs[:, 3:4]
        if i >= 2:
            S.op("dve", lambda e: e.tensor_tensor(out=lo, in0=mid, in1=W[:, NBIS:NBIS + 1], op=ALU.subtract), r=[bk, ("b_W", p2)], w=[bk])
        mbb = mb[p2]
        S.op("dve", lambda e: e.tensor_scalar(out=mbb[:, 0:n_i], in0=score[:, 0:n_i], scalar1=lo, scalar2=NEGM, op0=ALU.is_lt, op1=ALU.mult),
             r=allsc + [bk], w=[("b_mb", p2)])
        S.op("pool", lambda e: e.dma_start(out=k.mask_d[i * 128:(i + 1) * 128, 0:n_i], in_=mbb[:, 0:n_i]), r=[("b_mb", p2)], w=[("mask_d", i)], dma=True)

    for t in range(min(4, NT)):
        s1iq(t)
        yield
    for _ in idx_part(0):
        yield
    for t in range(NT):
        if t + 4 < NT:
            s1iq(t + 4)
            yield
        gi = idx_part(t + 1) if t + 1 < NT else iter(())
        steps = list(range(NBIS)) if t >= 2 else []
        n_idx = 8 * ((128 * (t + 2) + 511) // 512) if t + 1 < NT else 0
        per = max(1, -(-n_idx // max(1, len(steps)))) if steps else n_idx
        for kk in steps:
            bis_step(t, kk)
            yield
            for _ in range(per):
                if next(gi, "end") != "end":
                    yield
        for _ in gi:
            yield
        mask_part(t)
        yield
    A.mode = "top"
    A.release(m)
    A.mode = "bottom"


def mixer_b(k, l, yT):
    S, A, d = k.S, k.A, k.d
    m = A.mark()
    c0 = int(OFF[3])
    QT = A.alloc("b_QT", [128, 5, SEQ], BF16)
    vaug = A.alloc("b_vaug", [128, NT, 80], BF16)
    bias3 = A.alloc("b_bias3", [128, 3, 8, 128], F32)
    gains = A.alloc("b_gains", [128, 2, 64], F32)
    S.op("sp", lambda e: e.dma_start(out=bias3[:], in_=d["b_bias3"]), w=["b_bias3"], dma=True)
    S.op("sp", lambda e: e.dma_start(out=gains[:, 0, :], in_=d["b_q_gain"][l].partition_broadcast(128)), w=["gains"], dma=True)
    S.op("sp", lambda e: e.dma_start(out=gains[:, 1, :], in_=d["b_k_gain"][l].partition_broadcast(128)), w=["gains"], dma=True)
    S.op("dve", lambda e: e.tensor_scalar(out=gains[:, 0, :], in0=gains[:, 0, :], scalar1=0.125, scalar2=None, op0=ALU.mult), r=["gains"], w=["gains"])
    for c in range(2):
        S.op("dve", lambda e, c=c: e.tensor_tensor(out=bias3[:, c, :, :], in0=bias3[:, c, :, :], in1=bias3[:, 2, :, :], op=ALU.subtract),
             r=["b_bias3"], w=["b_bias3"])
    S.op("pool", lambda e: e.memset(vaug[:, :, 64:65], 1.0), w=[("b_v", j) for j in range(NT)])
    gqf = A.alloc("b_gqf", [128, 512], F32)
    rep_gain(k, gqf[:], gains[:, 0, :], 8, 64)
    hin = [A.alloc("b_hin", [128, 640], BF16) for _ in range(2)]
    sq = A.alloc("b_sq", [128, 576], F32)
    tmp = A.alloc("b_tmp", [128, 576], F32)
    qn = [A.alloc("b_qn", [128, 640], BF16) for _ in range(2)]
    ss = A.alloc("b_ss", [128, NT, 16], F32)
    rs = A.alloc("b_rs", [128, NT, 16], F32)

    def s1a(i):
        b = i % 2
        load_h(k, "sp", hin[b][:], i, c0, 640, ("b_hin", b))
        head_norm(k, hin[b][:, 0:512], 8, 64, gqf[:], qn[b][:, 0:512], ss[:, i, 0:8], rs[:, i, 0:8], tmp[:, 0:512],
                  [("b_hin", b)], ("b_ss", i, 0), ("b_rs", i, 0), ("b_tmp", 0), ("b_qn", b, 0), sq[:, 0:512])
        head_norm(k, hin[b][:, 512:576], 1, 64, gains[:, 1, :], qn[b][:, 512:576], ss[:, i, 8:9], rs[:, i, 8:9], tmp[:, 512:576],
                  [("b_hin", b)], ("b_ss", i, 1), ("b_rs", i, 1), ("b_tmp", 1), ("b_qn", b, 1), sq[:, 512:576], gkey="gains")
        S.op("pool", lambda e, b=b: e.tensor_copy(out=qn[b][:, 576:640], in_=qn[b][:, 512:576]), r=[("b_qn", b, 1)], w=[("b_qn", b, 2)])
        S.op("pool", lambda e, b=b, i=i: e.tensor_copy(out=vaug[:, i, 0:64], in_=hin[b][:, 576:640]), r=[("b_hin", b)], w=[("b_v", i)])

    def s1b(i):
        b = i % 2
        blocks = [(qn[b][:, c * 128:(c + 1) * 128], ("b_qn", b, 0)) for c in range(4)] + [(qn[b][:, 512:640], ("b_qn", b, 2))]
        transpose_to(k, blocks, QT[:, :, i * 128:(i + 1) * 128], ("b_QT", i), extra_r=[("b_qn", b, 1)])

    mbl = [A.alloc("b_mbl", [128, SEQ], BF16) for _ in range(2)]
    maskT = [A.alloc("b_maskT", [128, NT, 128], BF16) for _ in range(2)]
    ein = [A.alloc("b_ein", [128, 1024], F32) for _ in range(1)]
    expT = [A.alloc("b_expT", [128, 8, 128], BF16) for _ in range(2)]
    rden = A.alloc("b_rden", [128, 8], F32)
    yv = [A.alloc("b_yv", [128, 512], F32) for _ in range(2)]
    gb = gate_bufs(A)
    psO = [k.psf[4], k.psf[5]]
    pko = [("psf", 4), ("psf", 5)]

    def mload(i):
        p2 = i % 2
        n_i = 128 * (i + 1)
        S.op("sp", lambda e: e.dma_start(out=mbl[p2][:, 0:n_i], in_=k.mask_d[i * 128:(i + 1) * 128, 0:n_i]), r=[("mask_d", i)], w=[("b_mbl", p2)], dma=True)
        for g0 in range(0, i + 1, 8):
            nb_ = min(8, i + 1 - g0)
            blocks = [(mbl[p2][:, (g0 + bi) * 128:(g0 + bi + 1) * 128], ("b_mbl", p2)) for bi in range(nb_)]
            transpose_to(k, blocks, maskT[p2][:, g0:g0 + nb_, :], ("b_maskT", p2, g0 // 8))

    pairs = []
    for t in range(min(2, NT)):
        pairs += [dict(pre=("a", t)), dict(pre=("b", t))]
    pairs += [dict(pre=("mask", 0))]
    for i in range(NT):
        near = [dict(i=i, j=j) for j in (i, i - 1) if j >= 0]
        far = [dict(i=i, j=j) for j in range(0, i - 1)]
        real = near + far
        for p in real:
            p["first"] = p is real[0]
            p["last"] = p is real[-1]
        tile = list(near)
        if i + 1 < NT:
            tile.append(dict(pre=("mask", i + 1)))
        if i + 2 < NT:
            tile.append(dict(pre=("a", i + 2)))
        tile += far
        if i + 2 < NT:
            tile.append(dict(pre=("b", i + 2)))
        pairs += tile
    s1 = {"a": s1a, "b": s1b, "mask": mload}

    def stage1(p, b2):
        if "pre" in p:
            s1[p["pre"][0]](p["pre"][1])
            return
        i, j = p["i"], p["j"]
        dl = min(i - j, 2)
        mT = maskT[i % 2]
        pss = [ps_next(k), ps_next(k)]
        for hp in range(2):
            ps, pk = pss[hp]
            S.op("pe", lambda e, hp=hp, ps=ps: e.matmul(ps[:, :], QT[hp * 64:(hp + 1) * 64, 4, j * 128:(j + 1) * 128],
                                                      QT[hp * 64:(hp + 1) * 64, 0:4, i * 128:(i + 1) * 128], start=True, stop=False,
                                                      skip_group_check=True),
                 r=[("b_QT", i), ("b_QT", j)], w=[pk])
        for hg in range(2):
            ps, pk = pss[hg]
            S.op("pe", lambda e, ps=ps: e.matmul(ps[:, :], k.ident[:], mT[:, j, :].unsqueeze(1).to_broadcast([128, 4, 128]), start=False, stop=True, skip_group_check=True),
                 r=[("b_maskT", i % 2, j // 8), "ident"], w=[pk])
        for hg in range(2):
            ps, pk = pss[hg]
            if dl < 2:
                S.op("dve", lambda e, hg=hg, ps=ps: e.tensor_tensor(
                    out=ein[0][:, hg * 512:(hg + 1) * 512], in0=ps[:, :],
                    in1=bias3[:, dl, hg * 4:(hg + 1) * 4, :].rearrange("p h t -> p (h t)"), op=ALU.add),
                    r=[pk, "b_bias3"], w=[("b_ein", 0, hg)])
                S.op("act", lambda e, hg=hg: e.activation(out=expT[b2][:, hg * 4:(hg + 1) * 4, :].rearrange("p h t -> p (h t)"),
                                                        in_=ein[0][:, hg * 512:(hg + 1) * 512], func=AF.Exp),
                     r=[("b_ein", 0, hg)], w=[("b_expT", b2, hg)])
            else:
                S.op("act", lambda e, hg=hg, ps=ps: e.activation(out=expT[b2][:, hg * 4:(hg + 1) * 4, :].rearrange("p h t -> p (h t)"),
                                                               in_=ps[:, :], func=AF.Exp),
                     r=[pk], w=[("b_expT", b2, hg)])

    def stage2(p, b2):
        if "pre" in p:
            return
        i, j = p["i"], p["j"]
        for h in range(8):
            po = psO[h // 4]
            S.op("pe", lambda e, h=h, po=po: e.matmul(po[:, (h % 4) * 128:(h % 4) * 128 + 65], expT[b2][:, (h % 2) * 4 + h // 2, :], vaug[:, j, 0:65],
                                                    start=(p["first"] and h % 4 == 0), stop=(p["last"] and h % 4 == 3), skip_group_check=True),
                 r=[("b_expT", b2, h % 2), ("b_v", j)], w=[pko[h // 4]])
        if p["last"]:
            b = i % 2
            finalize_heads(k, "b", psO, pko, rden, yv[b], ("b_yv", b), 4, 64, 128)
            gate_and_store(k, l, yv[b][:], ("b_yv", b), c0 + 1224, i, 1, yT, gb, b)

    run_pipelined(pairs, stage1, stage2)
    A.release(m)


def build_program(mode="full", nlayers=DEPTH, mixers="ABCD"):
    nc = bass.Bass("TRN2", target_bir_lowering=False)
    k = K()
    k.nc = nc
    k.S = Sched(nc)
    k.A = Arena(nc)
    k.ps_i = k.pst_i = k.ev_i = k.ps_lo = k.ps_hi = 0
    k.out_dmas = []
    d = {}

    def inp(name, shape, dt=F32):
        d[name] = nc.dram_tensor(name, list(shape), dt, kind="ExternalInput").ap()

    inp("x", [SEQ, D_MODEL])
    inp("w_in", [DEPTH, D_MODEL, IN_COLS])
    inp("w_out", [DEPTH, D_MODEL, D_MODEL])
    inp("norm_g", [DEPTH, D_MODEL])
    inp("ident", [128, 128])
    inp("a_wsT", [DEPTH, 128, 4, 128])
    inp("a_bsT", [DEPTH, 128, 4])
    inp("a_v_gain", [DEPTH, 512])
    inp("b_bias3", [128, 3, 8, 128])
    inp("b_q_gain", [DEPTH, 64])
    inp("b_k_gain", [DEPTH, 64])
    inp("c_w_qb", [DEPTH, 384, 768])
    inp("c_w_kvb", [DEPTH, 128, 1024])
    inp("c_qa_gain", [DEPTH, 384])
    inp("c_kva_gain", [DEPTH, 128])
    inp("c_q_gain", [DEPTH, 192])
    inp("c_k_gain", [DEPTH, 192])
    inp("rope_cs", [128, NT, 64])
    inp("d_bias5", [DEPTH, 128, 5, 8, 128])
    inp("d_q_gain", [DEPTH, 64])
    inp("d_k_gain", [DEPTH, 64])
    k.d = d
    dbg = mode != "full"
    mixonly = mode == "mixonly"
    out_d = nc.dram_tensor("out", [SEQ, D_MODEL], F32, kind="ExternalOutput").ap()
    k.h_d = nc.dram_tensor("h_scr", [SEQ, IN_COLS], BF16, kind="ExternalInput" if mixonly else ("ExternalOutput" if dbg else "Internal")).ap()
    xs_d = nc.dram_tensor("x_scr", [SEQ, D_MODEL], F32).ap()
    k.mask_d = nc.dram_tensor("mask_scr", [SEQ, SEQ], BF16).ap()
    if dbg:
        ydbg = nc.dram_tensor("ydbg", [D_MODEL, SEQ], BF16, kind="ExternalOutput").ap()

    with ExitStack() as es:
        S, A = k.S, k.A
        k.psf = [es.enter_context(nc.psum_tensor("psf%d" % i, [128, 512], F32)) for i in range(6)]
        k.pstt = [es.enter_context(nc.psum_tensor("pst%d" % i, [128, 1024], BF16)) for i in range(2)]
        idf = A.alloc("idf", [128, 128], F32)
        k.ident = A.alloc("ident", [128, 128], BF16)
        S.op("sp", lambda e: e.dma_start(out=idf[:], in_=d["ident"]), w=["idf"], dma=True)
        S.op("dve", lambda e: e.tensor_copy(out=k.ident[:], in_=idf[:]), r=["idf"], w=["ident"])
        k.eps_t = A.alloc("eps_t", [128, 1], F32)
        S.op("dve", lambda e: e.memset(k.eps_t[:], EPS), w=["eps_t"])

        for l in range(nlayers):
            x_src = d["x"] if l == 0 else xs_d
            x_dst = out_d if l == nlayers - 1 else xs_d
            m0 = A.mark()
            use_b = "B" in mixers
            k.b_pre_pool = "all" if mixonly else "hi"
            if not mixonly:
                xnT = A.alloc("xnT", [128, KC, SEQ], BF16)
                pb = proj_bufs(A)
                hsb = [A.alloc("hsb", [128, 512], BF16) for _ in range(4)]
                phase_norm(k, l, x_src, xnT)
                norm_last = [S.ops[e][-1] for e in S.ENGS if S.ops[e] and S.ops[e][-1].fn is not None]
                gen = [None]
                ncb = (IN_COLS + 511) // 512
                order = [4, 5] + [cb for cb in range(ncb) if cb not in (4, 5)]

                n_y = 4 + 8 + sum((1 if t + 4 < NT else 0) + (18 if t >= 2 else 0) + (8 * ((128 * (t + 2) + 511) // 512) if t + 1 < NT else 0) + 1 for t in range(NT))
                quota = -(-n_y // ((ncb - 2) * NT))

                def hook(pos, i):
                    if not use_b or pos < 1 or (pos == 1 and i < 5):
                        return
                    if gen[0] is None:
                        for e in S.ENGS:
                            S.fence(e, norm_last)
                        gen[0] = b_pre_gen(k, l)
                    for _ in range(quota):
                        if next(gen[0], "end") == "end":
                            break

                phase_inproj(k, l, xnT, pb, hsb, order, hook, keep_dve_free=use_b)
                S.barrier()
            elif use_b:
                gen = [b_pre_gen(k, l)]
            A.release(m0)
            yT = A.alloc("yT", [128, KC, SEQ], BF16)
            if mixonly:
                S.op("pool", lambda e: e.memset(yT[:], 0.0), w=[("yT", i) for i in range(NT)])
            done_a = False
            if use_b:
                if "A" in mixers:
                    ga = mixer_a_gen(k, l, yT, ps_pool="lo" if not mixonly else "all")
                    for _ in ga:
                        for _ in range(8):
                            if next(gen[0], "end") == "end":
                                break
                    done_a = True
                for _ in gen[0]:
                    pass
                S.barrier()
            for g, (nm, fn) in enumerate((("A", mixer_a), ("B", mixer_b), ("C", mixer_c), ("D", mixer_d))):
                if nm == "A" and done_a:
                    continue
                if nm in mixers and fn is not None:
                    fn(k, l, yT)
                elif mixonly:
                    continue
                else:
                    mixer_stub(k, l, yT, g)
                S.barrier()
            if dbg and l == nlayers - 1:
                S.op("pool", lambda e: e.dma_start(out=ydbg.rearrange("(fc p) t -> p fc t", p=128), in_=yT[:]),
                     r=[("yT", i) for i in range(NT)], w=["ydbg"], dma=True)
            if not mixonly:
                phase_outproj(k, l, yT, x_src, x_dst)
            S.barrier()
            A.release(m0)
        if not mixonly:
            S.fence("pool", k.out_dmas[-64:] + S.dmas_since_barrier)
        S.emit(es)
    return nc


def host_consts(inputs):
    f = np.float32
    hc = {}
    s_ = np.arange(128)[:, None, None]
    dl = np.arange(5)[None, :, None]
    t_ = np.arange(128)[None, None, :]
    dist = 128 * dl + t_ - s_
    dq = 2 * dl + t_ // 64 - s_ // 64
    valid = (dq >= 0) & (dq <= 8)
    idx = np.clip(dist, -128, 128) + 128
    rb = np.asarray(inputs["d_rel_bias"], dtype=f)
    tab = rb[:, idx]
    tab = np.where(valid[None, ..., None], tab, f(NEGM)).transpose(0, 1, 2, 4, 3)
    tab = tab[:, :, :, [0, 2, 4, 6, 1, 3, 5, 7], :]
    hc["d_bias5"] = np.ascontiguousarray(tab, dtype=f)
    def t5_bucket_np(rel):
        nb, max_exact = 16, 8
        ret = np.where(rel > 0, nb, 0)
        n = np.abs(rel)
        nf = np.maximum(n, 1).astype(np.float32)
        large = max_exact + (np.log(nf / np.float32(max_exact)) / np.float32(np.log(128 / max_exact)) * np.float32(nb - max_exact)).astype(np.int32)
        large = np.minimum(large, nb - 1)
        return ret + np.where(n < max_exact, n, large)
    s2 = np.arange(128)[:, None, None]
    cl = np.arange(3)[None, :, None]
    t2 = np.arange(128)[None, None, :]
    rel = s2 - t2 - 128 * cl - np.where(cl == 2, 4096, 0)
    bk = t5_bucket_np(rel.astype(np.int64))
    t5 = np.asarray(inputs["t5_bias"], dtype=f)[bk]
    t5 = t5.transpose(0, 1, 3, 2)[:, :, [0, 2, 4, 6, 1, 3, 5, 7], :]
    hc["b_bias3"] = np.ascontiguousarray(t5, dtype=f)
    inv = (10000.0 ** (-np.arange(0, 64, 2, dtype=np.float32) / np.float32(64))).astype(f)
    ang = np.arange(SEQ, dtype=f)[:, None] * inv[None, :]
    cs = np.concatenate([np.cos(ang), np.sin(ang)], axis=1).astype(f)
    hc["rope_cs"] = np.ascontiguousarray(cs.reshape(NT, 128, 64).transpose(1, 0, 2))
    return hc


def host_inputs(inputs, b, hc=None):
    f = np.float32
    if hc is None:
        hc = host_consts(inputs)
    hi = {
        "x": np.ascontiguousarray(inputs["x"][b], dtype=f),
        "w_in": np.ascontiguousarray(inputs["w_in"], dtype=f),
        "w_out": np.ascontiguousarray(inputs["w_out"], dtype=f),
        "norm_g": np.ascontiguousarray(inputs["norm_g"], dtype=f),
        "ident": np.eye(128, dtype=f),
        "a_wsT": np.ascontiguousarray(np.asarray(inputs["a_ws"], dtype=f).transpose(0, 3, 1, 2)),
        "a_bsT": np.ascontiguousarray(np.asarray(inputs["a_bs"], dtype=f).transpose(0, 2, 1)),
        "a_v_gain": np.ascontiguousarray(inputs["a_v_gain"], dtype=f),
        "d_q_gain": np.ascontiguousarray(inputs["d_q_gain"], dtype=f),
        "b_q_gain": np.ascontiguousarray(inputs["b_q_gain"], dtype=f),
        "b_k_gain": np.ascontiguousarray(inputs["b_k_gain"], dtype=f),
        "c_w_qb": np.ascontiguousarray(inputs["c_w_qb"], dtype=f),
        "c_w_kvb": np.ascontiguousarray(inputs["c_w_kvb"], dtype=f),
        "c_qa_gain": np.ascontiguousarray(inputs["c_qa_gain"], dtype=f),
        "c_kva_gain": np.ascontiguousarray(inputs["c_kva_gain"], dtype=f),
        "c_q_gain": np.ascontiguousarray(inputs["c_q_gain"], dtype=f),
        "c_k_gain": np.ascontiguousarray(inputs["c_k_gain"], dtype=f),
        "d_k_gain": np.ascontiguousarray(inputs["d_k_gain"], dtype=f),
    }
    hi.update(hc)
    return hi


def kernel(**inputs):
    nc = build_program("full")
    n = 8
    hc = host_consts(inputs)
    in_maps = [host_inputs(inputs, b, hc) for b in range(n)]
    res = run_bass_kernel_spmd(nc, in_maps, core_ids=list(range(n)))
    return np.stack([np.asarray(r["out"]) for r in res.results], axis=0).astype(np.float32)
```
